# Optimizing a Trainium2 kernel written in Bass

```python
import jax, jax.numpy as jnp
from jax import lax
import numpy as np

D_MODEL = 2048
BATCH = 4
SEQ = 4096
DEPTH = 1

N_ATTN_HEADS = 8
HEAD_DIM = 128
ATTN_WIDTH = N_ATTN_HEADS * HEAD_DIM
POOL_WINDOWS = (2, 4, 8, 16)
N_POOL_GROUPS = len(POOL_WINDOWS)
POOL_GROUP_WIDTH = 256
POOL_WIDTH = N_POOL_GROUPS * POOL_GROUP_WIDTH
N_BRANCHES = 2
IN_WIDTH = 3 * ATTN_WIDTH + POOL_WIDTH + N_BRANCHES * D_MODEL
D_FF = 5632
CONV_WIDTH = 3
PLE_DIM = 256
Q_BLOCK = 128
EPS = 1e-6

kernel_name = "hybrid_stickbreak_pool_convffn_layer"


def rmsnorm(x, gain):
    xf = x.astype(jnp.float32)
    y = xf * lax.rsqrt(jnp.mean(xf * xf, axis=-1, keepdims=True) + EPS)
    return (y * gain.astype(jnp.float32)).astype(x.dtype)


def stick_breaking_attention(q, k, v):
    B, S, H, Dh = q.shape
    nb = S // Q_BLOCK
    scale = Dh ** -0.5
    qb = q.reshape(B, nb, Q_BLOCK, H, Dh).transpose(1, 0, 3, 2, 4)
    starts = jnp.arange(nb, dtype=jnp.int32) * Q_BLOCK
    key_pos = jnp.arange(S, dtype=jnp.int32)

    def block(args):
        q_i, start = args
        z = jnp.einsum('bhqd,bkhd->bhqk', q_i, k).astype(jnp.float32) * scale
        q_pos = start + jnp.arange(Q_BLOCK, dtype=jnp.int32)
        causal = key_pos[None, :] < q_pos[:, None]
        log_1m_beta = jnp.where(causal, jax.nn.log_sigmoid(-z), 0.0)
        suffix = lax.cumsum(log_1m_beta, axis=3, reverse=True) - log_1m_beta
        log_a = jax.nn.log_sigmoid(z) + suffix
        a = jnp.where(causal, jnp.exp(log_a), 0.0)
        return jnp.einsum('bhqk,bkhd->bqhd', a.astype(v.dtype), v)

    out = lax.map(block, (qb, starts))
    return out.transpose(1, 0, 2, 3, 4).reshape(B, S, H * Dh)


def multiscale_causal_pool(u):
    B, S, _ = u.shape
    groups = u.astype(jnp.float32).reshape(B, S, N_POOL_GROUPS, POOL_GROUP_WIDTH)
    csum = jnp.pad(jnp.cumsum(groups, axis=1), ((0, 0), (1, 0), (0, 0), (0, 0)))
    pos = jnp.arange(S, dtype=jnp.int32)
    means = []
    for g, w in enumerate(POOL_WINDOWS):
        upper = csum[:, 1:, g]
        lower = jnp.pad(csum[:, :S + 1 - w, g], ((0, 0), (w - 1, 0), (0, 0)))
        count = jnp.minimum(pos + 1, w).astype(jnp.float32)[None, :, None]
        means.append((upper - lower) / count)
    return jnp.stack(means, axis=2) - groups


def causal_depthwise_conv(h, w, b):
    S = h.shape[1]
    hp = jnp.pad(h, ((0, 0), (CONV_WIDTH - 1, 0), (0, 0)))
    out = b
    for j in range(CONV_WIDTH):
        out = out + hp[:, j:j + S] * w[j]
    return out


def setup_inputs(seed: int = 0) -> dict:
    key = jax.random.key(seed)
    ks = jax.random.split(key, 20)
    f32 = jnp.float32

    def w(k, shape, fan_in):
        return jax.random.normal(k, shape, f32) * (fan_in ** -0.5)

    def gain(k, shape):
        return 1.0 + 0.02 * jax.random.normal(k, shape, f32)

    return {
        "x": jax.random.normal(ks[0], (BATCH, SEQ, D_MODEL), f32),
        "p": jax.random.normal(ks[1], (DEPTH, BATCH, SEQ, PLE_DIM), f32),
        "norm_mix_pre": gain(ks[2], (DEPTH, D_MODEL)),
        "w_in": w(ks[3], (DEPTH, D_MODEL, IN_WIDTH), D_MODEL),
        "w_attn_branch": w(ks[4], (DEPTH, ATTN_WIDTH, D_MODEL), ATTN_WIDTH),
        "w_pool_group": w(ks[5], (DEPTH, N_POOL_GROUPS, POOL_GROUP_WIDTH, POOL_GROUP_WIDTH), POOL_GROUP_WIDTH),
        "pool_scale": gain(ks[6], (DEPTH, POOL_WIDTH)),
        "w_pool_branch": w(ks[7], (DEPTH, POOL_WIDTH, D_MODEL), POOL_WIDTH),
        "w_out": w(ks[8], (DEPTH, D_MODEL, D_MODEL), D_MODEL),
        "norm_mix_post": gain(ks[9], (DEPTH, D_MODEL)),
        "norm_ffn_pre": gain(ks[10], (DEPTH, D_MODEL)),
        "w_up": w(ks[11], (DEPTH, D_MODEL, 2 * D_FF), D_MODEL),
        "conv_w": w(ks[12], (DEPTH, CONV_WIDTH, 2 * D_FF), CONV_WIDTH),
        "conv_b": 0.01 * jax.random.normal(ks[13], (DEPTH, 2 * D_FF), f32),
        "w_down": w(ks[14], (DEPTH, D_FF, D_MODEL), D_FF),
        "norm_ffn_post": gain(ks[15], (DEPTH, D_MODEL)),
        "w_ple": w(ks[16], (DEPTH, PLE_DIM, D_MODEL), PLE_DIM),
        "w_ple_gate": w(ks[17], (DEPTH, D_MODEL, D_MODEL), D_MODEL),
        "norm_ple_post": gain(ks[18], (DEPTH, D_MODEL)),
    }


def reference(x, p, norm_mix_pre, w_in, w_attn_branch, w_pool_group, pool_scale, w_pool_branch, w_out,
              norm_mix_post, norm_ffn_pre, w_up, conv_w, conv_b, w_down, norm_ffn_post, w_ple, w_ple_gate,
              norm_ple_post):
    B, S, _ = x.shape
    splits = [ATTN_WIDTH, 2 * ATTN_WIDTH, 3 * ATTN_WIDTH, 3 * ATTN_WIDTH + POOL_WIDTH,
              3 * ATTN_WIDTH + POOL_WIDTH + D_MODEL]
    for i in range(DEPTH):
        h = rmsnorm(x, norm_mix_pre[i])
        proj = h @ w_in[i]
        q, k, v, u, g_attn, g_pool = jnp.split(proj, splits, axis=-1)
        q = q.reshape(B, S, N_ATTN_HEADS, HEAD_DIM)
        k = k.reshape(B, S, N_ATTN_HEADS, HEAD_DIM)
        v = v.reshape(B, S, N_ATTN_HEADS, HEAD_DIM)
        y_attn = stick_breaking_attention(q, k, v) @ w_attn_branch[i]

        pooled = multiscale_causal_pool(u).astype(u.dtype)
        pooled = jnp.einsum('bsgc,gcd->bsgd', pooled, w_pool_group[i]).reshape(B, S, POOL_WIDTH)
        y_pool = (pooled * pool_scale[i]) @ w_pool_branch[i]

        mixed = jax.nn.sigmoid(g_attn) * y_attn + jax.nn.sigmoid(g_pool) * y_pool
        x = x + rmsnorm(mixed @ w_out[i], norm_mix_post[i])

        h = rmsnorm(x, norm_ffn_pre[i])
        up = causal_depthwise_conv(h @ w_up[i], conv_w[i], conv_b[i])
        gate, val = jnp.split(up, 2, axis=-1)
        y_ffn = (jax.nn.gelu(gate, approximate=True) * val) @ w_down[i]
        x = x + rmsnorm(y_ffn, norm_ffn_post[i])

        e = p[i] @ w_ple[i]
        x = x + rmsnorm(jax.nn.sigmoid(x @ w_ple_gate[i]) * e, norm_ple_post[i])
    return x
```

```python
import numpy as np
import concourse.bass as bass
import concourse.mybir as mybir
from concourse.bass_utils import run_bass_kernel_spmd

F32 = mybir.dt.float32
BF16 = mybir.dt.bfloat16
U8 = mybir.dt.uint8
AF = mybir.ActivationFunctionType
ALU = mybir.AluOpType

D = 2048
NH = 8
DFF = 5632
NFC = DFF // 128
PLE = 256
EPS = 1e-6
NVB = 32
QSCALE = 128.0 ** -0.5
SLOT_ELEMS = 4096
NSLOT = 4
SB_BYTES = 212480

CM_IDENT = 0
CM_NTRI = 1
CM_NONES = 2
CM_NTRI_P = 3
CM_NONES_P = 4
CM_RW = 5
CM_RP = 9
CM_RW0 = 13
CM_RP0 = 17
CM_MASK2A = 21
CM_MASK2B = 25
CM_MASK1 = 29
CM_N = 33

CF_PSCALE = 0
CF_CONVW = 8
CF_CONVB = 8 + 264
CF_N = 8 + 264 + 88


class DSem:
    def __init__(self, h):
        self.h = h
        self.count = 0


class Buf:
    def __init__(self, SB, lo, nelem, dt):
        self.esz = 4 if dt == F32 else 2
        self.lo = lo
        self.n = nelem
        self.dt = dt
        self.ap = SB[:, lo:lo + nelem * self.esz].bitcast(dt)

    def v(self, a=0, b=None):
        b = self.n if b is None else b
        return self.ap[:, a:b]

    def r(self, a=0, b=None):
        b = self.n if b is None else b
        return ("sb", self.lo + a * self.esz, self.lo + b * self.esz)


class Prog:
    COMPUTE = ("pe", "act", "dve", "pool")

    def __init__(self, nc, esems):
        self.nc = nc
        self.q = {e: [] for e in ("pe", "act", "dve", "pool", "sp")}
        self.esem = esems
        self.cnt = {e: 0 for e in self.COMPUTE}
        self.waited = {e: {} for e in self.q}
        self.regs = {}
        self.dsems = []
        self.nwaits = 0

    def _entries(self, reg):
        sp, lo, hi = reg
        lst = self.regs.setdefault(sp, [])
        return [e for e in lst if e[0] < hi and lo < e[1]], lst

    def _collect(self, reads, writes):
        raw, other = {}, {}

        def add(d, tok):
            if tok is None:
                return
            k = tok[0]
            if k not in d or d[k][2] < tok[2]:
                d[k] = tok

        for reg in reads:
            ov, _ = self._entries(reg)
            for e in ov:
                add(raw, e[2])
        for reg in writes:
            ov, _ = self._entries(reg)
            for e in ov:
                add(other, e[2])
                for t in e[3].values():
                    add(other, t)
        return raw, other

    def _commit(self, tok, reads, writes):
        for reg in reads:
            sp, lo, hi = reg
            ov, lst = self._entries(reg)
            pos = lo
            for e in sorted(ov, key=lambda e: e[0]):
                if e[0] > pos:
                    lst.append([pos, e[0], None, {tok[0]: tok}])
                old = e[3].get(tok[0])
                if old is None or old[2] < tok[2]:
                    e[3][tok[0]] = tok
                pos = max(pos, e[1])
            if pos < hi:
                lst.append([pos, hi, None, {tok[0]: tok}])
        for reg in writes:
            sp, lo, hi = reg
            ov, lst = self._entries(reg)
            for e in ov:
                lst.remove(e)
                if e[0] < lo:
                    lst.append([e[0], lo, e[2], dict(e[3])])
                if e[1] > hi:
                    lst.append([hi, e[1], e[2], dict(e[3])])
            lst.append([lo, hi, tok, {}])

    def _waits(self, eng, raw, other):
        need = dict(raw)
        for k, t in other.items():
            if k == eng and eng == "pe":
                continue
            if k not in need or need[k][2] < t[2]:
                need[k] = t
        out = []
        wd = self.waited[eng]
        for k, t in need.items():
            if wd.get(k, 0) >= t[2]:
                continue
            wd[k] = t[2]
            out.append((t[1], t[2]))
        self.nwaits += len(out)
        return out

    def op(self, eng, fn, reads=(), writes=()):
        raw, other = self._collect(reads, writes)
        waits = self._waits(eng, raw, other)
        n = self.cnt[eng] + 1
        self.cnt[eng] = n
        sem = self.esem[eng]
        tok = (eng, sem, n)
        def emit(e, waits=waits, fn=fn, sem=sem):
            for (s, v) in waits:
                e.wait_ge(s, v)
            fn(e).then_inc(sem, 1)
        self.q[eng].append(emit)
        self._commit(tok, reads, writes)
        return tok

    def dma(self, q, dsem, pairs, reads=(), writes=()):
        raw, other = self._collect(reads, writes)
        waits = self._waits(q, raw, other)
        dsem.count += 16 * len(pairs)
        tok = (id(dsem), dsem.h, dsem.count)
        def emit(e, waits=waits, pairs=pairs, h=dsem.h):
            for (s, v) in waits:
                e.wait_ge(s, v)
            for (o, i) in pairs:
                e.dma_start(out=o, in_=i).then_inc(h, 16)
        self.q[q].append(emit)
        self._commit(tok, reads, writes)
        return tok

    def final_wait(self, q):
        sems = [(d.h, d.count) for d in self.dsems if d.count > 0]
        sems += [(self.esem[e], self.cnt[e]) for e in self.COMPUTE if self.cnt[e] > 0]
        def emit(e):
            for (s, v) in sems:
                e.wait_ge(s, v)
        self.q[q].append(emit)


def build_program(dbg=None):
    nc = bass.Bass("TRN2", target_bir_lowering=False)
    dt_in = lambda name, shape: nc.dram_tensor(name, shape, F32, kind="ExternalInput").ap()
    xall = dt_in("xall", [4096, D])
    pown = dt_in("pown", [2048, PLE])
    w_in = dt_in("w_in", [D, 8192])
    w_ab = dt_in("w_ab", [1024, D])
    w_pg = dt_in("w_pg", [4, 256, 256])
    w_pb = dt_in("w_pb", [1024, D])
    w_o = dt_in("w_o", [D, D])
    w_up = dt_in("w_up", [D, 2 * DFF])
    w_d = dt_in("w_d", [DFF, D])
    w_ple = dt_in("w_ple", [PLE, D])
    w_pgate = dt_in("w_pgate", [D, D])
    gains = dt_in("gains", [5, D])
    colf = dt_in("colf", [128, CF_N])
    cmat = dt_in("cmat", [128, CM_N * 128])
    out = nc.dram_tensor("out", [2048, D], F32, kind="ExternalOutput").ap()
    KTs = nc.dram_tensor("KTs", [NH, 128, 4096], BF16, kind="Internal").ap()
    VS = nc.dram_tensor("VS", [NH, 128, NVB, 128], BF16, kind="Internal").ap()
    dbg_out = {}
    if dbg:
        for name, shape in dbg.items():
            dbg_out[name] = nc.dram_tensor("dbg_" + name, shape, F32, kind="ExternalOutput").ap()

    wdefs = {}

    def defw(name, src, K, c_lo, c_hi, cw, ksplit=16):
        chunks = []
        nkc = K // 128
        for ks in range(0, nkc, ksplit):
            nk = min(ksplit, nkc - ks)
            for c0 in range(c_lo, c_hi, cw):
                chunks.append((ks, nk, c0, cw))
        scr = nc.dram_tensor("ws_" + name, [len(chunks), 128, SLOT_ELEMS], BF16, kind="Internal").ap()
        wdefs[name] = (src, chunks, scr)

    defw("q", w_in, D, 0, 1024, 256)
    defw("u", w_in, D, 3072, 4096, 256)
    defw("ga", w_in, D, 4096, 6144, 256)
    defw("gp", w_in, D, 6144, 8192, 256)
    defw("ab", w_ab, 1024, 0, D, 256)
    defw("pb", w_pb, 1024, 0, D, 256)
    defw("o", w_o, D, 0, D, 256)
    defw("upg", w_up, D, 0, DFF, 256)
    defw("upv", w_up, D, DFF, 2 * DFF, 256)
    defw("d", w_d, DFF, 0, D, 512, ksplit=8)
    defw("ple", w_ple, PLE, 0, D, 2048)
    defw("pgate", w_pgate, D, 0, D, 256)

    import contextlib
    with contextlib.ExitStack() as es:
        SBT = es.enter_context(nc.sbuf_tensor("SB", [128, SB_BYTES], U8))
        PS = es.enter_context(nc.psum_tensor("PS", [128, 8 * 512], F32))
        esems = {e: es.enter_context(nc.semaphore("sem_" + e)) for e in Prog.COMPUTE}
        P = Prog(nc, esems)

        def new_dsem(name):
            d = DSem(es.enter_context(nc.semaphore(name)))
            P.dsems.append(d)
            return d

        off = [0]

        def alloc(nelem, dt):
            esz = 4 if dt == F32 else 2
            lo = (off[0] + 63) // 64 * 64
            off[0] = lo + nelem * esz
            assert off[0] <= SB_BYTES, off[0]
            return Buf(SBT, lo, nelem, dt)

        cm = alloc(CM_N * 128, BF16)
        cf = alloc(CF_N, F32)
        wpg = alloc(4 * 2 * 256, BF16)
        A = alloc(4 * D, F32)
        xn = alloc(D, BF16)
        gbc = [alloc(D, F32)]
        hT = alloc(16 * 512, BF16)
        small = alloc(128, F32)
        uph = alloc(88 * 2, F32)
        uprev = alloc(1024, BF16)
        gsq = alloc(512, BF16)
        kv_lo = off[0]
        Bf = alloc(4 * D, F32)
        ring = [alloc(SLOT_ELEMS, BF16) for _ in range(NSLOT)]
        union_lo = off[0]
        qT = alloc(8 * 256, BF16)
        oT = alloc(8 * 256, BF16)
        ub = alloc(2 * 1024, BF16)
        plT = alloc(8 * 256, BF16)
        pgT = alloc(8 * 256, BF16)
        mxT = alloc(16 * 256, BF16)
        Eb = [alloc(512, F32) for _ in range(2)]
        Lb = [alloc(512, BF16) for _ in range(3)]
        ab_ = [alloc(512, BF16) for _ in range(3)]
        Sb = [alloc(512, BF16) for _ in range(2)]
        kvr = [alloc(2048, BF16) for _ in range(3)]
        sg = [alloc(2 * 256, F32) for _ in range(2)]
        tt = alloc(2 * 256, F32)
        mixer_hi = off[0]
        off[0] = kv_lo
        wk = alloc(16 * 1024, BF16)
        wv = alloc(16 * 1024, BF16)
        hT4 = hT
        kst = [alloc(512, BF16) for _ in range(3)]
        vst = [alloc(1024, BF16) for _ in range(2)]
        kv_hi = off[0]
        off[0] = union_lo
        actT = alloc(NFC * 512, BF16)
        ubuf = [alloc(514, F32) for _ in range(2)]
        tcv = [alloc(512, F32) for _ in range(5)]
        gl = [alloc(512, F32) for _ in range(2)]
        ffn_hi = off[0]
        off[0] = union_lo
        pin = alloc(4 * PLE, F32)
        pnb = alloc(4 * PLE, BF16)
        pT = alloc(2 * 512, BF16)
        sgp = [alloc(256, F32) for _ in range(2)]
        wpleb = alloc(2 * 2048, BF16)
        ple_hi = off[0]
        off[0] = max(mixer_hi, kv_hi, ffn_hi, ple_hi)
        sb_used = off[0]

        s_wgrp = [new_dsem("s_wg%d" % i) for i in range(4)]
        s_kvw = new_dsem("s_kvw")
        s_A = [new_dsem("s_A%d" % i) for i in range(4)]
        s_Ast = [new_dsem("s_Ast%d" % i) for i in range(4)]
        s_g = [new_dsem("s_g%d" % i) for i in range(1)]
        s_ring = [new_dsem("s_ring%d" % i) for i in range(NSLOT)]
        s_kvr = [new_dsem("s_kvr%d" % i) for i in range(3)]
        s_kst = [new_dsem("s_kst%d" % i) for i in range(3)]
        s_vst = [new_dsem("s_vst%d" % i) for i in range(2)]
        s_pin = new_dsem("s_pin")
        s_wple = new_dsem("s_wple")
        s_c = [new_dsem("s_c%d" % i) for i in range(5)]
        s_dbg = new_dsem("s_dbg")

        def bank(b, n=512, c0=0):
            return PS[:, b * 512 + c0:b * 512 + c0 + n]

        def bankr(b):
            return ("ps", b, b + 1)

        fb = [0]

        def next_fbank():
            b = fb[0] % 6
            fb[0] += 1
            return b

        tb = [0]

        def next_tbank():
            b = 6 + tb[0] % 2
            tb[0] += 1
            return b

        def cmv(idx, n=128):
            return cm.v(idx * 128, idx * 128 + n)

        cmr = cm.r()

        P.dma("pool", s_c[0], [(cm.v(), cmat)], writes=[cm.r()])
        P.dma("sp", s_c[1], [(cf.v(), colf)], writes=[cf.r()])
        P.dma("pool", s_c[2],
              [(wpg.v().rearrange("p (g k c) -> p g k c", g=4, k=2),
                w_pg.rearrange("g (k p) c -> p g k c", p=128))], writes=[wpg.r()])
        for (wb, c0, sc) in ((wk, 1024, s_c[3]), (wv, 2048, s_c[4])):
            pairs = []
            for k4 in range(0, 16, 4):
                pairs.append((wb.v().rearrange("p (k c) -> p k c", c=1024)[:, k4:k4 + 4, :],
                              w_in[k4 * 128:(k4 + 4) * 128, c0:c0 + 1024].rearrange("(k p) c -> p k c", p=128)))
            P.dma("pool", sc, pairs, writes=[wb.r()])
        P.op("dve", lambda e: e.memset(uph.v(), 0.0), writes=[uph.r()])
        P.op("dve", lambda e: e.memset(uprev.v(), 0.0), writes=[uprev.r()])

        conv_order = [("q", 0), ("u", 0), ("ga", 1), ("ab", 1), ("gp", 1), ("pb", 1), ("o", 1),
                      ("upg", 2), ("upv", 2), ("d", 3), ("ple", 3), ("pgate", 3)]
        wtok = {}
        grp_pairs = {g: [] for g in range(4)}
        for name, g in conv_order:
            src, chunks, scr = wdefs[name]
            for ci, (ks, nk, c0, cw) in enumerate(chunks):
                dst = scr[ci][:, 0:nk * cw].rearrange("p (k c) -> p k c", c=cw)
                for k4 in range(0, nk, 4):
                    kk = min(4, nk - k4)
                    r0 = (ks + k4) * 128
                    grp_pairs[g].append((dst[:, k4:k4 + kk, :],
                                         src[r0:r0 + kk * 128, c0:c0 + cw].rearrange("(k p) c -> p k c", p=128)))
        for g in range(2):
            tok = P.dma("pool", s_wgrp[g], grp_pairs[g], reads=[wk.r(), wv.r(), cm.r()], writes=[("wscr", g, g + 1)])

        def late_conversions():
            for g in range(2, 4):
                P.dma("pool", s_wgrp[g], grp_pairs[g], reads=[("vs", NVB - 1, NVB)], writes=[("wscr", g, g + 1)])
        wgrp_of = {name: g for name, g in conv_order}

        ring_state = {"next": 0, "queue": []}

        def ring_issue(item):
            name, ci = item
            src, chunks, scr = wdefs[name]
            ks, nk, c0, cw = chunks[ci]
            slot = ring_state["next"] % NSLOT
            ring_state["next"] += 1
            n = nk * cw
            g = wgrp_of[name]
            P.dma("sp", s_ring[slot], [(ring[slot].v(0, n), scr[ci][:, 0:n])],
                  reads=[("wscr", g, g + 1)], writes=[ring[slot].r()])
            return slot

        class WStream:
            def __init__(self, items):
                self.items = items
                self.issued = 0
                self.slots = {}
                self.cur = 0
                for _ in range(min(NSLOT, len(items))):
                    self._issue()

            def _issue(self):
                if self.issued < len(self.items):
                    self.slots[self.issued] = ring_issue(self.items[self.issued])
                    self.issued += 1

            def take(self, name, ci):
                assert self.items[self.cur] == (name, ci), (self.items[self.cur], name, ci)
                slot = self.slots.pop(self.cur)
                self.cur += 1
                return ring[slot]

            def done(self):
                self._issue()

        def load_gain(gi, slot):
            P.dma("sp", s_g[slot], [(gbc[slot].v(), gains[gi].partition_broadcast(128))], writes=[gbc[slot].r()])

        def rstd_from(col_in, col_out, scratch_col):
            ci, co, cs_ = small.v(col_in, col_in + 1), small.v(col_out, col_out + 1), small.v(scratch_col, scratch_col + 1)
            P.op("dve", lambda e: e.tensor_scalar(out=cs_, in0=ci, scalar1=1.0 / D, scalar2=EPS, op0=ALU.mult, op1=ALU.add),
                 reads=[small.r(col_in, col_in + 1)], writes=[small.r(scratch_col, scratch_col + 1)])
            P.op("act", lambda e: e.activation(out=cs_, in_=cs_, func=AF.Ln),
                 reads=[small.r(scratch_col, scratch_col + 1)], writes=[small.r(scratch_col, scratch_col + 1)])
            P.op("act", lambda e: e.activation(out=co, in_=cs_, func=AF.Exp, scale=-0.5),
                 reads=[small.r(scratch_col, scratch_col + 1)], writes=[small.r(col_out, col_out + 1)])

        def transpose_block(xs, hTb, NT, jj):
            xnr = xn.r(xs * D, (xs + 1) * D)
            for half in range(2):
                b = next_tbank()
                pb = bank(b).bitcast(BF16)

                def tr(e, half=half, pb=pb, xs=xs):
                    ins = None
                    for k in range(8):
                        kc = half * 8 + k
                        ins = e.transpose(pb[:, k * 128:(k + 1) * 128],
                                          xn.v(xs * D + kc * 128, xs * D + (kc + 1) * 128), cmv(CM_IDENT))
                    return ins
                P.op("pe", tr, reads=[xnr, cmr], writes=[bankr(b)])
                dst = hTb.v().rearrange("p (k t) -> p k t", t=NT)[:, half * 8:half * 8 + 8, jj * 128:(jj + 1) * 128]
                srcv = pb.rearrange("p (k t) -> p k t", t=128)
                if half == 0:
                    P.op("act", lambda e, dst=dst, srcv=srcv: e.activation(out=dst, in_=srcv, func=AF.Copy),
                         reads=[bankr(b)], writes=[hTb.r()])
                else:
                    P.op("dve", lambda e, dst=dst, srcv=srcv: e.tensor_copy(dst, srcv),
                         reads=[bankr(b)], writes=[hTb.r()])

        def front_end(src_blocks, gslot, hTb, NT, j0=0):
            for j, (xap, xr) in enumerate(src_blocks):
                xs = 0
                sq = small.v(j, j + 1)
                rs = small.v(8 + j, 9 + j)
                xnv = xn.v(xs * D, (xs + 1) * D)
                xnr = xn.r(xs * D, (xs + 1) * D)
                P.op("act", lambda e, xap=xap, xnv=xnv, sq=sq: e.activation(out=xnv, in_=xap, func=AF.Square, accum_out=sq),
                     reads=[xr], writes=[xnr, small.r(j, j + 1)])
                rstd_from(j, 8 + j, 16 + j)
                P.op("dve", lambda e, xap=xap, rs=rs, xnv=xnv: e.scalar_tensor_tensor(
                    out=xnv, in0=xap, scalar=rs, in1=gbc[gslot].v(), op0=ALU.mult, op1=ALU.mult),
                    reads=[xr, small.r(8 + j, 9 + j), gbc[gslot].r()], writes=[xnr])
                transpose_block(xs, hTb, NT, j0 + j)

        def mm_group(b, pieces, first_start=True):
            def f(e):
                ins = None
                for i, (o, l, r) in enumerate(pieces):
                    ins = e.matmul(o, l, r, start=(first_start and i == 0), stop=(i == len(pieces) - 1),
                                   skip_group_check=True)
                return ins
            return f

        def w3(slotbuf, nk, cw):
            return slotbuf.v(0, nk * cw).rearrange("p (k c) -> p k c", c=cw)

        def formA(ws, name, ci, actTb, NT, mlist, evac):
            src, chunks, scr = wdefs[name]
            ks, nk, c0, cw = chunks[ci]
            slot = ws.take(name, ci)
            wv3 = w3(slot, nk, cw)
            for ml in mlist:
                b = next_fbank()
                pieces = [(bank(b, NT), wv3[:, k, ml * 128:(ml + 1) * 128],
                           actTb.v((ks + k) * NT, (ks + k + 1) * NT)) for k in range(nk)]
                P.op("pe", mm_group(b, pieces), reads=[slot.r(), actTb.r()], writes=[bankr(b)])
                evac(b, ml)
            ws.done()

        load_gain(0, 0)
        for t in range(8):
            for pr in range(2):
                for j in range(2):
                    vb = t * 4 + pr * 2 + j
                    P.dma("sp", s_A[j], [(A.v(j * D, (j + 1) * D), xall[vb * 128:(vb + 1) * 128, :])],
                          writes=[A.r(j * D, (j + 1) * D)])
                blocks = [(A.v(j * D, (j + 1) * D), A.r(j * D, (j + 1) * D)) for j in range(2)]
                front_end(blocks, 0, hT4, 512, j0=pr * 2)
            wk3 = wk.v().rearrange("p (k c) -> p k c", c=1024)
            wv3 = wv.v().rearrange("p (k c) -> p k c", c=1024)
            for h in range(NH):
                b = next_fbank()
                pieces = [(bank(b), wk3[:, k, h * 128:(h + 1) * 128], hT4.v(k * 512, (k + 1) * 512)) for k in range(16)]
                P.op("pe", mm_group(b, pieces), reads=[wk.r(), hT4.r()], writes=[bankr(b)])
                st = (t * NH + h) % 3
                if h % 2 == 0:
                    P.op("act", lambda e, st=st, b=b: e.activation(out=kst[st].v(), in_=bank(b), func=AF.Copy),
                         reads=[bankr(b)], writes=[kst[st].r()])
                else:
                    P.op("dve", lambda e, st=st, b=b: e.tensor_copy(kst[st].v(), bank(b)),
                         reads=[bankr(b)], writes=[kst[st].r()])
                P.dma("sp", s_kst[st], [(KTs[h][:, t * 512:(t + 1) * 512], kst[st].v())],
                      reads=[kst[st].r()], writes=[("kt", h * 8 + t, h * 8 + t + 1)])
            for jj in range(4):
                vb = t * 4 + jj
                st = vb % 2
                for cg in range(2):
                    b = next_fbank()
                    pieces = [(bank(b), hT4.v(k * 512 + jj * 128, k * 512 + (jj + 1) * 128),
                               wv3[:, k, cg * 512:(cg + 1) * 512]) for k in range(16)]
                    P.op("pe", mm_group(b, pieces), reads=[wv.r(), hT4.r()], writes=[bankr(b)])
                    if cg == 0:
                        P.op("act", lambda e, st=st, b=b: e.activation(out=vst[st].v(0, 512), in_=bank(b), func=AF.Copy),
                             reads=[bankr(b)], writes=[vst[st].r(0, 512)])
                    else:
                        P.op("dve", lambda e, st=st, b=b: e.tensor_copy(vst[st].v(512, 1024), bank(b)),
                             reads=[bankr(b)], writes=[vst[st].r(512, 1024)])
                P.dma("sp", s_vst[st],
                      [(VS[:, :, vb, :].rearrange("h p d -> p h d"), vst[st].v().rearrange("p (h d) -> p h d", d=128))],
                      reads=[vst[st].r()], writes=[("vs", vb, vb + 1)])

        late_conversions()

        def mixer_items():
            it = [("q", c) for c in range(4)] + [("u", c) for c in range(4)]
            for M in range(8):
                it += [("ga", M), ("ab", M), ("gp", M), ("pb", M)]
            it += [("o", g) for g in range(8)]
            return it

        def ffn_items(up_only=False):
            it = []
            for Fg in range(22):
                it += [("upg", Fg), ("upv", Fg)]
            if not up_only:
                for cg in range(4):
                    for ksi in range(6):
                        it.append(("d", ksi * 4 + cg))
            return it

        def ple_items():
            return [("pgate", g) for g in range(8)]

        def norm_residual(ncg, NB, gslot, ybuf, sscol0, resid, dest):
            for j in range(NB):
                c = sscol0 + j * 8
                ssum = small.v(32 + j, 33 + j)
                P.op("dve", lambda e, c=c, ssum=ssum: e.tensor_reduce(out=ssum, in_=small.v(c, c + ncg),
                                                                      axis=mybir.AxisListType.X, op=ALU.add),
                     reads=[small.r(c, c + ncg)], writes=[small.r(32 + j, 33 + j)])
                rstd_from(32 + j, 36 + j, 20 + j)
                ssum = small.v(36 + j, 37 + j)
                yv = ybuf.v(j * D, (j + 1) * D)
                yr = ybuf.r(j * D, (j + 1) * D)
                P.op("dve", lambda e, yv=yv, ssum=ssum: e.scalar_tensor_tensor(
                    out=yv, in0=yv, scalar=ssum, in1=gbc[gslot].v(), op0=ALU.mult, op1=ALU.mult),
                    reads=[yr, small.r(36 + j, 37 + j), gbc[gslot].r()], writes=[yr])
                rv, rr = resid[j]
                dv, dr = dest[j]
                P.op("dve", lambda e, yv=yv, rv=rv, dv=dv: e.tensor_tensor(out=dv, in0=yv, in1=rv, op=ALU.add),
                     reads=[yr, rr], writes=[dr])

        def formB_tokmajor(ws, name, cis, actTb, NT, NB, ybuf, sscol0, nk_total_chunks=None):
            for cg, chunk_list in enumerate(cis):
                banks = [next_fbank() for _ in range(NB)]
                for idx, ci in enumerate(chunk_list):
                    src, chunks, scr = wdefs[name]
                    ks, nk, c0, cw = chunks[ci]
                    slot = ws.take(name, ci)
                    wv3_ = w3(slot, nk, cw)
                    for j in range(NB):
                        b = banks[j]
                        pieces = [(bank(b, cw), actTb.v((ks + k) * NT + j * 128, (ks + k) * NT + (j + 1) * 128), wv3_[:, k, :])
                                  for k in range(nk)]
                        P.op("pe", mm_group(b, pieces, first_start=(idx == 0)), reads=[slot.r(), actTb.r()],
                             writes=[bankr(b)])
                    ws.done()
                for j in range(NB):
                    b = banks[j]
                    yv = ybuf.v(j * D + cg * cw, j * D + (cg + 1) * cw)
                    yr = ybuf.r(j * D + cg * cw, j * D + (cg + 1) * cw)
                    c = sscol0 + j * 8 + cg
                    P.op("dve", lambda e, yv=yv, b=b, cw=cw: e.tensor_copy(yv, bank(b, cw)), reads=[bankr(b)], writes=[yr])
                    P.op("act", lambda e, yv=yv, c=c, cw=cw: e.activation(out=gsq.v(0, cw), in_=yv, func=AF.Square,
                                                                   accum_out=small.v(c, c + 1)),
                         reads=[yr], writes=[gsq.r(), small.r(c, c + 1)])


        def attention(vq0, NB, NT):
            HPU = 2
            W = HPU * NT
            nun = NH // HPU
            vmax = vq0 + NB - 1
            units = [(hg, kb) for hg in range(nun) for kb in range(vmax, -1, -1)]
            n = len(units)
            loaded = {}
            slot_key = {}
            kvi = [0]

            lastuse = {}

            def kv_get(hg, G, i, prefetch=False):
                key = (hg, G)
                if key in loaded:
                    buf, slot = loaded[key]
                    if not prefetch:
                        lastuse[slot] = i
                    return buf
                slot = kvi[0] % 3
                if lastuse.get(slot, -100) + 5 > i:
                    assert prefetch, (hg, G, i, lastuse)
                    return None
                kvi[0] += 1
                if slot in slot_key:
                    loaded.pop(slot_key[slot], None)
                slot_key[slot] = key
                buf = kvr[slot]
                pairs, rd = [], []
                for hh in range(HPU):
                    h = hg * HPU + hh
                    pairs.append((buf.v(hh * 512, (hh + 1) * 512), KTs[h][:, G * 512:(G + 1) * 512]))
                    pairs.append((buf.v(1024 + hh * 512, 1024 + (hh + 1) * 512).rearrange("p (b d) -> p b d", d=128),
                                  VS[h][:, G * 4:(G + 1) * 4, :]))
                    rd.append(("kt", h * 8 + G, h * 8 + G + 1))
                rd.append(("vs", G * 4, G * 4 + 4))
                P.dma("sp", s_kvr[slot], pairs, reads=rd, writes=[buf.r()])
                loaded[key] = (buf, slot)
                if not prefetch:
                    lastuse[slot] = i
                return buf

            zb_of, kvb = {}, {}

            def maskidx(kb):
                if kb < vq0:
                    return None
                if NB == 1:
                    return CM_MASK1
                return CM_MASK2B if kb == vq0 else CM_MASK2A

            def stage0(i):
                hg, kb = units[i]
                G, bi = kb // 4, kb % 4
                buf = kv_get(hg, G, i)
                kvb[i] = buf
                if i + 4 < n:
                    hg2, kb2 = units[i + 4]
                    kv_get(hg2, kb2 // 4, i, prefetch=True)
                zb = i % 4
                zb_of[i] = zb
                pieces = []
                for hh in range(HPU):
                    h = hg * HPU + hh
                    pieces.append((bank(zb, NT, hh * NT), buf.v(hh * 512 + bi * 128, hh * 512 + (bi + 1) * 128),
                                   qT.v(h * NT, (h + 1) * NT)))
                P.op("pe", mm_group(zb, pieces), reads=[buf.r(), qT.r()], writes=[bankr(zb)])

            def stage1(i):
                zb = zb_of[i]
                eb = Eb[i % 2]
                P.op("act", lambda e: e.activation(out=eb.v(0, W), in_=bank(zb, W), func=AF.Exp),
                     reads=[bankr(zb)], writes=[eb.r()])

            def stage2(i):
                hg, kb = units[i]
                eb = Eb[i % 2]
                lb = Lb[i % 3]
                P.op("act", lambda e: e.activation(out=lb.v(0, W), in_=eb.v(0, W), func=AF.Ln, bias=1.0),
                     reads=[eb.r()], writes=[lb.r()])
                mi = maskidx(kb)
                if mi is not None:
                    P.op("dve", lambda e: e.tensor_tensor(out=lb.v(0, W), in0=lb.v(0, W), in1=cm.v(mi * 128, mi * 128 + W),
                                                          op=ALU.mult),
                         reads=[lb.r(), cmr], writes=[lb.r()])

            def stage3(i):
                hg, kb = units[i]
                zb = zb_of[i]
                lb = Lb[i % 3]
                sbuf = Sb[hg % 2]
                first = (kb == vmax)
                prev = kb < 16
                tri = cmv(CM_NTRI_P if prev else CM_NTRI)
                ones = cmv(CM_NONES_P if prev else CM_NONES)
                pieces = [(bank(zb, W), tri, lb.v(0, W))]
                rds = [lb.r(), cmr]
                if not first:
                    pieces.append((bank(zb, W), ones, sbuf.v(0, W)))
                    rds.append(sbuf.r())
                P.op("pe", mm_group(zb, pieces, first_start=False), reads=rds, writes=[bankr(zb)])
                if first:
                    P.op("dve", lambda e: e.tensor_copy(sbuf.v(0, W), lb.v(0, W)), reads=[lb.r()], writes=[sbuf.r()])
                elif kb > 0:
                    P.op("dve", lambda e: e.tensor_tensor(out=sbuf.v(0, W), in0=sbuf.v(0, W), in1=lb.v(0, W), op=ALU.add),
                         reads=[lb.r(), sbuf.r()], writes=[sbuf.r()])

            def stage4(i):
                hg, kb = units[i]
                zb = zb_of[i]
                av = ab_[i % 3]
                P.op("act", lambda e: e.activation(out=av.v(0, W), in_=bank(zb, W), func=AF.Exp),
                     reads=[bankr(zb)], writes=[av.r()])
                mi = maskidx(kb)
                if mi is not None:
                    P.op("dve", lambda e: e.tensor_tensor(out=av.v(0, W), in0=av.v(0, W), in1=cm.v(mi * 128, mi * 128 + W),
                                                          op=ALU.mult),
                         reads=[av.r(), cmr], writes=[av.r()])

            def stage5(i):
                hg, kb = units[i]
                bi = kb % 4
                buf = kvb[i]
                av = ab_[i % 3]
                ob = 4 + hg % 2
                pieces = []
                for hh in range(HPU):
                    vo = 1024 + hh * 512 + bi * 128
                    pieces.append((bank(ob, NT, hh * NT), buf.v(vo, vo + 128), av.v(hh * NT, (hh + 1) * NT)))
                P.op("pe", mm_group(ob, pieces, first_start=(kb == vmax)), reads=[buf.r(), av.r()], writes=[bankr(ob)])
                if kb == 0:
                    h0 = hg * HPU
                    P.op("dve", lambda e: e.tensor_copy(oT.v(h0 * NT, (h0 + HPU) * NT), bank(ob, W)),
                         reads=[bankr(ob)], writes=[oT.r(h0 * NT, (h0 + HPU) * NT)])

            stages = [stage0, stage1, stage2, stage3, stage4, stage5]
            for s_ in range(n + 5):
                for k in (5, 4, 3, 2, 1, 0):
                    i = s_ - k
                    if 0 <= i < n:
                        stages[k](i)

        def mixer(vq0, NB, a0=0):
            NT = NB * 128
            ws = WStream(mixer_items())
            load_gain(0, 0)
            Ablk = [(A.v((a0 + j) * D, (a0 + j + 1) * D), A.r((a0 + j) * D, (a0 + j + 1) * D)) for j in range(NB)]
            front_end(Ablk, 0, hT, NT)
            load_gain(1, 0)
            for c in range(4):
                def ev(b, ml, c=c):
                    h = c * 2 + ml
                    P.op("act", lambda e: e.activation(out=qT.v(h * NT, (h + 1) * NT), in_=bank(b, NT), func=AF.Copy, scale=QSCALE),
                         reads=[bankr(b)], writes=[qT.r(h * NT, (h + 1) * NT)])
                formA(ws, "q", c, hT, NT, range(2), ev)
            for c in range(4):
                slot = ws.take("u", c)
                wv3_ = w3(slot, 16, 256)
                for j in range(NB):
                    b = next_fbank()
                    pieces = [(bank(b, 256), hT.v(k * NT + j * 128, k * NT + (j + 1) * 128), wv3_[:, k, :]) for k in range(16)]
                    P.op("pe", mm_group(b, pieces), reads=[slot.r(), hT.r()], writes=[bankr(b)])
                    uo = j * 1024 + c * 256
                    P.op("dve", lambda e, uo=uo, b=b: e.tensor_copy(ub.v(uo, uo + 256), bank(b, 256)),
                         reads=[bankr(b)], writes=[ub.r(uo, uo + 256)])
                ws.done()
            for cc in range(8):
                g = cc // 2
                b = next_fbank()
                pieces = []
                for j in range(NB):
                    first_blk = (vq0 + j == 16)
                    rw = cmv((CM_RW0 if first_blk else CM_RW) + g)
                    rp = cmv((CM_RP0 if first_blk else CM_RP) + g)
                    pieces.append((bank(b, 128, j * 128), ub.v(j * 1024 + cc * 128, j * 1024 + (cc + 1) * 128), rw))
                    pv = uprev.v(cc * 128, (cc + 1) * 128) if j == 0 else ub.v((j - 1) * 1024 + cc * 128, (j - 1) * 1024 + (cc + 1) * 128)
                    pieces.append((bank(b, 128, j * 128), pv, rp))
                P.op("pe", mm_group(b, pieces), reads=[ub.r(), uprev.r(), cmr], writes=[bankr(b)])
                P.op("act" if cc % 2 == 0 else "dve",
                     (lambda e, cc=cc, b=b: e.activation(out=plT.v(cc * NT, (cc + 1) * NT), in_=bank(b, NT), func=AF.Copy))
                     if cc % 2 == 0 else
                     (lambda e, cc=cc, b=b: e.tensor_copy(plT.v(cc * NT, (cc + 1) * NT), bank(b, NT))),
                     reads=[bankr(b)], writes=[plT.r(cc * NT, (cc + 1) * NT)])
            P.op("dve", lambda e: e.tensor_copy(uprev.v(), ub.v((NB - 1) * 1024, NB * 1024)),
                 reads=[ub.r((NB - 1) * 1024, NB * 1024)], writes=[uprev.r()])
            wpg4 = wpg.v().rearrange("p (g k c) -> p g k c", g=4, k=2)
            for dc in range(8):
                g, dd = dc // 2, dc % 2
                b = next_fbank()
                pieces = [(bank(b, NT), wpg4[:, g, kk, dd * 128:(dd + 1) * 128], plT.v((2 * g + kk) * NT, (2 * g + kk + 1) * NT))
                          for kk in range(2)]
                P.op("pe", mm_group(b, pieces), reads=[wpg.r(), plT.r()], writes=[bankr(b)])
                P.op("act", lambda e, dc=dc, b=b: e.activation(out=pgT.v(dc * NT, (dc + 1) * NT), in_=bank(b, NT), func=AF.Copy,
                                                               scale=cf.v(CF_PSCALE + dc, CF_PSCALE + dc + 1)),
                     reads=[bankr(b), cf.r()], writes=[pgT.r(dc * NT, (dc + 1) * NT)])
            attention(vq0, NB, NT)
            if dbg and NB == 2 and vq0 == 16 + 2 * dbg_cfg.get("dump_t", 0):
                for nm, bf_ in (("oT", oT), ("pgT", pgT), ("qT", qT)):
                    if nm in dbg_out:
                        P.dma("pool", s_dbg, [(dbg_out[nm], bf_.v())], reads=[bf_.r()])
            for M in range(8):
                sa, sp_ = sg[0], sg[1]
                def ev_ga(b, ml):
                    P.op("act", lambda e: e.activation(out=sa.v(ml * NT, (ml + 1) * NT), in_=bank(b, NT), func=AF.Sigmoid),
                         reads=[bankr(b)], writes=[sa.r(ml * NT, (ml + 1) * NT)])
                formA(ws, "ga", M, hT, NT, range(2), ev_ga)
                def ev_ab(b, ml):
                    P.op("dve", lambda e: e.tensor_tensor(out=tt.v(ml * NT, (ml + 1) * NT), in0=bank(b, NT),
                                                          in1=sa.v(ml * NT, (ml + 1) * NT), op=ALU.mult),
                         reads=[bankr(b), sa.r(ml * NT, (ml + 1) * NT)], writes=[tt.r(ml * NT, (ml + 1) * NT)])
                formA(ws, "ab", M, oT, NT, range(2), ev_ab)
                def ev_gp(b, ml):
                    P.op("act", lambda e: e.activation(out=sp_.v(ml * NT, (ml + 1) * NT), in_=bank(b, NT), func=AF.Sigmoid),
                         reads=[bankr(b)], writes=[sp_.r(ml * NT, (ml + 1) * NT)])
                formA(ws, "gp", M, hT, NT, range(2), ev_gp)
                def ev_pb(b, ml, M=M):
                    m = M * 2 + ml
                    P.op("dve", lambda e: e.tensor_tensor(out=sp_.v(ml * NT, (ml + 1) * NT), in0=bank(b, NT),
                                                          in1=sp_.v(ml * NT, (ml + 1) * NT), op=ALU.mult),
                         reads=[bankr(b), sp_.r(ml * NT, (ml + 1) * NT)], writes=[sp_.r(ml * NT, (ml + 1) * NT)])
                    P.op("dve", lambda e: e.tensor_tensor(out=mxT.v(m * NT, (m + 1) * NT), in0=sp_.v(ml * NT, (ml + 1) * NT),
                                                          in1=tt.v(ml * NT, (ml + 1) * NT), op=ALU.add),
                         reads=[sp_.r(ml * NT, (ml + 1) * NT), tt.r(ml * NT, (ml + 1) * NT)],
                         writes=[mxT.r(m * NT, (m + 1) * NT)])
                formA(ws, "pb", M, pgT, NT, range(2), ev_pb)
            formB_tokmajor(ws, "o", [[g] for g in range(8)], mxT, NT, NB, Bf, 40)
            norm_residual(8, NB, 0, Bf, 40, Ablk, Ablk)

        def ffn(NB, up_only=False):
            NT = NB * 128
            ws = WStream(ffn_items(up_only))
            load_gain(2, 0)
            Ablk = [(A.v(j * D, (j + 1) * D), A.r(j * D, (j + 1) * D)) for j in range(NB)]
            front_end(Ablk, 0, hT, NT)
            if not up_only:
                load_gain(3, 0)
            cw0 = CF_CONVW
            pending = []
            for Fg in range(22):
                for part in range(2):
                    name = "upg" if part == 0 else "upv"
                    def ev(b, ml, Fg=Fg, part=part):
                        f = Fg * 2 + ml
                        ch = part * NFC + f
                        u_ = ubuf[(f + part) % 2]
                        tcb = tcv[f % 3] if part == 0 else tcv[3 + f % 2]
                        hv = uph.v(ch * 2, ch * 2 + 2)
                        hr = uph.r(ch * 2, ch * 2 + 2)
                        P.op("act", lambda e: e.activation(out=u_.v(2, 2 + NT), in_=bank(b, NT), func=AF.Copy),
                             reads=[bankr(b)], writes=[u_.r(2, 2 + NT)])
                        P.op("pool", lambda e: e.tensor_copy(u_.v(0, 2), hv), reads=[hr], writes=[u_.r(0, 2)])
                        if up_only:
                            P.op("pool", lambda e: e.tensor_copy(hv, u_.v(NT, NT + 2)), reads=[u_.r(NT, NT + 2)], writes=[hr])
                            return
                        w0 = cf.v(cw0 + ch * 3 + 0, cw0 + ch * 3 + 1)
                        w1 = cf.v(cw0 + ch * 3 + 1, cw0 + ch * 3 + 2)
                        w2 = cf.v(cw0 + ch * 3 + 2, cw0 + ch * 3 + 3)
                        bb = cf.v(CF_CONVB + ch, CF_CONVB + ch + 1)
                        P.op("act", lambda e: e.activation(out=tcb.v(0, NT), in_=bank(b, NT), func=AF.Identity, scale=w2, bias=bb),
                             reads=[bankr(b), cf.r()], writes=[tcb.r(0, NT)])
                        P.op("dve", lambda e: e.scalar_tensor_tensor(out=tcb.v(0, NT), in0=u_.v(1, 1 + NT), scalar=w1,
                                                                     in1=tcb.v(0, NT), op0=ALU.mult, op1=ALU.add),
                             reads=[u_.r(1, 1 + NT), tcb.r(0, NT), cf.r()], writes=[tcb.r(0, NT)])
                        P.op("dve", lambda e: e.scalar_tensor_tensor(out=tcb.v(0, NT), in0=u_.v(0, NT), scalar=w0,
                                                                     in1=tcb.v(0, NT), op0=ALU.mult, op1=ALU.add),
                             reads=[u_.r(0, NT), tcb.r(0, NT), cf.r()], writes=[tcb.r(0, NT)])
                        P.op("pool", lambda e: e.tensor_copy(hv, u_.v(NT, NT + 2)), reads=[u_.r(NT, NT + 2)], writes=[hr])
                        glb = gl[ml]
                        if part == 0:
                            def tail():
                                P.op("act", lambda e: e.activation(out=glb.v(0, NT), in_=tcb.v(0, NT), func=AF.Gelu_apprx_tanh),
                                     reads=[tcb.r(0, NT)], writes=[glb.r(0, NT)])
                        else:
                            def tail():
                                P.op("dve", lambda e: e.tensor_tensor(out=actT.v(f * NT, (f + 1) * NT), in0=glb.v(0, NT),
                                                                      in1=tcb.v(0, NT), op=ALU.mult),
                                     reads=[glb.r(0, NT), tcb.r(0, NT)], writes=[actT.r(f * NT, (f + 1) * NT)])
                        pending.append(tail)
                        while len(pending) > 1:
                            pending.pop(0)()
                    formA(ws, name, Fg, hT, NT, range(2), ev)
            while pending:
                pending.pop(0)()
            if up_only:
                return
            cis = [[ksi * 4 + cg for ksi in range(6)] for cg in range(4)]
            formB_tokmajor(ws, "d", cis, actT, NT, NB, Bf, 40)
            norm_residual(4, NB, 0, Bf, 40, Ablk, Ablk)

        def ple(NB, tok0):
            NT = NB * 128
            ws = WStream(ple_items())
            load_gain(4, 0)
            Ablk = [(A.v(j * D, (j + 1) * D), A.r(j * D, (j + 1) * D)) for j in range(NB)]
            P.dma("sp", s_pin, [(pin.v().rearrange("p (j c) -> p j c", c=PLE),
                                 pown[tok0:tok0 + NT, :].rearrange("(j p) c -> p j c", p=128))], writes=[pin.r()])
            P.op("dve", lambda e: e.tensor_copy(pnb.v(), pin.v()), reads=[pin.r()], writes=[pnb.r()])
            for j in range(NB):
                xs = 0
                xnv = xn.v(xs * D, (xs + 1) * D)
                xnr = xn.r(xs * D, (xs + 1) * D)
                P.op("act", lambda e, j=j, xnv=xnv: e.activation(out=xnv, in_=Ablk[j][0], func=AF.Copy),
                     reads=[Ablk[j][1]], writes=[xnr])
                for half in range(2):
                    b = next_tbank()
                    pb = bank(b).bitcast(BF16)

                    def tr(e, half=half, pb=pb, xs=xs):
                        ins = None
                        for k in range(8):
                            kc = half * 8 + k
                            ins = e.transpose(pb[:, k * 128:(k + 1) * 128],
                                              xn.v(xs * D + kc * 128, xs * D + (kc + 1) * 128), cmv(CM_IDENT))
                        return ins
                    P.op("pe", tr, reads=[xnr, cmr], writes=[bankr(b)])
                    dst = hT.v().rearrange("p (k t) -> p k t", t=NT)[:, half * 8:half * 8 + 8, j * 128:(j + 1) * 128]
                    srcv = pb.rearrange("p (k t) -> p k t", t=128)
                    if half == 0:
                        P.op("act", lambda e, dst=dst, srcv=srcv: e.activation(out=dst, in_=srcv, func=AF.Copy),
                             reads=[bankr(b)], writes=[hT.r()])
                    else:
                        P.op("dve", lambda e, dst=dst, srcv=srcv: e.tensor_copy(dst, srcv), reads=[bankr(b)], writes=[hT.r()])
                b = next_tbank()
                pb = bank(b).bitcast(BF16)

                def trp(e, j=j, pb=pb):
                    ins = None
                    for k in range(2):
                        ins = e.transpose(pb[:, k * 128:(k + 1) * 128], pnb.v(j * PLE + k * 128, j * PLE + (k + 1) * 128),
                                          cmv(CM_IDENT))
                    return ins
                P.op("pe", trp, reads=[pnb.r(), cmr], writes=[bankr(b)])
                dst = pT.v().rearrange("p (k t) -> p k t", t=NT)[:, :, j * 128:(j + 1) * 128]
                srcv = pb[:, 0:256].rearrange("p (k t) -> p k t", t=128)
                P.op("dve", lambda e, dst=dst, srcv=srcv: e.tensor_copy(dst, srcv), reads=[bankr(b)], writes=[pT.r()])
            P.dma("sp", s_wple, [(wpleb.v(), wdefs["ple"][2][0][:, 0:4096])], reads=[("wscr", 3, 4)], writes=[wpleb.r()])
            slot_ple = wpleb
            wple3 = wpleb.v().rearrange("p (k c) -> p k c", c=2048)
            pend = []
            for cg in range(8):
                slot = ws.take("pgate", cg)
                wg3 = w3(slot, 16, 256)
                for j in range(NB):
                    bg = next_fbank()
                    pieces = [(bank(bg, 256), hT.v(k * NT + j * 128, k * NT + (j + 1) * 128), wg3[:, k, :]) for k in range(16)]
                    P.op("pe", mm_group(bg, pieces), reads=[slot.r(), hT.r()], writes=[bankr(bg)])
                    be = next_fbank()
                    pieces = [(bank(be, 256), pT.v(k * NT + j * 128, k * NT + (j + 1) * 128), wple3[:, k, cg * 256:(cg + 1) * 256])
                              for k in range(2)]
                    P.op("pe", mm_group(be, pieces), reads=[slot_ple.r(), pT.r()], writes=[bankr(be)])
                    sgb = sgp[(cg * NB + j) % 2]
                    yv = Bf.v(j * D + cg * 256, j * D + (cg + 1) * 256)
                    yr = Bf.r(j * D + cg * 256, j * D + (cg + 1) * 256)
                    c = 40 + j * 8 + cg
                    P.op("act", lambda e, sgb=sgb, bg=bg: e.activation(out=sgb.v(), in_=bank(bg, 256), func=AF.Sigmoid),
                         reads=[bankr(bg)], writes=[sgb.r()])
                    P.op("dve", lambda e, sgb=sgb, be=be, yv=yv: e.tensor_tensor(out=yv, in0=bank(be, 256), in1=sgb.v(), op=ALU.mult),
                         reads=[bankr(be), sgb.r()], writes=[yr])

                    def tail(yv=yv, yr=yr, c=c):
                        P.op("act", lambda e: e.activation(out=gsq.v(0, 256), in_=yv, func=AF.Square,
                                                           accum_out=small.v(c, c + 1)),
                             reads=[yr], writes=[gsq.r(), small.r(c, c + 1)])
                    pend.append(tail)
                    while len(pend) > 1:
                        pend.pop(0)()
                ws.done()
            while pend:
                pend.pop(0)()
            norm_residual(8, NB, 0, Bf, 40, Ablk, Ablk)

        def load_x(vb0, NB, a0=0):
            for j in range(NB):
                vb = vb0 + j
                aj = a0 + j
                P.dma("sp", s_A[aj], [(A.v(aj * D, (aj + 1) * D), xall[vb * 128:(vb + 1) * 128, :])],
                      writes=[A.r(aj * D, (aj + 1) * D)])

        def dump(name, buf_ap, reg, dram_ap):
            P.dma("pool", s_dbg, [(dram_ap, buf_ap)], reads=[reg])

        load_x(15, 1)
        mixer(15, 1)
        ffn(1, up_only=True)
        ntiles = dbg_cfg.get("ntiles", 4)
        for t in range(ntiles):
            for sub in range(2):
                vb0 = 16 + 4 * t + 2 * sub
                load_x(vb0, 2, a0=2 * sub)
                mixer(vb0, 2, a0=2 * sub)
            if dbg and t == dbg_cfg.get("dump_t", 0) and "x1" in dbg_out:
                dump("x1", A.v().rearrange("p (j c) -> p j c", c=D), A.r(),
                     dbg_out["x1"].rearrange("(j p) c -> p j c", p=128))
            ffn(4)
            if dbg and t == dbg_cfg.get("dump_t", 0) and "x2" in dbg_out:
                dump("x2", A.v().rearrange("p (j c) -> p j c", c=D), A.r(),
                     dbg_out["x2"].rearrange("(j p) c -> p j c", p=128))
            ple(4, t * 512)
            for j in range(4):
                r0 = t * 512 + j * 128
                P.dma("pool", s_Ast[j], [(out[r0:r0 + 128, :], A.v(j * D, (j + 1) * D))], reads=[A.r(j * D, (j + 1) * D)])
        P.final_wait("sp")

        with nc.Block() as block:
            @block.tensor
            def _(e):
                for f in P.q["pe"]:
                    f(e)

            @block.scalar
            def _(e):
                for f in P.q["act"]:
                    f(e)

            @block.vector
            def _(e):
                for f in P.q["dve"]:
                    f(e)

            @block.gpsimd
            def _(e):
                for f in P.q["pool"]:
                    f(e)

            @block.sync
            def _(e):
                for f in P.q["sp"]:
                    f(e)
        build_program.stats = dict(sb_used=sb_used, ops={k: len(v) for k, v in P.q.items()}, waits=P.nwaits)
    return nc


dbg_cfg = {}


def _const_mats(flag):
    j = np.arange(128)[:, None]
    s = np.arange(128)[None, :]
    mats = np.zeros((CM_N, 128, 128), np.float32)
    mats[CM_IDENT] = np.eye(128, dtype=np.float32)
    ntri = -(j >= s).astype(np.float32)
    mats[CM_NTRI] = ntri
    mats[CM_NONES] = -1.0
    mats[CM_NTRI_P] = ntri * flag
    mats[CM_NONES_P] = -1.0 * flag
    t = np.arange(128)[None, :]
    sidx = np.arange(128)[:, None]
    for g, w in enumerate((2, 4, 8, 16)):
        within = ((sidx <= t) & (sidx > t - w)).astype(np.float32)
        prevm = ((sidx - 128) > (t - w)).astype(np.float32)
        cnt_n = np.full((1, 128), float(w), np.float32)
        mats[CM_RW + g] = within / cnt_n - np.eye(128, dtype=np.float32)
        mats[CM_RP + g] = prevm / cnt_n
        if flag == 0.0:
            cnt0 = np.minimum(t + 1, w).astype(np.float32)
            mats[CM_RW0 + g] = within / cnt0 - np.eye(128, dtype=np.float32)
            mats[CM_RP0 + g] = 0.0
        else:
            mats[CM_RW0 + g] = mats[CM_RW + g]
            mats[CM_RP0 + g] = mats[CM_RP + g]
    tri = (j < s).astype(np.float32)
    z = np.zeros((128, 128), np.float32)
    o = np.ones((128, 128), np.float32)
    for k, m in enumerate((z, tri, z, tri)):
        mats[CM_MASK2A + k] = m
    for k, m in enumerate((tri, o, tri, o)):
        mats[CM_MASK2B + k] = m
    for k in range(4):
        mats[CM_MASK1 + k] = tri
    return np.ascontiguousarray(mats.transpose(1, 0, 2).reshape(128, CM_N * 128))


_NC_CACHE = {}


def kernel(x, p, norm_mix_pre, w_in, w_attn_branch, w_pool_group, pool_scale, w_pool_branch, w_out,
           norm_mix_post, norm_ffn_pre, w_up, conv_w, conv_b, w_down, norm_ffn_post, w_ple, w_ple_gate,
           norm_ple_post, _dbg=None):
    f = lambda a: np.ascontiguousarray(np.asarray(a, dtype=np.float32))
    x = f(x); p = f(p)
    gains = np.stack([f(norm_mix_pre)[0], f(norm_mix_post)[0], f(norm_ffn_pre)[0], f(norm_ffn_post)[0],
                      f(norm_ple_post)[0]], axis=0)
    colf = np.zeros((128, CF_N), np.float32)
    colf[:, CF_PSCALE:CF_PSCALE + 8] = f(pool_scale)[0].reshape(8, 128).T
    cw = f(conv_w)[0]
    colf[:, CF_CONVW:CF_CONVW + 264] = cw.reshape(3, 88, 128).transpose(2, 1, 0).reshape(128, 264)
    colf[:, CF_CONVB:CF_CONVB + 88] = f(conv_b)[0].reshape(88, 128).T
    shared = dict(w_in=f(w_in)[0], w_ab=f(w_attn_branch)[0], w_pg=f(w_pool_group)[0], w_pb=f(w_pool_branch)[0],
                  w_o=f(w_out)[0], w_up=f(w_up)[0], w_d=f(w_down)[0], w_ple=f(w_ple)[0], w_pgate=f(w_ple_gate)[0],
                  gains=gains, colf=colf)
    cm = {0.0: _const_mats(0.0), 1.0: _const_mats(1.0)}
    in_maps = []
    for c in range(8):
        b, half = c // 2, c % 2
        if half == 0:
            xall = np.concatenate([np.zeros((2048, D), np.float32), x[b, :2048]], axis=0)
        else:
            xall = x[b]
        m = dict(shared)
        m["xall"] = np.ascontiguousarray(xall)
        m["pown"] = np.ascontiguousarray(p[0, b, half * 2048:(half + 1) * 2048])
        m["cmat"] = cm[float(half)]
        in_maps.append(m)
    key = repr(_dbg)
    if key not in _NC_CACHE:
        _NC_CACHE[key] = build_program(_dbg)
    nc = _NC_CACHE[key]
    res = run_bass_kernel_spmd(nc, in_maps, core_ids=list(range(8)))
    outp = np.empty((4, 4096, D), np.float32)
    for c in range(8):
        b, half = c // 2, c % 2
        outp[b, half * 2048:(half + 1) * 2048] = res.results[c]["out"]
    if _dbg:
        return outp, res.results
    return outp
```

```python
import numpy as np
import concourse.bass as bass
import concourse.mybir as mybir
from concourse.bass_utils import run_bass_kernel_spmd

F32 = mybir.dt.float32
BF16 = mybir.dt.bfloat16
U8 = mybir.dt.uint8
AF = mybir.ActivationFunctionType
ALU = mybir.AluOpType

D = 2048
NH = 8
DFF = 5632
NFC = DFF // 128
PLE = 256
EPS = 1e-6
NVB = 32
QSCALE = 128.0 ** -0.5
SLOT_ELEMS = 4096
NSLOT = 4
SB_BYTES = 212480

CM_IDENT = 0
CM_NTRI = 1
CM_NONES = 2
CM_NTRI_P = 3
CM_NONES_P = 4
CM_RW = 5
CM_RP = 9
CM_RW0 = 13
CM_RP0 = 17
CM_MASK2A = 21
CM_MASK2B = 25
CM_MASK1 = 29
CM_N = 33

CF_PSCALE = 0
CF_CONVW = 8
CF_CONVB = 8 + 264
CF_N = 8 + 264 + 88


class DSem:
    def __init__(self, h):
        self.h = h
        self.count = 0


class Buf:
    def __init__(self, SB, lo, nelem, dt):
        self.esz = 4 if dt == F32 else 2
        self.lo = lo
        self.n = nelem
        self.dt = dt
        self.ap = SB[:, lo:lo + nelem * self.esz].bitcast(dt)

    def v(self, a=0, b=None):
        b = self.n if b is None else b
        return self.ap[:, a:b]

    def r(self, a=0, b=None):
        b = self.n if b is None else b
        return ("sb", self.lo + a * self.esz, self.lo + b * self.esz)


class Prog:
    COMPUTE = ("pe", "act", "dve", "pool")

    def __init__(self, nc, esems):
        self.nc = nc
        self.q = {e: [] for e in ("pe", "act", "dve", "pool", "sp")}
        self.esem = esems
        self.cnt = {e: 0 for e in self.COMPUTE}
        self.waited = {e: {} for e in self.q}
        self.regs = {}
        self.dsems = []
        self.nwaits = 0

    def _entries(self, reg):
        sp, lo, hi = reg
        lst = self.regs.setdefault(sp, [])
        return [e for e in lst if e[0] < hi and lo < e[1]], lst

    def _collect(self, reads, writes):
        raw, other = {}, {}

        def add(d, tok):
            if tok is None:
                return
            k = tok[0]
            if k not in d or d[k][2] < tok[2]:
                d[k] = tok

        for reg in reads:
            ov, _ = self._entries(reg)
            for e in ov:
                add(raw, e[2])
        for reg in writes:
            ov, _ = self._entries(reg)
            for e in ov:
                add(other, e[2])
                for t in e[3].values():
                    add(other, t)
        return raw, other

    def _commit(self, tok, reads, writes):
        for reg in reads:
            sp, lo, hi = reg
            ov, lst = self._entries(reg)
            pos = lo
            for e in sorted(ov, key=lambda e: e[0]):
                if e[0] > pos:
                    lst.append([pos, e[0], None, {tok[0]: tok}])
                old = e[3].get(tok[0])
                if old is None or old[2] < tok[2]:
                    e[3][tok[0]] = tok
                pos = max(pos, e[1])
            if pos < hi:
                lst.append([pos, hi, None, {tok[0]: tok}])
        for reg in writes:
            sp, lo, hi = reg
            ov, lst = self._entries(reg)
            for e in ov:
                lst.remove(e)
                if e[0] < lo:
                    lst.append([e[0], lo, e[2], dict(e[3])])
                if e[1] > hi:
                    lst.append([hi, e[1], e[2], dict(e[3])])
            lst.append([lo, hi, tok, {}])

    def _waits(self, eng, raw, other):
        need = dict(raw)
        for k, t in other.items():
            if k == eng and eng == "pe":
                continue
            if k not in need or need[k][2] < t[2]:
                need[k] = t
        out = []
        wd = self.waited[eng]
        for k, t in need.items():
            if wd.get(k, 0) >= t[2]:
                continue
            wd[k] = t[2]
            out.append((t[1], t[2]))
        self.nwaits += len(out)
        return out

    def op(self, eng, fn, reads=(), writes=()):
        raw, other = self._collect(reads, writes)
        waits = self._waits(eng, raw, other)
        n = self.cnt[eng] + 1
        self.cnt[eng] = n
        sem = self.esem[eng]
        tok = (eng, sem, n)
        def emit(e, waits=waits, fn=fn, sem=sem):
            for (s, v) in waits:
                e.wait_ge(s, v)
            fn(e).then_inc(sem, 1)
        self.q[eng].append(emit)
        self._commit(tok, reads, writes)
        return tok

    def dma(self, q, dsem, pairs, reads=(), writes=()):
        raw, other = self._collect(reads, writes)
        waits = self._waits(q, raw, other)
        dsem.count += 16 * len(pairs)
        tok = (id(dsem), dsem.h, dsem.count)
        def emit(e, waits=waits, pairs=pairs, h=dsem.h):
            for (s, v) in waits:
                e.wait_ge(s, v)
            for (o, i) in pairs:
                e.dma_start(out=o, in_=i).then_inc(h, 16)
        self.q[q].append(emit)
        self._commit(tok, reads, writes)
        return tok

    def final_wait(self, q):
        sems = [(d.h, d.count) for d in self.dsems if d.count > 0]
        sems += [(self.esem[e], self.cnt[e]) for e in self.COMPUTE if self.cnt[e] > 0]
        def emit(e):
            for (s, v) in sems:
                e.wait_ge(s, v)
        self.q[q].append(emit)


def build_program(dbg=None):
    nc = bass.Bass("TRN2", target_bir_lowering=False)
    dt_in = lambda name, shape: nc.dram_tensor(name, shape, F32, kind="ExternalInput").ap()
    xall = dt_in("xall", [4096, D])
    pown = dt_in("pown", [2048, PLE])
    w_in = dt_in("w_in", [D, 8192])
    w_ab = dt_in("w_ab", [1024, D])
    w_pg = dt_in("w_pg", [4, 256, 256])
    w_pb = dt_in("w_pb", [1024, D])
    w_o = dt_in("w_o", [D, D])
    w_up = dt_in("w_up", [D, 2 * DFF])
    w_d = dt_in("w_d", [DFF, D])
    w_ple = dt_in("w_ple", [PLE, D])
    w_pgate = dt_in("w_pgate", [D, D])
    gains = dt_in("gains", [5, D])
    colf = dt_in("colf", [128, CF_N])
    cmat = dt_in("cmat", [128, CM_N * 128])
    out = nc.dram_tensor("out", [2048, D], F32, kind="ExternalOutput").ap()
    KTs = nc.dram_tensor("KTs", [NH, 128, 4096], BF16, kind="Internal").ap()
    VS = nc.dram_tensor("VS", [NH, 128, NVB, 128], BF16, kind="Internal").ap()
    dbg_out = {}
    if dbg:
        for name, shape in dbg.items():
            dbg_out[name] = nc.dram_tensor("dbg_" + name, shape, F32, kind="ExternalOutput").ap()

    wdefs = {}

    def defw(name, src, K, c_lo, c_hi, cw, ksplit=16):
        chunks = []
        nkc = K // 128
        for ks in range(0, nkc, ksplit):
            nk = min(ksplit, nkc - ks)
            for c0 in range(c_lo, c_hi, cw):
                chunks.append((ks, nk, c0, cw))
        scr = nc.dram_tensor("ws_" + name, [len(chunks), 128, SLOT_ELEMS], BF16, kind="Internal").ap()
        wdefs[name] = (src, chunks, scr)

    defw("q", w_in, D, 0, 1024, 256)
    defw("u", w_in, D, 3072, 4096, 256)
    defw("ga", w_in, D, 4096, 6144, 256)
    defw("gp", w_in, D, 6144, 8192, 256)
    defw("ab", w_ab, 1024, 0, D, 256)
    defw("pb", w_pb, 1024, 0, D, 256)
    defw("o", w_o, D, 0, D, 256)
    defw("upg", w_up, D, 0, DFF, 256)
    defw("upv", w_up, D, DFF, 2 * DFF, 256)
    defw("d", w_d, DFF, 0, D, 512, ksplit=8)
    defw("ple", w_ple, PLE, 0, D, 2048)
    defw("pgate", w_pgate, D, 0, D, 256)

    import contextlib
    with contextlib.ExitStack() as es:
        SBT = es.enter_context(nc.sbuf_tensor("SB", [128, SB_BYTES], U8))
        PS = es.enter_context(nc.psum_tensor("PS", [128, 8 * 512], F32))
        esems = {e: es.enter_context(nc.semaphore("sem_" + e)) for e in Prog.COMPUTE}
        P = Prog(nc, esems)

        def new_dsem(name):
            d = DSem(es.enter_context(nc.semaphore(name)))
            P.dsems.append(d)
            return d

        off = [0]

        def alloc(nelem, dt):
            esz = 4 if dt == F32 else 2
            lo = (off[0] + 63) // 64 * 64
            off[0] = lo + nelem * esz
            assert off[0] <= SB_BYTES, off[0]
            return Buf(SBT, lo, nelem, dt)

        cm = alloc(CM_N * 128, BF16)
        cf = alloc(CF_N, F32)
        wpg = alloc(4 * 2 * 256, BF16)
        A = alloc(4 * D, F32)
        xn = alloc(D, BF16)
        gbc = [alloc(D, F32)]
        hT = alloc(16 * 512, BF16)
        small = alloc(128, F32)
        uph = alloc(88 * 2, F32)
        uprev = alloc(1024, BF16)
        gsq = alloc(512, BF16)
        kv_lo = off[0]
        Bf = alloc(4 * D, F32)
        ring = [alloc(SLOT_ELEMS, BF16) for _ in range(NSLOT)]
        union_lo = off[0]
        qT = alloc(8 * 256, BF16)
        oT = alloc(8 * 256, BF16)
        ub = alloc(2 * 1024, BF16)
        plT = alloc(8 * 256, BF16)
        pgT = alloc(8 * 256, BF16)
        mxT = alloc(16 * 256, BF16)
        Eb = [alloc(512, F32) for _ in range(2)]
        Lb = [alloc(512, BF16) for _ in range(3)]
        ab_ = [alloc(512, BF16) for _ in range(3)]
        Sb = [alloc(512, BF16) for _ in range(2)]
        kvr = [alloc(2048, BF16) for _ in range(3)]
        sg = [alloc(2 * 256, F32) for _ in range(2)]
        tt = alloc(2 * 256, F32)
        mixer_hi = off[0]
        off[0] = kv_lo
        wk = alloc(16 * 1024, BF16)
        wv = alloc(16 * 1024, BF16)
        hT4 = hT
        kst = [alloc(512, BF16) for _ in range(3)]
        vst = [alloc(1024, BF16) for _ in range(2)]
        kv_hi = off[0]
        off[0] = union_lo
        actT = alloc(NFC * 512, BF16)
        ubuf = [alloc(514, F32) for _ in range(2)]
        tcv = [alloc(512, F32) for _ in range(5)]
        gl = [alloc(512, F32) for _ in range(2)]
        ffn_hi = off[0]
        off[0] = union_lo
        pin = alloc(4 * PLE, F32)
        pnb = alloc(4 * PLE, BF16)
        pT = alloc(2 * 512, BF16)
        sgp = [alloc(256, F32) for _ in range(2)]
        wpleb = alloc(2 * 2048, BF16)
        ple_hi = off[0]
        off[0] = max(mixer_hi, kv_hi, ffn_hi, ple_hi)
        sb_used = off[0]

        s_wgrp = [new_dsem("s_wg%d" % i) for i in range(4)]
        s_kvw = new_dsem("s_kvw")
        s_A = [new_dsem("s_A%d" % i) for i in range(4)]
        s_Ast = [new_dsem("s_Ast%d" % i) for i in range(4)]
        s_g = [new_dsem("s_g%d" % i) for i in range(1)]
        s_ring = [new_dsem("s_ring%d" % i) for i in range(NSLOT)]
        s_kvr = [new_dsem("s_kvr%d" % i) for i in range(3)]
        s_kst = [new_dsem("s_kst%d" % i) for i in range(3)]
        s_vst = [new_dsem("s_vst%d" % i) for i in range(2)]
        s_pin = new_dsem("s_pin")
        s_wple = new_dsem("s_wple")
        s_c = [new_dsem("s_c%d" % i) for i in range(5)]
        s_dbg = new_dsem("s_dbg")

        def bank(b, n=512, c0=0):
            return PS[:, b * 512 + c0:b * 512 + c0 + n]

        def bankr(b):
            return ("ps", b, b + 1)

        fb = [0]

        def next_fbank():
            b = fb[0] % 6
            fb[0] += 1
            return b

        tb = [0]

        def next_tbank():
            b = 6 + tb[0] % 2
            tb[0] += 1
            return b

        def cmv(idx, n=128):
            return cm.v(idx * 128, idx * 128 + n)

        cmr = cm.r()

        P.dma("pool", s_c[0], [(cm.v(), cmat)], writes=[cm.r()])
        P.dma("sp", s_c[1], [(cf.v(), colf)], writes=[cf.r()])
        P.dma("pool", s_c[2],
              [(wpg.v().rearrange("p (g k c) -> p g k c", g=4, k=2),
                w_pg.rearrange("g (k p) c -> p g k c", p=128))], writes=[wpg.r()])
        for (wb, c0, sc) in ((wk, 1024, s_c[3]), (wv, 2048, s_c[4])):
            pairs = []
            for k4 in range(0, 16, 4):
                pairs.append((wb.v().rearrange("p (k c) -> p k c", c=1024)[:, k4:k4 + 4, :],
                              w_in[k4 * 128:(k4 + 4) * 128, c0:c0 + 1024].rearrange("(k p) c -> p k c", p=128)))
            P.dma("pool", sc, pairs, writes=[wb.r()])
        P.op("dve", lambda e: e.memset(uph.v(), 0.0), writes=[uph.r()])
        P.op("dve", lambda e: e.memset(uprev.v(), 0.0), writes=[uprev.r()])

        conv_order = [("q", 0), ("u", 0), ("ga", 1), ("ab", 1), ("gp", 1), ("pb", 1), ("o", 1),
                      ("upg", 2), ("upv", 2), ("d", 3), ("ple", 3), ("pgate", 3)]
        wtok = {}
        grp_pairs = {g: [] for g in range(4)}
        for name, g in conv_order:
            src, chunks, scr = wdefs[name]
            for ci, (ks, nk, c0, cw) in enumerate(chunks):
                dst = scr[ci][:, 0:nk * cw].rearrange("p (k c) -> p k c", c=cw)
                for k4 in range(0, nk, 4):
                    kk = min(4, nk - k4)
                    r0 = (ks + k4) * 128
                    grp_pairs[g].append((dst[:, k4:k4 + kk, :],
                                         src[r0:r0 + kk * 128, c0:c0 + cw].rearrange("(k p) c -> p k c", p=128)))
        for g in range(2):
            tok = P.dma("pool", s_wgrp[g], grp_pairs[g], reads=[wk.r(), wv.r(), cm.r()], writes=[("wscr", g, g + 1)])

        def late_conversions():
            for g in range(2, 4):
                P.dma("pool", s_wgrp[g], grp_pairs[g], reads=[("vs", NVB - 1, NVB)], writes=[("wscr", g, g + 1)])
        wgrp_of = {name: g for name, g in conv_order}

        ring_state = {"next": 0, "queue": []}

        def ring_issue(item):
            name, ci = item
            src, chunks, scr = wdefs[name]
            ks, nk, c0, cw = chunks[ci]
            slot = ring_state["next"] % NSLOT
            ring_state["next"] += 1
            n = nk * cw
            g = wgrp_of[name]
            P.dma("sp", s_ring[slot], [(ring[slot].v(0, n), scr[ci][:, 0:n])],
                  reads=[("wscr", g, g + 1)], writes=[ring[slot].r()])
            return slot

        class WStream:
            def __init__(self, items):
                self.items = items
                self.issued = 0
                self.slots = {}
                self.cur = 0
                for _ in range(min(NSLOT, len(items))):
                    self._issue()

            def _issue(self):
                if self.issued < len(self.items):
                    self.slots[self.issued] = ring_issue(self.items[self.issued])
                    self.issued += 1

            def take(self, name, ci):
                assert self.items[self.cur] == (name, ci), (self.items[self.cur], name, ci)
                slot = self.slots.pop(self.cur)
                self.cur += 1
                return ring[slot]

            def done(self):
                self._issue()

        def load_gain(gi, slot):
            P.dma("sp", s_g[slot], [(gbc[slot].v(), gains[gi].partition_broadcast(128))], writes=[gbc[slot].r()])

        def rstd_from(col_in, col_out, scratch_col):
            ci, co, cs_ = small.v(col_in, col_in + 1), small.v(col_out, col_out + 1), small.v(scratch_col, scratch_col + 1)
            P.op("dve", lambda e: e.tensor_scalar(out=cs_, in0=ci, scalar1=1.0 / D, scalar2=EPS, op0=ALU.mult, op1=ALU.add),
                 reads=[small.r(col_in, col_in + 1)], writes=[small.r(scratch_col, scratch_col + 1)])
            P.op("act", lambda e: e.activation(out=cs_, in_=cs_, func=AF.Ln),
                 reads=[small.r(scratch_col, scratch_col + 1)], writes=[small.r(scratch_col, scratch_col + 1)])
            P.op("act", lambda e: e.activation(out=co, in_=cs_, func=AF.Exp, scale=-0.5),
                 reads=[small.r(scratch_col, scratch_col + 1)], writes=[small.r(col_out, col_out + 1)])

        def transpose_block(xs, hTb, NT, jj):
            xnr = xn.r(xs * D, (xs + 1) * D)
            for half in range(2):
                b = next_tbank()
                pb = bank(b).bitcast(BF16)

                def tr(e, half=half, pb=pb, xs=xs):
                    ins = None
                    for k in range(8):
                        kc = half * 8 + k
                        ins = e.transpose(pb[:, k * 128:(k + 1) * 128],
                                          xn.v(xs * D + kc * 128, xs * D + (kc + 1) * 128), cmv(CM_IDENT))
                    return ins
                P.op("pe", tr, reads=[xnr, cmr], writes=[bankr(b)])
                dst = hTb.v().rearrange("p (k t) -> p k t", t=NT)[:, half * 8:half * 8 + 8, jj * 128:(jj + 1) * 128]
                srcv = pb.rearrange("p (k t) -> p k t", t=128)
                if half == 0:
                    P.op("act", lambda e, dst=dst, srcv=srcv: e.activation(out=dst, in_=srcv, func=AF.Copy),
                         reads=[bankr(b)], writes=[hTb.r()])
                else:
                    P.op("dve", lambda e, dst=dst, srcv=srcv: e.tensor_copy(dst, srcv),
                         reads=[bankr(b)], writes=[hTb.r()])

        def front_end(src_blocks, gslot, hTb, NT, j0=0):
            for j, (xap, xr) in enumerate(src_blocks):
                xs = 0
                sq = small.v(j, j + 1)
                rs = small.v(8 + j, 9 + j)
                xnv = xn.v(xs * D, (xs + 1) * D)
                xnr = xn.r(xs * D, (xs + 1) * D)
                P.op("act", lambda e, xap=xap, xnv=xnv, sq=sq: e.activation(out=xnv, in_=xap, func=AF.Square, accum_out=sq),
                     reads=[xr], writes=[xnr, small.r(j, j + 1)])
                rstd_from(j, 8 + j, 16 + j)
                P.op("dve", lambda e, xap=xap, rs=rs, xnv=xnv: e.scalar_tensor_tensor(
                    out=xnv, in0=xap, scalar=rs, in1=gbc[gslot].v(), op0=ALU.mult, op1=ALU.mult),
                    reads=[xr, small.r(8 + j, 9 + j), gbc[gslot].r()], writes=[xnr])
                transpose_block(xs, hTb, NT, j0 + j)

        def mm_group(b, pieces, first_start=True):
            def f(e):
                ins = None
                for i, (o, l, r) in enumerate(pieces):
                    ins = e.matmul(o, l, r, start=(first_start and i == 0), stop=(i == len(pieces) - 1),
                                   skip_group_check=True)
                return ins
            return f

        def w3(slotbuf, nk, cw):
            return slotbuf.v(0, nk * cw).rearrange("p (k c) -> p k c", c=cw)

        def formA(ws, name, ci, actTb, NT, mlist, evac):
            src, chunks, scr = wdefs[name]
            ks, nk, c0, cw = chunks[ci]
            slot = ws.take(name, ci)
            wv3 = w3(slot, nk, cw)
            for ml in mlist:
                b = next_fbank()
                pieces = [(bank(b, NT), wv3[:, k, ml * 128:(ml + 1) * 128],
                           actTb.v((ks + k) * NT, (ks + k + 1) * NT)) for k in range(nk)]
                P.op("pe", mm_group(b, pieces), reads=[slot.r(), actTb.r()], writes=[bankr(b)])
                evac(b, ml)
            ws.done()

        load_gain(0, 0)
        for t in range(8):
            for pr in range(2):
                for j in range(2):
                    vb = t * 4 + pr * 2 + j
                    aj = pr * 2 + j
                    P.dma("sp", s_A[aj], [(A.v(aj * D, (aj + 1) * D), xall[vb * 128:(vb + 1) * 128, :])],
                          writes=[A.r(aj * D, (aj + 1) * D)])
                blocks = [(A.v((pr * 2 + j) * D, (pr * 2 + j + 1) * D), A.r((pr * 2 + j) * D, (pr * 2 + j + 1) * D))
                          for j in range(2)]
                front_end(blocks, 0, hT4, 512, j0=pr * 2)
            wk3 = wk.v().rearrange("p (k c) -> p k c", c=1024)
            wv3 = wv.v().rearrange("p (k c) -> p k c", c=1024)
            for h in range(NH):
                b = next_fbank()
                pieces = [(bank(b), wk3[:, k, h * 128:(h + 1) * 128], hT4.v(k * 512, (k + 1) * 512)) for k in range(16)]
                P.op("pe", mm_group(b, pieces), reads=[wk.r(), hT4.r()], writes=[bankr(b)])
                st = (t * NH + h) % 3
                if h % 2 == 0:
                    P.op("act", lambda e, st=st, b=b: e.activation(out=kst[st].v(), in_=bank(b), func=AF.Copy),
                         reads=[bankr(b)], writes=[kst[st].r()])
                else:
                    P.op("dve", lambda e, st=st, b=b: e.tensor_copy(kst[st].v(), bank(b)),
                         reads=[bankr(b)], writes=[kst[st].r()])
                P.dma("sp", s_kst[st], [(KTs[h][:, t * 512:(t + 1) * 512], kst[st].v())],
                      reads=[kst[st].r()], writes=[("kt", h * 8 + t, h * 8 + t + 1)])
            for jj in range(4):
                vb = t * 4 + jj
                st = vb % 2
                for cg in range(2):
                    b = next_fbank()
                    pieces = [(bank(b), hT4.v(k * 512 + jj * 128, k * 512 + (jj + 1) * 128),
                               wv3[:, k, cg * 512:(cg + 1) * 512]) for k in range(16)]
                    P.op("pe", mm_group(b, pieces), reads=[wv.r(), hT4.r()], writes=[bankr(b)])
                    if cg == 0:
                        P.op("act", lambda e, st=st, b=b: e.activation(out=vst[st].v(0, 512), in_=bank(b), func=AF.Copy),
                             reads=[bankr(b)], writes=[vst[st].r(0, 512)])
                    else:
                        P.op("dve", lambda e, st=st, b=b: e.tensor_copy(vst[st].v(512, 1024), bank(b)),
                             reads=[bankr(b)], writes=[vst[st].r(512, 1024)])
                P.dma("sp", s_vst[st],
                      [(VS[:, :, vb, :].rearrange("h p d -> p h d"), vst[st].v().rearrange("p (h d) -> p h d", d=128))],
                      reads=[vst[st].r()], writes=[("vs", vb, vb + 1)])

        late_conversions()

        def mixer_items():
            it = [("q", c) for c in range(4)] + [("u", c) for c in range(4)]
            for M in range(8):
                it += [("ga", M), ("ab", M), ("gp", M), ("pb", M)]
            it += [("o", g) for g in range(8)]
            return it

        def ffn_items(up_only=False):
            it = []
            for Fg in range(22):
                it += [("upg", Fg), ("upv", Fg)]
            if not up_only:
                for cg in range(4):
                    for ksi in range(6):
                        it.append(("d", ksi * 4 + cg))
            return it

        def ple_items():
            return [("pgate", g) for g in range(8)]

        def norm_residual(ncg, NB, gslot, ybuf, sscol0, resid, dest):
            for j in range(NB):
                c = sscol0 + j * 8
                ssum = small.v(32 + j, 33 + j)
                P.op("dve", lambda e, c=c, ssum=ssum: e.tensor_reduce(out=ssum, in_=small.v(c, c + ncg),
                                                                      axis=mybir.AxisListType.X, op=ALU.add),
                     reads=[small.r(c, c + ncg)], writes=[small.r(32 + j, 33 + j)])
                rstd_from(32 + j, 36 + j, 20 + j)
                ssum = small.v(36 + j, 37 + j)
                yv = ybuf.v(j * D, (j + 1) * D)
                yr = ybuf.r(j * D, (j + 1) * D)
                P.op("dve", lambda e, yv=yv, ssum=ssum: e.scalar_tensor_tensor(
                    out=yv, in0=yv, scalar=ssum, in1=gbc[gslot].v(), op0=ALU.mult, op1=ALU.mult),
                    reads=[yr, small.r(36 + j, 37 + j), gbc[gslot].r()], writes=[yr])
                rv, rr = resid[j]
                dv, dr = dest[j]
                P.op("dve", lambda e, yv=yv, rv=rv, dv=dv: e.tensor_tensor(out=dv, in0=yv, in1=rv, op=ALU.add),
                     reads=[yr, rr], writes=[dr])

        def formB_tokmajor(ws, name, cis, actTb, NT, NB, ybuf, sscol0, nk_total_chunks=None):
            for cg, chunk_list in enumerate(cis):
                banks = [next_fbank() for _ in range(NB)]
                for idx, ci in enumerate(chunk_list):
                    src, chunks, scr = wdefs[name]
                    ks, nk, c0, cw = chunks[ci]
                    slot = ws.take(name, ci)
                    wv3_ = w3(slot, nk, cw)
                    for j in range(NB):
                        b = banks[j]
                        pieces = [(bank(b, cw), actTb.v((ks + k) * NT + j * 128, (ks + k) * NT + (j + 1) * 128), wv3_[:, k, :])
                                  for k in range(nk)]
                        P.op("pe", mm_group(b, pieces, first_start=(idx == 0)), reads=[slot.r(), actTb.r()],
                             writes=[bankr(b)])
                    ws.done()
                for j in range(NB):
                    b = banks[j]
                    yv = ybuf.v(j * D + cg * cw, j * D + (cg + 1) * cw)
                    yr = ybuf.r(j * D + cg * cw, j * D + (cg + 1) * cw)
                    c = sscol0 + j * 8 + cg
                    P.op("dve", lambda e, yv=yv, b=b, cw=cw: e.tensor_copy(yv, bank(b, cw)), reads=[bankr(b)], writes=[yr])
                    P.op("act", lambda e, yv=yv, c=c, cw=cw: e.activation(out=gsq.v(0, cw), in_=yv, func=AF.Square,
                                                                   accum_out=small.v(c, c + 1)),
                         reads=[yr], writes=[gsq.r(), small.r(c, c + 1)])


        def attention(vq0, NB, NT):
            HPU = 2
            W = HPU * NT
            nun = NH // HPU
            vmax = vq0 + NB - 1
            units = [(hg, kb) for hg in range(nun) for kb in range(vmax, -1, -1)]
            n = len(units)
            loaded = {}
            slot_key = {}
            kvi = [0]

            lastuse = {}

            def kv_get(hg, G, i, prefetch=False):
                key = (hg, G)
                if key in loaded:
                    buf, slot = loaded[key]
                    if not prefetch:
                        lastuse[slot] = i
                    return buf
                slot = kvi[0] % 3
                if lastuse.get(slot, -100) + 5 > i:
                    assert prefetch, (hg, G, i, lastuse)
                    return None
                kvi[0] += 1
                if slot in slot_key:
                    loaded.pop(slot_key[slot], None)
                slot_key[slot] = key
                buf = kvr[slot]
                pairs, rd = [], []
                for hh in range(HPU):
                    h = hg * HPU + hh
                    pairs.append((buf.v(hh * 512, (hh + 1) * 512), KTs[h][:, G * 512:(G + 1) * 512]))
                    pairs.append((buf.v(1024 + hh * 512, 1024 + (hh + 1) * 512).rearrange("p (b d) -> p b d", d=128),
                                  VS[h][:, G * 4:(G + 1) * 4, :]))
                    rd.append(("kt", h * 8 + G, h * 8 + G + 1))
                rd.append(("vs", G * 4, G * 4 + 4))
                P.dma("sp", s_kvr[slot], pairs, reads=rd, writes=[buf.r()])
                loaded[key] = (buf, slot)
                if not prefetch:
                    lastuse[slot] = i
                return buf

            zb_of, kvb = {}, {}

            def maskidx(kb):
                if kb < vq0:
                    return None
                if NB == 1:
                    return CM_MASK1
                return CM_MASK2B if kb == vq0 else CM_MASK2A

            def stage0(i):
                hg, kb = units[i]
                G, bi = kb // 4, kb % 4
                buf = kv_get(hg, G, i)
                kvb[i] = buf
                if i + 4 < n:
                    hg2, kb2 = units[i + 4]
                    kv_get(hg2, kb2 // 4, i, prefetch=True)
                zb = i % 4
                zb_of[i] = zb
                pieces = []
                for hh in range(HPU):
                    h = hg * HPU + hh
                    pieces.append((bank(zb, NT, hh * NT), buf.v(hh * 512 + bi * 128, hh * 512 + (bi + 1) * 128),
                                   qT.v(h * NT, (h + 1) * NT)))
                P.op("pe", mm_group(zb, pieces), reads=[buf.r(), qT.r()], writes=[bankr(zb)])

            def stage1(i):
                zb = zb_of[i]
                eb = Eb[i % 2]
                P.op("act", lambda e: e.activation(out=eb.v(0, W), in_=bank(zb, W), func=AF.Exp),
                     reads=[bankr(zb)], writes=[eb.r()])

            def stage2(i):
                hg, kb = units[i]
                eb = Eb[i % 2]
                lb = Lb[i % 3]
                P.op("act", lambda e: e.activation(out=lb.v(0, W), in_=eb.v(0, W), func=AF.Ln, bias=1.0),
                     reads=[eb.r()], writes=[lb.r()])
                mi = maskidx(kb)
                if mi is not None:
                    P.op("dve", lambda e: e.tensor_tensor(out=lb.v(0, W), in0=lb.v(0, W), in1=cm.v(mi * 128, mi * 128 + W),
                                                          op=ALU.mult),
                         reads=[lb.r(), cmr], writes=[lb.r()])

            def stage3(i):
                hg, kb = units[i]
                zb = zb_of[i]
                lb = Lb[i % 3]
                sbuf = Sb[hg % 2]
                first = (kb == vmax)
                prev = kb < 16
                tri = cmv(CM_NTRI_P if prev else CM_NTRI)
                ones = cmv(CM_NONES_P if prev else CM_NONES)
                pieces = [(bank(zb, W), tri, lb.v(0, W))]
                rds = [lb.r(), cmr]
                if not first:
                    pieces.append((bank(zb, W), ones, sbuf.v(0, W)))
                    rds.append(sbuf.r())
                P.op("pe", mm_group(zb, pieces, first_start=False), reads=rds, writes=[bankr(zb)])
                if first:
                    P.op("dve", lambda e: e.tensor_copy(sbuf.v(0, W), lb.v(0, W)), reads=[lb.r()], writes=[sbuf.r()])
                elif kb > 0:
                    P.op("dve", lambda e: e.tensor_tensor(out=sbuf.v(0, W), in0=sbuf.v(0, W), in1=lb.v(0, W), op=ALU.add),
                         reads=[lb.r(), sbuf.r()], writes=[sbuf.r()])

            def stage4(i):
                hg, kb = units[i]
                zb = zb_of[i]
                av = ab_[i % 3]
                P.op("act", lambda e: e.activation(out=av.v(0, W), in_=bank(zb, W), func=AF.Exp),
                     reads=[bankr(zb)], writes=[av.r()])
                mi = maskidx(kb)
                if mi is not None:
                    P.op("dve", lambda e: e.tensor_tensor(out=av.v(0, W), in0=av.v(0, W), in1=cm.v(mi * 128, mi * 128 + W),
                                                          op=ALU.mult),
                         reads=[av.r(), cmr], writes=[av.r()])

            def stage5(i):
                hg, kb = units[i]
                bi = kb % 4
                buf = kvb[i]
                av = ab_[i % 3]
                ob = 4 + hg % 2
                pieces = []
                for hh in range(HPU):
                    vo = 1024 + hh * 512 + bi * 128
                    pieces.append((bank(ob, NT, hh * NT), buf.v(vo, vo + 128), av.v(hh * NT, (hh + 1) * NT)))
                P.op("pe", mm_group(ob, pieces, first_start=(kb == vmax)), reads=[buf.r(), av.r()], writes=[bankr(ob)])
                if kb == 0:
                    h0 = hg * HPU
                    P.op("dve", lambda e: e.tensor_copy(oT.v(h0 * NT, (h0 + HPU) * NT), bank(ob, W)),
                         reads=[bankr(ob)], writes=[oT.r(h0 * NT, (h0 + HPU) * NT)])

            stages = [stage0, stage1, stage2, stage3, stage4, stage5]
            for s_ in range(n + 5):
                for k in (5, 4, 3, 2, 1, 0):
                    i = s_ - k
                    if 0 <= i < n:
                        stages[k](i)

        def mixer(vq0, NB, a0=0):
            NT = NB * 128
            ws = WStream(mixer_items())
            load_gain(0, 0)
            Ablk = [(A.v((a0 + j) * D, (a0 + j + 1) * D), A.r((a0 + j) * D, (a0 + j + 1) * D)) for j in range(NB)]
            front_end(Ablk, 0, hT, NT)
            load_gain(1, 0)
            for c in range(4):
                def ev(b, ml, c=c):
                    h = c * 2 + ml
                    P.op("act", lambda e: e.activation(out=qT.v(h * NT, (h + 1) * NT), in_=bank(b, NT), func=AF.Copy, scale=QSCALE),
                         reads=[bankr(b)], writes=[qT.r(h * NT, (h + 1) * NT)])
                formA(ws, "q", c, hT, NT, range(2), ev)
            for c in range(4):
                slot = ws.take("u", c)
                wv3_ = w3(slot, 16, 256)
                for j in range(NB):
                    b = next_fbank()
                    pieces = [(bank(b, 256), hT.v(k * NT + j * 128, k * NT + (j + 1) * 128), wv3_[:, k, :]) for k in range(16)]
                    P.op("pe", mm_group(b, pieces), reads=[slot.r(), hT.r()], writes=[bankr(b)])
                    uo = j * 1024 + c * 256
                    P.op("dve", lambda e, uo=uo, b=b: e.tensor_copy(ub.v(uo, uo + 256), bank(b, 256)),
                         reads=[bankr(b)], writes=[ub.r(uo, uo + 256)])
                ws.done()
            for cc in range(8):
                g = cc // 2
                b = next_fbank()
                pieces = []
                for j in range(NB):
                    first_blk = (vq0 + j == 16)
                    rw = cmv((CM_RW0 if first_blk else CM_RW) + g)
                    rp = cmv((CM_RP0 if first_blk else CM_RP) + g)
                    pieces.append((bank(b, 128, j * 128), ub.v(j * 1024 + cc * 128, j * 1024 + (cc + 1) * 128), rw))
                    pv = uprev.v(cc * 128, (cc + 1) * 128) if j == 0 else ub.v((j - 1) * 1024 + cc * 128, (j - 1) * 1024 + (cc + 1) * 128)
                    pieces.append((bank(b, 128, j * 128), pv, rp))
                P.op("pe", mm_group(b, pieces), reads=[ub.r(), uprev.r(), cmr], writes=[bankr(b)])
                P.op("act" if cc % 2 == 0 else "dve",
                     (lambda e, cc=cc, b=b: e.activation(out=plT.v(cc * NT, (cc + 1) * NT), in_=bank(b, NT), func=AF.Copy))
                     if cc % 2 == 0 else
                     (lambda e, cc=cc, b=b: e.tensor_copy(plT.v(cc * NT, (cc + 1) * NT), bank(b, NT))),
                     reads=[bankr(b)], writes=[plT.r(cc * NT, (cc + 1) * NT)])
            P.op("dve", lambda e: e.tensor_copy(uprev.v(), ub.v((NB - 1) * 1024, NB * 1024)),
                 reads=[ub.r((NB - 1) * 1024, NB * 1024)], writes=[uprev.r()])
            wpg4 = wpg.v().rearrange("p (g k c) -> p g k c", g=4, k=2)
            for dc in range(8):
                g, dd = dc // 2, dc % 2
                b = next_fbank()
                pieces = [(bank(b, NT), wpg4[:, g, kk, dd * 128:(dd + 1) * 128], plT.v((2 * g + kk) * NT, (2 * g + kk + 1) * NT))
                          for kk in range(2)]
                P.op("pe", mm_group(b, pieces), reads=[wpg.r(), plT.r()], writes=[bankr(b)])
                P.op("act", lambda e, dc=dc, b=b: e.activation(out=pgT.v(dc * NT, (dc + 1) * NT), in_=bank(b, NT), func=AF.Copy,
                                                               scale=cf.v(CF_PSCALE + dc, CF_PSCALE + dc + 1)),
                     reads=[bankr(b), cf.r()], writes=[pgT.r(dc * NT, (dc + 1) * NT)])
            attention(vq0, NB, NT)
            if dbg and NB == 2 and vq0 == 16 + 2 * dbg_cfg.get("dump_t", 0):
                for nm, bf_ in (("oT", oT), ("pgT", pgT), ("qT", qT)):
                    if nm in dbg_out:
                        P.dma("pool", s_dbg, [(dbg_out[nm], bf_.v())], reads=[bf_.r()])
            for M in range(8):
                sa, sp_ = sg[0], sg[1]
                def ev_ga(b, ml):
                    P.op("act", lambda e: e.activation(out=sa.v(ml * NT, (ml + 1) * NT), in_=bank(b, NT), func=AF.Sigmoid),
                         reads=[bankr(b)], writes=[sa.r(ml * NT, (ml + 1) * NT)])
                formA(ws, "ga", M, hT, NT, range(2), ev_ga)
                def ev_ab(b, ml):
                    P.op("dve", lambda e: e.tensor_tensor(out=tt.v(ml * NT, (ml + 1) * NT), in0=bank(b, NT),
                                                          in1=sa.v(ml * NT, (ml + 1) * NT), op=ALU.mult),
                         reads=[bankr(b), sa.r(ml * NT, (ml + 1) * NT)], writes=[tt.r(ml * NT, (ml + 1) * NT)])
                formA(ws, "ab", M, oT, NT, range(2), ev_ab)
                def ev_gp(b, ml):
                    P.op("act", lambda e: e.activation(out=sp_.v(ml * NT, (ml + 1) * NT), in_=bank(b, NT), func=AF.Sigmoid),
                         reads=[bankr(b)], writes=[sp_.r(ml * NT, (ml + 1) * NT)])
                formA(ws, "gp", M, hT, NT, range(2), ev_gp)
                def ev_pb(b, ml, M=M):
                    m = M * 2 + ml
                    P.op("dve", lambda e: e.tensor_tensor(out=sp_.v(ml * NT, (ml + 1) * NT), in0=bank(b, NT),
                                                          in1=sp_.v(ml * NT, (ml + 1) * NT), op=ALU.mult),
                         reads=[bankr(b), sp_.r(ml * NT, (ml + 1) * NT)], writes=[sp_.r(ml * NT, (ml + 1) * NT)])
                    P.op("dve", lambda e: e.tensor_tensor(out=mxT.v(m * NT, (m + 1) * NT), in0=sp_.v(ml * NT, (ml + 1) * NT),
                                                          in1=tt.v(ml * NT, (ml + 1) * NT), op=ALU.add),
                         reads=[sp_.r(ml * NT, (ml + 1) * NT), tt.r(ml * NT, (ml + 1) * NT)],
                         writes=[mxT.r(m * NT, (m + 1) * NT)])
                formA(ws, "pb", M, pgT, NT, range(2), ev_pb)
            formB_tokmajor(ws, "o", [[g] for g in range(8)], mxT, NT, NB, Bf, 40)
            norm_residual(8, NB, 0, Bf, 40, Ablk, Ablk)

        def ffn(NB, up_only=False, src=None):
            NT = NB * 128
            ws = WStream(ffn_items(up_only))
            load_gain(2, 0)
            Ablk = [(A.v(j * D, (j + 1) * D), A.r(j * D, (j + 1) * D)) for j in range(NB)]
            front_end(src if src is not None else Ablk, 0, hT, NT)
            if not up_only:
                load_gain(3, 0)
            cw0 = CF_CONVW
            pending = []
            for Fg in range(22):
                for part in range(2):
                    name = "upg" if part == 0 else "upv"
                    def ev(b, ml, Fg=Fg, part=part):
                        f = Fg * 2 + ml
                        ch = part * NFC + f
                        u_ = ubuf[(f + part) % 2]
                        tcb = tcv[f % 3] if part == 0 else tcv[3 + f % 2]
                        hv = uph.v(ch * 2, ch * 2 + 2)
                        hr = uph.r(ch * 2, ch * 2 + 2)
                        P.op("act", lambda e: e.activation(out=u_.v(2, 2 + NT), in_=bank(b, NT), func=AF.Copy),
                             reads=[bankr(b)], writes=[u_.r(2, 2 + NT)])
                        P.op("pool", lambda e: e.tensor_copy(u_.v(0, 2), hv), reads=[hr], writes=[u_.r(0, 2)])
                        if up_only:
                            P.op("pool", lambda e: e.tensor_copy(hv, u_.v(NT, NT + 2)), reads=[u_.r(NT, NT + 2)], writes=[hr])
                            return
                        w0 = cf.v(cw0 + ch * 3 + 0, cw0 + ch * 3 + 1)
                        w1 = cf.v(cw0 + ch * 3 + 1, cw0 + ch * 3 + 2)
                        w2 = cf.v(cw0 + ch * 3 + 2, cw0 + ch * 3 + 3)
                        bb = cf.v(CF_CONVB + ch, CF_CONVB + ch + 1)
                        P.op("act", lambda e: e.activation(out=tcb.v(0, NT), in_=bank(b, NT), func=AF.Identity, scale=w2, bias=bb),
                             reads=[bankr(b), cf.r()], writes=[tcb.r(0, NT)])
                        P.op("dve", lambda e: e.scalar_tensor_tensor(out=tcb.v(0, NT), in0=u_.v(1, 1 + NT), scalar=w1,
                                                                     in1=tcb.v(0, NT), op0=ALU.mult, op1=ALU.add),
                             reads=[u_.r(1, 1 + NT), tcb.r(0, NT), cf.r()], writes=[tcb.r(0, NT)])
                        P.op("dve", lambda e: e.scalar_tensor_tensor(out=tcb.v(0, NT), in0=u_.v(0, NT), scalar=w0,
                                                                     in1=tcb.v(0, NT), op0=ALU.mult, op1=ALU.add),
                             reads=[u_.r(0, NT), tcb.r(0, NT), cf.r()], writes=[tcb.r(0, NT)])
                        P.op("pool", lambda e: e.tensor_copy(hv, u_.v(NT, NT + 2)), reads=[u_.r(NT, NT + 2)], writes=[hr])
                        glb = gl[ml]
                        if part == 0:
                            def tail():
                                P.op("act", lambda e: e.activation(out=glb.v(0, NT), in_=tcb.v(0, NT), func=AF.Gelu_apprx_tanh),
                                     reads=[tcb.r(0, NT)], writes=[glb.r(0, NT)])
                        else:
                            def tail():
                                P.op("dve", lambda e: e.tensor_tensor(out=actT.v(f * NT, (f + 1) * NT), in0=glb.v(0, NT),
                                                                      in1=tcb.v(0, NT), op=ALU.mult),
                                     reads=[glb.r(0, NT), tcb.r(0, NT)], writes=[actT.r(f * NT, (f + 1) * NT)])
                        pending.append(tail)
                        while len(pending) > 1:
                            pending.pop(0)()
                    formA(ws, name, Fg, hT, NT, range(2), ev)
            while pending:
                pending.pop(0)()
            if up_only:
                return
            cis = [[ksi * 4 + cg for ksi in range(6)] for cg in range(4)]
            formB_tokmajor(ws, "d", cis, actT, NT, NB, Bf, 40)
            norm_residual(4, NB, 0, Bf, 40, Ablk, Ablk)

        def ple(NB, tok0):
            NT = NB * 128
            ws = WStream(ple_items())
            load_gain(4, 0)
            Ablk = [(A.v(j * D, (j + 1) * D), A.r(j * D, (j + 1) * D)) for j in range(NB)]
            P.dma("sp", s_pin, [(pin.v().rearrange("p (j c) -> p j c", c=PLE),
                                 pown[tok0:tok0 + NT, :].rearrange("(j p) c -> p j c", p=128))], writes=[pin.r()])
            P.op("dve", lambda e: e.tensor_copy(pnb.v(), pin.v()), reads=[pin.r()], writes=[pnb.r()])
            for j in range(NB):
                xs = 0
                xnv = xn.v(xs * D, (xs + 1) * D)
                xnr = xn.r(xs * D, (xs + 1) * D)
                P.op("act", lambda e, j=j, xnv=xnv: e.activation(out=xnv, in_=Ablk[j][0], func=AF.Copy),
                     reads=[Ablk[j][1]], writes=[xnr])
                for half in range(2):
                    b = next_tbank()
                    pb = bank(b).bitcast(BF16)

                    def tr(e, half=half, pb=pb, xs=xs):
                        ins = None
                        for k in range(8):
                            kc = half * 8 + k
                            ins = e.transpose(pb[:, k * 128:(k + 1) * 128],
                                              xn.v(xs * D + kc * 128, xs * D + (kc + 1) * 128), cmv(CM_IDENT))
                        return ins
                    P.op("pe", tr, reads=[xnr, cmr], writes=[bankr(b)])
                    dst = hT.v().rearrange("p (k t) -> p k t", t=NT)[:, half * 8:half * 8 + 8, j * 128:(j + 1) * 128]
                    srcv = pb.rearrange("p (k t) -> p k t", t=128)
                    if half == 0:
                        P.op("act", lambda e, dst=dst, srcv=srcv: e.activation(out=dst, in_=srcv, func=AF.Copy),
                             reads=[bankr(b)], writes=[hT.r()])
                    else:
                        P.op("dve", lambda e, dst=dst, srcv=srcv: e.tensor_copy(dst, srcv), reads=[bankr(b)], writes=[hT.r()])
                b = next_tbank()
                pb = bank(b).bitcast(BF16)

                def trp(e, j=j, pb=pb):
                    ins = None
                    for k in range(2):
                        ins = e.transpose(pb[:, k * 128:(k + 1) * 128], pnb.v(j * PLE + k * 128, j * PLE + (k + 1) * 128),
                                          cmv(CM_IDENT))
                    return ins
                P.op("pe", trp, reads=[pnb.r(), cmr], writes=[bankr(b)])
                dst = pT.v().rearrange("p (k t) -> p k t", t=NT)[:, :, j * 128:(j + 1) * 128]
                srcv = pb[:, 0:256].rearrange("p (k t) -> p k t", t=128)
                P.op("dve", lambda e, dst=dst, srcv=srcv: e.tensor_copy(dst, srcv), reads=[bankr(b)], writes=[pT.r()])
            P.dma("sp", s_wple, [(wpleb.v(), wdefs["ple"][2][0][:, 0:4096])], reads=[("wscr", 3, 4)], writes=[wpleb.r()])
            slot_ple = wpleb
            wple3 = wpleb.v().rearrange("p (k c) -> p k c", c=2048)
            pend = []
            for cg in range(8):
                slot = ws.take("pgate", cg)
                wg3 = w3(slot, 16, 256)
                for j in range(NB):
                    bg = next_fbank()
                    pieces = [(bank(bg, 256), hT.v(k * NT + j * 128, k * NT + (j + 1) * 128), wg3[:, k, :]) for k in range(16)]
                    P.op("pe", mm_group(bg, pieces), reads=[slot.r(), hT.r()], writes=[bankr(bg)])
                    be = next_fbank()
                    pieces = [(bank(be, 256), pT.v(k * NT + j * 128, k * NT + (j + 1) * 128), wple3[:, k, cg * 256:(cg + 1) * 256])
                              for k in range(2)]
                    P.op("pe", mm_group(be, pieces), reads=[slot_ple.r(), pT.r()], writes=[bankr(be)])
                    sgb = sgp[(cg * NB + j) % 2]
                    yv = Bf.v(j * D + cg * 256, j * D + (cg + 1) * 256)
                    yr = Bf.r(j * D + cg * 256, j * D + (cg + 1) * 256)
                    c = 40 + j * 8 + cg
                    P.op("act", lambda e, sgb=sgb, bg=bg: e.activation(out=sgb.v(), in_=bank(bg, 256), func=AF.Sigmoid),
                         reads=[bankr(bg)], writes=[sgb.r()])
                    P.op("dve", lambda e, sgb=sgb, be=be, yv=yv: e.tensor_tensor(out=yv, in0=bank(be, 256), in1=sgb.v(), op=ALU.mult),
                         reads=[bankr(be), sgb.r()], writes=[yr])

                    def tail(yv=yv, yr=yr, c=c):
                        P.op("act", lambda e: e.activation(out=gsq.v(0, 256), in_=yv, func=AF.Square,
                                                           accum_out=small.v(c, c + 1)),
                             reads=[yr], writes=[gsq.r(), small.r(c, c + 1)])
                    pend.append(tail)
                    while len(pend) > 1:
                        pend.pop(0)()
                ws.done()
            while pend:
                pend.pop(0)()
            norm_residual(8, NB, 0, Bf, 40, Ablk, Ablk)

        def load_x(vb0, NB, a0=0):
            for j in range(NB):
                vb = vb0 + j
                aj = a0 + j
                P.dma("sp", s_A[aj], [(A.v(aj * D, (aj + 1) * D), xall[vb * 128:(vb + 1) * 128, :])],
                      writes=[A.r(aj * D, (aj + 1) * D)])

        def dump(name, buf_ap, reg, dram_ap):
            P.dma("pool", s_dbg, [(dram_ap, buf_ap)], reads=[reg])

        load_x(15, 1)
        mixer(15, 1)
        halo_x1 = (Bf.v(3 * D, 4 * D), Bf.r(3 * D, 4 * D))
        P.op("act", lambda e: e.activation(out=halo_x1[0], in_=A.v(0, D), func=AF.Copy), reads=[A.r(0, D)], writes=[halo_x1[1]])
        ntiles = dbg_cfg.get("ntiles", 4)
        for t in range(ntiles):
            for sub in range(2):
                vb0 = 16 + 4 * t + 2 * sub
                load_x(vb0, 2, a0=2 * sub)
                mixer(vb0, 2, a0=2 * sub)
            if t == 0:
                ffn(1, up_only=True, src=[halo_x1])
            if dbg and t == dbg_cfg.get("dump_t", 0) and "x1" in dbg_out:
                dump("x1", A.v().rearrange("p (j c) -> p j c", c=D), A.r(),
                     dbg_out["x1"].rearrange("(j p) c -> p j c", p=128))
            ffn(4)
            if dbg and t == dbg_cfg.get("dump_t", 0) and "x2" in dbg_out:
                dump("x2", A.v().rearrange("p (j c) -> p j c", c=D), A.r(),
                     dbg_out["x2"].rearrange("(j p) c -> p j c", p=128))
            ple(4, t * 512)
            for j in range(4):
                r0 = t * 512 + j * 128
                P.dma("pool", s_Ast[j], [(out[r0:r0 + 128, :], A.v(j * D, (j + 1) * D))], reads=[A.r(j * D, (j + 1) * D)])
        P.final_wait("sp")

        with nc.Block() as block:
            @block.tensor
            def _(e):
                for f in P.q["pe"]:
                    f(e)

            @block.scalar
            def _(e):
                for f in P.q["act"]:
                    f(e)

            @block.vector
            def _(e):
                for f in P.q["dve"]:
                    f(e)

            @block.gpsimd
            def _(e):
                for f in P.q["pool"]:
                    f(e)

            @block.sync
            def _(e):
                for f in P.q["sp"]:
                    f(e)
        build_program.stats = dict(sb_used=sb_used, ops={k: len(v) for k, v in P.q.items()}, waits=P.nwaits)
    return nc


dbg_cfg = {}


def _const_mats(flag):
    j = np.arange(128)[:, None]
    s = np.arange(128)[None, :]
    mats = np.zeros((CM_N, 128, 128), np.float32)
    mats[CM_IDENT] = np.eye(128, dtype=np.float32)
    ntri = -(j >= s).astype(np.float32)
    mats[CM_NTRI] = ntri
    mats[CM_NONES] = -1.0
    mats[CM_NTRI_P] = ntri * flag
    mats[CM_NONES_P] = -1.0 * flag
    t = np.arange(128)[None, :]
    sidx = np.arange(128)[:, None]
    for g, w in enumerate((2, 4, 8, 16)):
        within = ((sidx <= t) & (sidx > t - w)).astype(np.float32)
        prevm = ((sidx - 128) > (t - w)).astype(np.float32)
        cnt_n = np.full((1, 128), float(w), np.float32)
        mats[CM_RW + g] = within / cnt_n - np.eye(128, dtype=np.float32)
        mats[CM_RP + g] = prevm / cnt_n
        if flag == 0.0:
            cnt0 = np.minimum(t + 1, w).astype(np.float32)
            mats[CM_RW0 + g] = within / cnt0 - np.eye(128, dtype=np.float32)
            mats[CM_RP0 + g] = 0.0
        else:
            mats[CM_RW0 + g] = mats[CM_RW + g]
            mats[CM_RP0 + g] = mats[CM_RP + g]
    tri = (j < s).astype(np.float32)
    z = np.zeros((128, 128), np.float32)
    o = np.ones((128, 128), np.float32)
    for k, m in enumerate((z, tri, z, tri)):
        mats[CM_MASK2A + k] = m
    for k, m in enumerate((tri, o, tri, o)):
        mats[CM_MASK2B + k] = m
    for k in range(4):
        mats[CM_MASK1 + k] = tri
    return np.ascontiguousarray(mats.transpose(1, 0, 2).reshape(128, CM_N * 128))


_NC_CACHE = {}


def kernel(x, p, norm_mix_pre, w_in, w_attn_branch, w_pool_group, pool_scale, w_pool_branch, w_out,
           norm_mix_post, norm_ffn_pre, w_up, conv_w, conv_b, w_down, norm_ffn_post, w_ple, w_ple_gate,
           norm_ple_post, _dbg=None):
    f = lambda a: np.ascontiguousarray(np.asarray(a, dtype=np.float32))
    x = f(x); p = f(p)
    gains = np.stack([f(norm_mix_pre)[0], f(norm_mix_post)[0], f(norm_ffn_pre)[0], f(norm_ffn_post)[0],
                      f(norm_ple_post)[0]], axis=0)
    colf = np.zeros((128, CF_N), np.float32)
    colf[:, CF_PSCALE:CF_PSCALE + 8] = f(pool_scale)[0].reshape(8, 128).T
    cw = f(conv_w)[0]
    colf[:, CF_CONVW:CF_CONVW + 264] = cw.reshape(3, 88, 128).transpose(2, 1, 0).reshape(128, 264)
    colf[:, CF_CONVB:CF_CONVB + 88] = f(conv_b)[0].reshape(88, 128).T
    shared = dict(w_in=f(w_in)[0], w_ab=f(w_attn_branch)[0], w_pg=f(w_pool_group)[0], w_pb=f(w_pool_branch)[0],
                  w_o=f(w_out)[0], w_up=f(w_up)[0], w_d=f(w_down)[0], w_ple=f(w_ple)[0], w_pgate=f(w_ple_gate)[0],
                  gains=gains, colf=colf)
    cm = {0.0: _const_mats(0.0), 1.0: _const_mats(1.0)}
    in_maps = []
    for c in range(8):
        b, half = c // 2, c % 2
        if half == 0:
            xall = np.concatenate([np.zeros((2048, D), np.float32), x[b, :2048]], axis=0)
        else:
            xall = x[b]
        m = dict(shared)
        m["xall"] = np.ascontiguousarray(xall)
        m["pown"] = np.ascontiguousarray(p[0, b, half * 2048:(half + 1) * 2048])
        m["cmat"] = cm[float(half)]
        in_maps.append(m)
    key = repr(_dbg)
    if key not in _NC_CACHE:
        _NC_CACHE[key] = build_program(_dbg)
    nc = _NC_CACHE[key]
    res = run_bass_kernel_spmd(nc, in_maps, core_ids=list(range(8)))
    outp = np.empty((4, 4096, D), np.float32)
    for c in range(8):
        b, half = c // 2, c % 2
        outp[b, half * 2048:(half + 1) * 2048] = res.results[c]["out"]
    if _dbg:
        return outp, res.results
    return outp
```

```python
import numpy as np
import concourse.bass as bass
import concourse.mybir as mybir
from concourse.bass_utils import run_bass_kernel_spmd

F32 = mybir.dt.float32
BF16 = mybir.dt.bfloat16
U8 = mybir.dt.uint8
AF = mybir.ActivationFunctionType
ALU = mybir.AluOpType

D = 2048
NH = 8
DFF = 5632
NFC = DFF // 128
PLE = 256
EPS = 1e-6
NVB = 32
QSCALE = 128.0 ** -0.5
SLOT_ELEMS = 4096
NSLOT = 4
SB_BYTES = 212480

CM_IDENT = 0
CM_NTRI = 1
CM_NONES = 2
CM_NTRI_P = 3
CM_NONES_P = 4
CM_RW = 5
CM_RP = 9
CM_RW0 = 13
CM_RP0 = 17
CM_MASK2A = 21
CM_MASK2B = 25
CM_MASK1 = 29
CM_N = 33

CF_PSCALE = 0
CF_CONVW = 8
CF_CONVB = 8 + 264
CF_N = 8 + 264 + 88


class DSem:
    def __init__(self, h):
        self.h = h
        self.count = 0


class Buf:
    def __init__(self, SB, lo, nelem, dt):
        self.esz = 4 if dt == F32 else 2
        self.lo = lo
        self.n = nelem
        self.dt = dt
        self.ap = SB[:, lo:lo + nelem * self.esz].bitcast(dt)

    def v(self, a=0, b=None):
        b = self.n if b is None else b
        return self.ap[:, a:b]

    def r(self, a=0, b=None):
        b = self.n if b is None else b
        return ("sb", self.lo + a * self.esz, self.lo + b * self.esz)


class Prog:
    COMPUTE = ("pe", "act", "dve", "pool")

    def __init__(self, nc, esems):
        self.nc = nc
        self.q = {e: [] for e in ("pe", "act", "dve", "pool", "sp")}
        self.esem = esems
        self.cnt = {e: 0 for e in self.COMPUTE}
        self.waited = {e: {} for e in self.q}
        self.regs = {}
        self.dsems = []
        self.nwaits = 0

    def _entries(self, reg):
        sp, lo, hi = reg
        lst = self.regs.setdefault(sp, [])
        return [e for e in lst if e[0] < hi and lo < e[1]], lst

    def _collect(self, reads, writes):
        raw, other = {}, {}

        def add(d, tok):
            if tok is None:
                return
            k = tok[0]
            if k not in d or d[k][2] < tok[2]:
                d[k] = tok

        for reg in reads:
            ov, _ = self._entries(reg)
            for e in ov:
                add(raw, e[2])
        for reg in writes:
            ov, _ = self._entries(reg)
            for e in ov:
                add(other, e[2])
                for t in e[3].values():
                    add(other, t)
        return raw, other

    def _commit(self, tok, reads, writes):
        for reg in reads:
            sp, lo, hi = reg
            ov, lst = self._entries(reg)
            pos = lo
            for e in sorted(ov, key=lambda e: e[0]):
                if e[0] > pos:
                    lst.append([pos, e[0], None, {tok[0]: tok}])
                old = e[3].get(tok[0])
                if old is None or old[2] < tok[2]:
                    e[3][tok[0]] = tok
                pos = max(pos, e[1])
            if pos < hi:
                lst.append([pos, hi, None, {tok[0]: tok}])
        for reg in writes:
            sp, lo, hi = reg
            ov, lst = self._entries(reg)
            for e in ov:
                lst.remove(e)
                if e[0] < lo:
                    lst.append([e[0], lo, e[2], dict(e[3])])
                if e[1] > hi:
                    lst.append([hi, e[1], e[2], dict(e[3])])
            lst.append([lo, hi, tok, {}])

    def _waits(self, eng, raw, other):
        need = dict(raw)
        for k, t in other.items():
            if k == eng and eng == "pe":
                continue
            if k not in need or need[k][2] < t[2]:
                need[k] = t
        out = []
        wd = self.waited[eng]
        for k, t in need.items():
            if wd.get(k, 0) >= t[2]:
                continue
            wd[k] = t[2]
            out.append((t[1], t[2]))
        self.nwaits += len(out)
        return out

    def op(self, eng, fn, reads=(), writes=()):
        raw, other = self._collect(reads, writes)
        waits = self._waits(eng, raw, other)
        n = self.cnt[eng] + 1
        self.cnt[eng] = n
        sem = self.esem[eng]
        tok = (eng, sem, n)
        def emit(e, waits=waits, fn=fn, sem=sem):
            for (s, v) in waits:
                e.wait_ge(s, v)
            fn(e).then_inc(sem, 1)
        self.q[eng].append(emit)
        self._commit(tok, reads, writes)
        return tok

    def dma(self, q, dsem, pairs, reads=(), writes=()):
        raw, other = self._collect(reads, writes)
        waits = self._waits(q, raw, other)
        dsem.count += 16 * len(pairs)
        tok = (id(dsem), dsem.h, dsem.count)
        def emit(e, waits=waits, pairs=pairs, h=dsem.h):
            for (s, v) in waits:
                e.wait_ge(s, v)
            for (o, i) in pairs:
                e.dma_start(out=o, in_=i).then_inc(h, 16)
        self.q[q].append(emit)
        self._commit(tok, reads, writes)
        return tok

    def final_wait(self, q):
        sems = [(d.h, d.count) for d in self.dsems if d.count > 0]
        sems += [(self.esem[e], self.cnt[e]) for e in self.COMPUTE if self.cnt[e] > 0]
        def emit(e):
            for (s, v) in sems:
                e.wait_ge(s, v)
        self.q[q].append(emit)


def build_program(dbg=None):
    nc = bass.Bass("TRN2", target_bir_lowering=False)
    dt_in = lambda name, shape: nc.dram_tensor(name, shape, F32, kind="ExternalInput").ap()
    xall = dt_in("xall", [4096, D])
    pown = dt_in("pown", [2048, PLE])
    w_in = dt_in("w_in", [D, 8192])
    w_ab = dt_in("w_ab", [1024, D])
    w_pg = dt_in("w_pg", [4, 256, 256])
    w_pb = dt_in("w_pb", [1024, D])
    w_o = dt_in("w_o", [D, D])
    w_up = dt_in("w_up", [D, 2 * DFF])
    w_d = dt_in("w_d", [DFF, D])
    w_ple = dt_in("w_ple", [PLE, D])
    w_pgate = dt_in("w_pgate", [D, D])
    gains = dt_in("gains", [5, D])
    colf = dt_in("colf", [128, CF_N])
    cmat = dt_in("cmat", [128, CM_N * 128])
    out = nc.dram_tensor("out", [2048, D], F32, kind="ExternalOutput").ap()
    KTs = nc.dram_tensor("KTs", [NH, 128, 4096], BF16, kind="Internal").ap()
    VS = nc.dram_tensor("VS", [NH, 128, NVB, 128], BF16, kind="Internal").ap()
    dbg_out = {}
    if dbg:
        for name, shape in dbg.items():
            dbg_out[name] = nc.dram_tensor("dbg_" + name, shape, F32, kind="ExternalOutput").ap()

    wdefs = {}

    def defw(name, src, K, c_lo, c_hi, cw, ksplit=16):
        chunks = []
        nkc = K // 128
        for ks in range(0, nkc, ksplit):
            nk = min(ksplit, nkc - ks)
            for c0 in range(c_lo, c_hi, cw):
                chunks.append((ks, nk, c0, cw))
        scr = nc.dram_tensor("ws_" + name, [len(chunks), 128, SLOT_ELEMS], BF16, kind="Internal").ap()
        wdefs[name] = (src, chunks, scr)

    defw("q", w_in, D, 0, 1024, 256)
    defw("u", w_in, D, 3072, 4096, 256)
    defw("ga", w_in, D, 4096, 6144, 256)
    defw("gp", w_in, D, 6144, 8192, 256)
    defw("ab", w_ab, 1024, 0, D, 256)
    defw("pb", w_pb, 1024, 0, D, 256)
    defw("o", w_o, D, 0, D, 256)
    defw("upg", w_up, D, 0, DFF, 256)
    defw("upv", w_up, D, DFF, 2 * DFF, 256)
    defw("d", w_d, DFF, 0, D, 512, ksplit=8)
    defw("ple", w_ple, PLE, 0, D, 2048)
    defw("pgate", w_pgate, D, 0, D, 256)

    import contextlib
    with contextlib.ExitStack() as es:
        SBT = es.enter_context(nc.sbuf_tensor("SB", [128, SB_BYTES], U8))
        PS = es.enter_context(nc.psum_tensor("PS", [128, 8 * 512], F32))
        esems = {e: es.enter_context(nc.semaphore("sem_" + e)) for e in Prog.COMPUTE}
        P = Prog(nc, esems)

        def new_dsem(name):
            d = DSem(es.enter_context(nc.semaphore(name)))
            P.dsems.append(d)
            return d

        off = [0]

        def alloc(nelem, dt):
            esz = 4 if dt == F32 else 2
            lo = (off[0] + 63) // 64 * 64
            off[0] = lo + nelem * esz
            assert off[0] <= SB_BYTES, off[0]
            return Buf(SBT, lo, nelem, dt)

        cm = alloc(CM_N * 128, BF16)
        cf = alloc(CF_N, F32)
        wpg = alloc(4 * 2 * 256, BF16)
        A = alloc(4 * D, F32)
        xn = alloc(D, BF16)
        gbc = [alloc(D, F32)]
        hT = alloc(16 * 512, BF16)
        small = alloc(128, F32)
        uph = alloc(88 * 2, F32)
        uprev = alloc(1024, BF16)
        gsq = alloc(512, BF16)
        kv_lo = off[0]
        Bf = alloc(4 * D, F32)
        ring = [alloc(SLOT_ELEMS, BF16) for _ in range(NSLOT)]
        union_lo = off[0]
        qT = alloc(8 * 256, BF16)
        oT = alloc(8 * 256, BF16)
        ub = alloc(2 * 1024, BF16)
        plT = alloc(8 * 256, BF16)
        pgT = alloc(8 * 256, BF16)
        mxT = alloc(16 * 256, BF16)
        Eb = [alloc(512, F32) for _ in range(2)]
        Lb = [alloc(512, BF16) for _ in range(3)]
        ab_ = [alloc(512, BF16) for _ in range(3)]
        Sb = [alloc(512, BF16) for _ in range(2)]
        kvr = [alloc(2048, BF16) for _ in range(3)]
        sg = [alloc(2 * 256, F32) for _ in range(2)]
        tt = alloc(2 * 256, F32)
        mixer_hi = off[0]
        off[0] = kv_lo
        wk = alloc(16 * 1024, BF16)
        wv = alloc(16 * 1024, BF16)
        hT4s = [hT, alloc(16 * 512, BF16)]
        kst = [alloc(512, BF16) for _ in range(3)]
        vst = [alloc(1024, BF16) for _ in range(2)]
        kv_hi = off[0]
        off[0] = union_lo
        actT = alloc(NFC * 512, BF16)
        ubuf = [alloc(514, F32) for _ in range(2)]
        tcv = [alloc(512, F32) for _ in range(5)]
        gl = [alloc(512, F32) for _ in range(2)]
        ffn_hi = off[0]
        off[0] = union_lo
        pin = alloc(4 * PLE, F32)
        pnb = alloc(4 * PLE, BF16)
        pT = alloc(2 * 512, BF16)
        sgp = [alloc(256, F32) for _ in range(2)]
        wpleb = alloc(2 * 2048, BF16)
        ple_hi = off[0]
        off[0] = max(mixer_hi, kv_hi, ffn_hi, ple_hi)
        sb_used = off[0]

        s_wgrp = [new_dsem("s_wg%d" % i) for i in range(4)]
        s_kvw = new_dsem("s_kvw")
        s_A = [new_dsem("s_A%d" % i) for i in range(4)]
        s_Ast = [new_dsem("s_Ast%d" % i) for i in range(4)]
        s_g = [new_dsem("s_g%d" % i) for i in range(1)]
        s_ring = [new_dsem("s_ring%d" % i) for i in range(NSLOT)]
        s_kvr = [new_dsem("s_kvr%d" % i) for i in range(3)]
        s_kst = [new_dsem("s_kst%d" % i) for i in range(3)]
        s_vst = [new_dsem("s_vst%d" % i) for i in range(2)]
        s_pin = new_dsem("s_pin")
        s_wple = new_dsem("s_wple")
        s_c = [new_dsem("s_c%d" % i) for i in range(5)]
        s_dbg = new_dsem("s_dbg")

        def bank(b, n=512, c0=0):
            return PS[:, b * 512 + c0:b * 512 + c0 + n]

        def bankr(b):
            return ("ps", b, b + 1)

        fb = [0]

        def next_fbank():
            b = fb[0] % 6
            fb[0] += 1
            return b

        tb = [0]

        def next_tbank():
            b = 6 + tb[0] % 2
            tb[0] += 1
            return b

        def cmv(idx, n=128):
            return cm.v(idx * 128, idx * 128 + n)

        cmr = cm.r()

        P.dma("pool", s_c[0], [(cm.v(), cmat)], writes=[cm.r()])
        P.dma("sp", s_c[1], [(cf.v(), colf)], writes=[cf.r()])
        P.dma("pool", s_c[2],
              [(wpg.v().rearrange("p (g k c) -> p g k c", g=4, k=2),
                w_pg.rearrange("g (k p) c -> p g k c", p=128))], writes=[wpg.r()])
        for (wb, c0, sc) in ((wk, 1024, s_c[3]), (wv, 2048, s_c[4])):
            pairs = []
            for k4 in range(0, 16, 4):
                pairs.append((wb.v().rearrange("p (k c) -> p k c", c=1024)[:, k4:k4 + 4, :],
                              w_in[k4 * 128:(k4 + 4) * 128, c0:c0 + 1024].rearrange("(k p) c -> p k c", p=128)))
            P.dma("pool", sc, pairs, writes=[wb.r()])
        P.op("dve", lambda e: e.memset(uph.v(), 0.0), writes=[uph.r()])
        P.op("dve", lambda e: e.memset(uprev.v(), 0.0), writes=[uprev.r()])

        conv_order = [("q", 0), ("u", 0), ("ga", 1), ("ab", 1), ("gp", 1), ("pb", 1), ("o", 1),
                      ("upg", 2), ("upv", 2), ("d", 3), ("ple", 3), ("pgate", 3)]
        wtok = {}
        grp_pairs = {g: [] for g in range(4)}
        for name, g in conv_order:
            src, chunks, scr = wdefs[name]
            for ci, (ks, nk, c0, cw) in enumerate(chunks):
                dst = scr[ci][:, 0:nk * cw].rearrange("p (k c) -> p k c", c=cw)
                for k4 in range(0, nk, 4):
                    kk = min(4, nk - k4)
                    r0 = (ks + k4) * 128
                    grp_pairs[g].append((dst[:, k4:k4 + kk, :],
                                         src[r0:r0 + kk * 128, c0:c0 + cw].rearrange("(k p) c -> p k c", p=128)))
        for g in range(2):
            tok = P.dma("pool", s_wgrp[g], grp_pairs[g], reads=[wk.r(), wv.r(), cm.r()], writes=[("wscr", g, g + 1)])

        def late_conversions():
            for g in range(2, 4):
                P.dma("pool", s_wgrp[g], grp_pairs[g], reads=[("vs", NVB - 1, NVB)], writes=[("wscr", g, g + 1)])
        wgrp_of = {name: g for name, g in conv_order}

        ring_state = {"next": 0, "queue": []}

        def ring_issue(item):
            name, ci = item
            src, chunks, scr = wdefs[name]
            ks, nk, c0, cw = chunks[ci]
            slot = ring_state["next"] % NSLOT
            ring_state["next"] += 1
            n = nk * cw
            g = wgrp_of[name]
            P.dma("sp", s_ring[slot], [(ring[slot].v(0, n), scr[ci][:, 0:n])],
                  reads=[("wscr", g, g + 1)], writes=[ring[slot].r()])
            return slot

        class WStream:
            def __init__(self, items):
                self.items = items
                self.issued = 0
                self.slots = {}
                self.cur = 0
                for _ in range(min(NSLOT, len(items))):
                    self._issue()

            def _issue(self):
                if self.issued < len(self.items):
                    self.slots[self.issued] = ring_issue(self.items[self.issued])
                    self.issued += 1

            def take(self, name, ci):
                assert self.items[self.cur] == (name, ci), (self.items[self.cur], name, ci)
                slot = self.slots.pop(self.cur)
                self.cur += 1
                return ring[slot]

            def done(self):
                self._issue()

        def load_gain(gi, slot):
            P.dma("sp", s_g[slot], [(gbc[slot].v(), gains[gi].partition_broadcast(128))], writes=[gbc[slot].r()])

        def rstd_from(col_in, col_out, scratch_col):
            ci, co, cs_ = small.v(col_in, col_in + 1), small.v(col_out, col_out + 1), small.v(scratch_col, scratch_col + 1)
            P.op("dve", lambda e: e.tensor_scalar(out=cs_, in0=ci, scalar1=1.0 / D, scalar2=EPS, op0=ALU.mult, op1=ALU.add),
                 reads=[small.r(col_in, col_in + 1)], writes=[small.r(scratch_col, scratch_col + 1)])
            P.op("act", lambda e: e.activation(out=cs_, in_=cs_, func=AF.Ln),
                 reads=[small.r(scratch_col, scratch_col + 1)], writes=[small.r(scratch_col, scratch_col + 1)])
            P.op("act", lambda e: e.activation(out=co, in_=cs_, func=AF.Exp, scale=-0.5),
                 reads=[small.r(scratch_col, scratch_col + 1)], writes=[small.r(col_out, col_out + 1)])

        def transpose_block(xs, hTb, NT, jj):
            xnr = xn.r(xs * D, (xs + 1) * D)
            for half in range(2):
                b = next_tbank()
                pb = bank(b).bitcast(BF16)

                def tr(e, half=half, pb=pb, xs=xs):
                    ins = None
                    for k in range(8):
                        kc = half * 8 + k
                        ins = e.transpose(pb[:, k * 128:(k + 1) * 128],
                                          xn.v(xs * D + kc * 128, xs * D + (kc + 1) * 128), cmv(CM_IDENT))
                    return ins
                P.op("pe", tr, reads=[xnr, cmr], writes=[bankr(b)])
                dst = hTb.v().rearrange("p (k t) -> p k t", t=NT)[:, half * 8:half * 8 + 8, jj * 128:(jj + 1) * 128]
                srcv = pb.rearrange("p (k t) -> p k t", t=128)
                if half == 0:
                    P.op("act", lambda e, dst=dst, srcv=srcv: e.activation(out=dst, in_=srcv, func=AF.Copy),
                         reads=[bankr(b)], writes=[hTb.r()])
                else:
                    P.op("dve", lambda e, dst=dst, srcv=srcv: e.tensor_copy(dst, srcv),
                         reads=[bankr(b)], writes=[hTb.r()])

        def front_end(src_blocks, gslot, hTb, NT, j0=0):
            for j, (xap, xr) in enumerate(src_blocks):
                xs = 0
                sq = small.v(j, j + 1)
                rs = small.v(8 + j, 9 + j)
                xnv = xn.v(xs * D, (xs + 1) * D)
                xnr = xn.r(xs * D, (xs + 1) * D)
                P.op("act", lambda e, xap=xap, xnv=xnv, sq=sq: e.activation(out=xnv, in_=xap, func=AF.Square, accum_out=sq),
                     reads=[xr], writes=[xnr, small.r(j, j + 1)])
                rstd_from(j, 8 + j, 16 + j)
                P.op("dve", lambda e, xap=xap, rs=rs, xnv=xnv: e.scalar_tensor_tensor(
                    out=xnv, in0=xap, scalar=rs, in1=gbc[gslot].v(), op0=ALU.mult, op1=ALU.mult),
                    reads=[xr, small.r(8 + j, 9 + j), gbc[gslot].r()], writes=[xnr])
                transpose_block(xs, hTb, NT, j0 + j)

        def mm_group(b, pieces, first_start=True):
            def f(e):
                ins = None
                for i, (o, l, r) in enumerate(pieces):
                    ins = e.matmul(o, l, r, start=(first_start and i == 0), stop=(i == len(pieces) - 1),
                                   skip_group_check=True)
                return ins
            return f

        def w3(slotbuf, nk, cw):
            return slotbuf.v(0, nk * cw).rearrange("p (k c) -> p k c", c=cw)

        def formA(ws, name, ci, actTb, NT, mlist, evac):
            src, chunks, scr = wdefs[name]
            ks, nk, c0, cw = chunks[ci]
            slot = ws.take(name, ci)
            wv3 = w3(slot, nk, cw)
            for ml in mlist:
                b = next_fbank()
                pieces = [(bank(b, NT), wv3[:, k, ml * 128:(ml + 1) * 128],
                           actTb.v((ks + k) * NT, (ks + k + 1) * NT)) for k in range(nk)]
                P.op("pe", mm_group(b, pieces), reads=[slot.r(), actTb.r()], writes=[bankr(b)])
                evac(b, ml)
            ws.done()

        load_gain(0, 0)
        wk3 = wk.v().rearrange("p (k c) -> p k c", c=1024)
        wv3 = wv.v().rearrange("p (k c) -> p k c", c=1024)

        def kv_fe_block(t, jj):
            vb = t * 4 + jj
            P.dma("sp", s_A[jj], [(A.v(jj * D, (jj + 1) * D), xall[vb * 128:(vb + 1) * 128, :])],
                  writes=[A.r(jj * D, (jj + 1) * D)])
            front_end([(A.v(jj * D, (jj + 1) * D), A.r(jj * D, (jj + 1) * D))], 0, hT4s[t % 2], 512, j0=jj)

        def kv_mm_parts(t):
            hT4 = hT4s[t % 2]
            parts = []
            for h in range(NH):
                def part(h=h):
                    b = next_fbank()
                    pieces = [(bank(b), wk3[:, k, h * 128:(h + 1) * 128], hT4.v(k * 512, (k + 1) * 512)) for k in range(16)]
                    P.op("pe", mm_group(b, pieces), reads=[wk.r(), hT4.r()], writes=[bankr(b)])
                    st = (t * NH + h) % 3
                    if h % 2 == 0:
                        P.op("act", lambda e: e.activation(out=kst[st].v(), in_=bank(b), func=AF.Copy),
                             reads=[bankr(b)], writes=[kst[st].r()])
                    else:
                        P.op("dve", lambda e: e.tensor_copy(kst[st].v(), bank(b)),
                             reads=[bankr(b)], writes=[kst[st].r()])
                    P.dma("sp", s_kst[st], [(KTs[h][:, t * 512:(t + 1) * 512], kst[st].v())],
                          reads=[kst[st].r()], writes=[("kt", h * 8 + t, h * 8 + t + 1)])
                parts.append(part)
            for jj in range(4):
                for cg in range(2):
                    def part(jj=jj, cg=cg):
                        vb = t * 4 + jj
                        st = vb % 2
                        b = next_fbank()
                        pieces = [(bank(b), hT4.v(k * 512 + jj * 128, k * 512 + (jj + 1) * 128),
                                   wv3[:, k, cg * 512:(cg + 1) * 512]) for k in range(16)]
                        P.op("pe", mm_group(b, pieces), reads=[wv.r(), hT4.r()], writes=[bankr(b)])
                        if cg == 0:
                            P.op("act", lambda e: e.activation(out=vst[st].v(0, 512), in_=bank(b), func=AF.Copy),
                                 reads=[bankr(b)], writes=[vst[st].r(0, 512)])
                        else:
                            P.op("dve", lambda e: e.tensor_copy(vst[st].v(512, 1024), bank(b)),
                                 reads=[bankr(b)], writes=[vst[st].r(512, 1024)])
                            P.dma("sp", s_vst[st],
                                  [(VS[:, :, vb, :].rearrange("h p d -> p h d"),
                                    vst[st].v().rearrange("p (h d) -> p h d", d=128))],
                                  reads=[vst[st].r()], writes=[("vs", vb, vb + 1)])
                    parts.append(part)
            return parts

        for jj in range(4):
            kv_fe_block(0, jj)
        for t in range(8):
            parts = kv_mm_parts(t)
            for i in range(4):
                if t + 1 < 8:
                    kv_fe_block(t + 1, i)
                for p_ in parts[4 * i:4 * i + 4]:
                    p_()

        late_conversions()

        def mixer_items():
            it = [("q", c) for c in range(4)] + [("u", c) for c in range(4)]
            for M in range(8):
                it += [("ga", M), ("ab", M), ("gp", M), ("pb", M)]
            it += [("o", g) for g in range(8)]
            return it

        def ffn_items(up_only=False):
            it = []
            for Fg in range(22):
                it += [("upg", Fg), ("upv", Fg)]
            if not up_only:
                for cg in range(4):
                    for ksi in range(6):
                        it.append(("d", ksi * 4 + cg))
            return it

        def ple_items():
            return [("pgate", g) for g in range(8)]

        def norm_residual(ncg, NB, gslot, ybuf, sscol0, resid, dest):
            for j in range(NB):
                c = sscol0 + j * 8
                ssum = small.v(32 + j, 33 + j)
                P.op("dve", lambda e, c=c, ssum=ssum: e.tensor_reduce(out=ssum, in_=small.v(c, c + ncg),
                                                                      axis=mybir.AxisListType.X, op=ALU.add),
                     reads=[small.r(c, c + ncg)], writes=[small.r(32 + j, 33 + j)])
                rstd_from(32 + j, 36 + j, 20 + j)
                ssum = small.v(36 + j, 37 + j)
                yv = ybuf.v(j * D, (j + 1) * D)
                yr = ybuf.r(j * D, (j + 1) * D)
                P.op("dve", lambda e, yv=yv, ssum=ssum: e.scalar_tensor_tensor(
                    out=yv, in0=yv, scalar=ssum, in1=gbc[gslot].v(), op0=ALU.mult, op1=ALU.mult),
                    reads=[yr, small.r(36 + j, 37 + j), gbc[gslot].r()], writes=[yr])
                rv, rr = resid[j]
                dv, dr = dest[j]
                P.op("dve", lambda e, yv=yv, rv=rv, dv=dv: e.tensor_tensor(out=dv, in0=yv, in1=rv, op=ALU.add),
                     reads=[yr, rr], writes=[dr])

        def formB_tokmajor(ws, name, cis, actTb, NT, NB, ybuf, sscol0, nk_total_chunks=None):
            for cg, chunk_list in enumerate(cis):
                banks = [next_fbank() for _ in range(NB)]
                for idx, ci in enumerate(chunk_list):
                    src, chunks, scr = wdefs[name]
                    ks, nk, c0, cw = chunks[ci]
                    slot = ws.take(name, ci)
                    wv3_ = w3(slot, nk, cw)
                    for j in range(NB):
                        b = banks[j]
                        pieces = [(bank(b, cw), actTb.v((ks + k) * NT + j * 128, (ks + k) * NT + (j + 1) * 128), wv3_[:, k, :])
                                  for k in range(nk)]
                        P.op("pe", mm_group(b, pieces, first_start=(idx == 0)), reads=[slot.r(), actTb.r()],
                             writes=[bankr(b)])
                    ws.done()
                for j in range(NB):
                    b = banks[j]
                    yv = ybuf.v(j * D + cg * cw, j * D + (cg + 1) * cw)
                    yr = ybuf.r(j * D + cg * cw, j * D + (cg + 1) * cw)
                    c = sscol0 + j * 8 + cg
                    P.op("dve", lambda e, yv=yv, b=b, cw=cw: e.tensor_copy(yv, bank(b, cw)), reads=[bankr(b)], writes=[yr])
                    P.op("act", lambda e, yv=yv, c=c, cw=cw: e.activation(out=gsq.v(0, cw), in_=yv, func=AF.Square,
                                                                   accum_out=small.v(c, c + 1)),
                         reads=[yr], writes=[gsq.r(), small.r(c, c + 1)])


        def attention(vq0, NB, NT):
            HPU = 2
            W = HPU * NT
            nun = NH // HPU
            vmax = vq0 + NB - 1
            units = [(hg, kb) for hg in range(nun) for kb in range(vmax, -1, -1)]
            n = len(units)
            loaded = {}
            slot_key = {}
            kvi = [0]

            lastuse = {}

            def kv_get(hg, G, i, prefetch=False):
                key = (hg, G)
                if key in loaded:
                    buf, slot = loaded[key]
                    if not prefetch:
                        lastuse[slot] = i
                    return buf
                slot = kvi[0] % 3
                if lastuse.get(slot, -100) + 5 > i:
                    assert prefetch, (hg, G, i, lastuse)
                    return None
                kvi[0] += 1
                if slot in slot_key:
                    loaded.pop(slot_key[slot], None)
                slot_key[slot] = key
                buf = kvr[slot]
                pairs, rd = [], []
                for hh in range(HPU):
                    h = hg * HPU + hh
                    pairs.append((buf.v(hh * 512, (hh + 1) * 512), KTs[h][:, G * 512:(G + 1) * 512]))
                    pairs.append((buf.v(1024 + hh * 512, 1024 + (hh + 1) * 512).rearrange("p (b d) -> p b d", d=128),
                                  VS[h][:, G * 4:(G + 1) * 4, :]))
                    rd.append(("kt", h * 8 + G, h * 8 + G + 1))
                rd.append(("vs", G * 4, G * 4 + 4))
                P.dma("sp", s_kvr[slot], pairs, reads=rd, writes=[buf.r()])
                loaded[key] = (buf, slot)
                if not prefetch:
                    lastuse[slot] = i
                return buf

            zb_of, kvb = {}, {}

            def maskidx(kb):
                if kb < vq0:
                    return None
                if NB == 1:
                    return CM_MASK1
                return CM_MASK2B if kb == vq0 else CM_MASK2A

            def stage0(i):
                hg, kb = units[i]
                G, bi = kb // 4, kb % 4
                buf = kv_get(hg, G, i)
                kvb[i] = buf
                if i + 4 < n:
                    hg2, kb2 = units[i + 4]
                    kv_get(hg2, kb2 // 4, i, prefetch=True)
                zb = i % 4
                zb_of[i] = zb
                pieces = []
                for hh in range(HPU):
                    h = hg * HPU + hh
                    pieces.append((bank(zb, NT, hh * NT), buf.v(hh * 512 + bi * 128, hh * 512 + (bi + 1) * 128),
                                   qT.v(h * NT, (h + 1) * NT)))
                P.op("pe", mm_group(zb, pieces), reads=[buf.r(), qT.r()], writes=[bankr(zb)])

            def stage1(i):
                zb = zb_of[i]
                eb = Eb[i % 2]
                P.op("act", lambda e: e.activation(out=eb.v(0, W), in_=bank(zb, W), func=AF.Exp),
                     reads=[bankr(zb)], writes=[eb.r()])

            def stage2(i):
                hg, kb = units[i]
                eb = Eb[i % 2]
                lb = Lb[i % 3]
                P.op("act", lambda e: e.activation(out=lb.v(0, W), in_=eb.v(0, W), func=AF.Ln, bias=1.0),
                     reads=[eb.r()], writes=[lb.r()])
                mi = maskidx(kb)
                if mi is not None:
                    P.op("dve", lambda e: e.tensor_tensor(out=lb.v(0, W), in0=lb.v(0, W), in1=cm.v(mi * 128, mi * 128 + W),
                                                          op=ALU.mult),
                         reads=[lb.r(), cmr], writes=[lb.r()])

            def stage3(i):
                hg, kb = units[i]
                zb = zb_of[i]
                lb = Lb[i % 3]
                sbuf = Sb[hg % 2]
                first = (kb == vmax)
                prev = kb < 16
                tri = cmv(CM_NTRI_P if prev else CM_NTRI)
                ones = cmv(CM_NONES_P if prev else CM_NONES)
                pieces = [(bank(zb, W), tri, lb.v(0, W))]
                rds = [lb.r(), cmr]
                if not first:
                    pieces.append((bank(zb, W), ones, sbuf.v(0, W)))
                    rds.append(sbuf.r())
                P.op("pe", mm_group(zb, pieces, first_start=False), reads=rds, writes=[bankr(zb)])
                if first:
                    P.op("dve", lambda e: e.tensor_copy(sbuf.v(0, W), lb.v(0, W)), reads=[lb.r()], writes=[sbuf.r()])
                elif kb > 0:
                    P.op("dve", lambda e: e.tensor_tensor(out=sbuf.v(0, W), in0=sbuf.v(0, W), in1=lb.v(0, W), op=ALU.add),
                         reads=[lb.r(), sbuf.r()], writes=[sbuf.r()])

            def stage4(i):
                hg, kb = units[i]
                zb = zb_of[i]
                av = ab_[i % 3]
                P.op("act", lambda e: e.activation(out=av.v(0, W), in_=bank(zb, W), func=AF.Exp),
                     reads=[bankr(zb)], writes=[av.r()])
                mi = maskidx(kb)
                if mi is not None:
                    P.op("dve", lambda e: e.tensor_tensor(out=av.v(0, W), in0=av.v(0, W), in1=cm.v(mi * 128, mi * 128 + W),
                                                          op=ALU.mult),
                         reads=[av.r(), cmr], writes=[av.r()])

            def stage5(i):
                hg, kb = units[i]
                bi = kb % 4
                buf = kvb[i]
                av = ab_[i % 3]
                ob = 4 + hg % 2
                pieces = []
                for hh in range(HPU):
                    vo = 1024 + hh * 512 + bi * 128
                    pieces.append((bank(ob, NT, hh * NT), buf.v(vo, vo + 128), av.v(hh * NT, (hh + 1) * NT)))
                P.op("pe", mm_group(ob, pieces, first_start=(kb == vmax)), reads=[buf.r(), av.r()], writes=[bankr(ob)])
                if kb == 0:
                    h0 = hg * HPU
                    P.op("dve", lambda e: e.tensor_copy(oT.v(h0 * NT, (h0 + HPU) * NT), bank(ob, W)),
                         reads=[bankr(ob)], writes=[oT.r(h0 * NT, (h0 + HPU) * NT)])

            stages = [stage0, stage1, stage2, stage3, stage4, stage5]
            for s_ in range(n + 5):
                for k in (5, 4, 3, 2, 1, 0):
                    i = s_ - k
                    if 0 <= i < n:
                        stages[k](i)

        def mixer(vq0, NB, a0=0):
            NT = NB * 128
            ws = WStream(mixer_items())
            load_gain(0, 0)
            Ablk = [(A.v((a0 + j) * D, (a0 + j + 1) * D), A.r((a0 + j) * D, (a0 + j + 1) * D)) for j in range(NB)]
            front_end(Ablk, 0, hT, NT)
            load_gain(1, 0)
            for c in range(4):
                def ev(b, ml, c=c):
                    h = c * 2 + ml
                    P.op("act", lambda e: e.activation(out=qT.v(h * NT, (h + 1) * NT), in_=bank(b, NT), func=AF.Copy, scale=QSCALE),
                         reads=[bankr(b)], writes=[qT.r(h * NT, (h + 1) * NT)])
                formA(ws, "q", c, hT, NT, range(2), ev)
            for c in range(4):
                slot = ws.take("u", c)
                wv3_ = w3(slot, 16, 256)
                for j in range(NB):
                    b = next_fbank()
                    pieces = [(bank(b, 256), hT.v(k * NT + j * 128, k * NT + (j + 1) * 128), wv3_[:, k, :]) for k in range(16)]
                    P.op("pe", mm_group(b, pieces), reads=[slot.r(), hT.r()], writes=[bankr(b)])
                    uo = j * 1024 + c * 256
                    P.op("dve", lambda e, uo=uo, b=b: e.tensor_copy(ub.v(uo, uo + 256), bank(b, 256)),
                         reads=[bankr(b)], writes=[ub.r(uo, uo + 256)])
                ws.done()
            for cc in range(8):
                g = cc // 2
                b = next_fbank()
                pieces = []
                for j in range(NB):
                    first_blk = (vq0 + j == 16)
                    rw = cmv((CM_RW0 if first_blk else CM_RW) + g)
                    rp = cmv((CM_RP0 if first_blk else CM_RP) + g)
                    pieces.append((bank(b, 128, j * 128), ub.v(j * 1024 + cc * 128, j * 1024 + (cc + 1) * 128), rw))
                    pv = uprev.v(cc * 128, (cc + 1) * 128) if j == 0 else ub.v((j - 1) * 1024 + cc * 128, (j - 1) * 1024 + (cc + 1) * 128)
                    pieces.append((bank(b, 128, j * 128), pv, rp))
                P.op("pe", mm_group(b, pieces), reads=[ub.r(), uprev.r(), cmr], writes=[bankr(b)])
                P.op("act" if cc % 2 == 0 else "dve",
                     (lambda e, cc=cc, b=b: e.activation(out=plT.v(cc * NT, (cc + 1) * NT), in_=bank(b, NT), func=AF.Copy))
                     if cc % 2 == 0 else
                     (lambda e, cc=cc, b=b: e.tensor_copy(plT.v(cc * NT, (cc + 1) * NT), bank(b, NT))),
                     reads=[bankr(b)], writes=[plT.r(cc * NT, (cc + 1) * NT)])
            P.op("dve", lambda e: e.tensor_copy(uprev.v(), ub.v((NB - 1) * 1024, NB * 1024)),
                 reads=[ub.r((NB - 1) * 1024, NB * 1024)], writes=[uprev.r()])
            wpg4 = wpg.v().rearrange("p (g k c) -> p g k c", g=4, k=2)
            for dc in range(8):
                g, dd = dc // 2, dc % 2
                b = next_fbank()
                pieces = [(bank(b, NT), wpg4[:, g, kk, dd * 128:(dd + 1) * 128], plT.v((2 * g + kk) * NT, (2 * g + kk + 1) * NT))
                          for kk in range(2)]
                P.op("pe", mm_group(b, pieces), reads=[wpg.r(), plT.r()], writes=[bankr(b)])
                P.op("act", lambda e, dc=dc, b=b: e.activation(out=pgT.v(dc * NT, (dc + 1) * NT), in_=bank(b, NT), func=AF.Copy,
                                                               scale=cf.v(CF_PSCALE + dc, CF_PSCALE + dc + 1)),
                     reads=[bankr(b), cf.r()], writes=[pgT.r(dc * NT, (dc + 1) * NT)])
            attention(vq0, NB, NT)
            if dbg and NB == 2 and vq0 == 16 + 2 * dbg_cfg.get("dump_t", 0):
                for nm, bf_ in (("oT", oT), ("pgT", pgT), ("qT", qT)):
                    if nm in dbg_out:
                        P.dma("pool", s_dbg, [(dbg_out[nm], bf_.v())], reads=[bf_.r()])
            for M in range(8):
                sa, sp_ = sg[0], sg[1]
                def ev_ga(b, ml):
                    P.op("act", lambda e: e.activation(out=sa.v(ml * NT, (ml + 1) * NT), in_=bank(b, NT), func=AF.Sigmoid),
                         reads=[bankr(b)], writes=[sa.r(ml * NT, (ml + 1) * NT)])
                formA(ws, "ga", M, hT, NT, range(2), ev_ga)
                def ev_ab(b, ml):
                    P.op("dve", lambda e: e.tensor_tensor(out=tt.v(ml * NT, (ml + 1) * NT), in0=bank(b, NT),
                                                          in1=sa.v(ml * NT, (ml + 1) * NT), op=ALU.mult),
                         reads=[bankr(b), sa.r(ml * NT, (ml + 1) * NT)], writes=[tt.r(ml * NT, (ml + 1) * NT)])
                formA(ws, "ab", M, oT, NT, range(2), ev_ab)
                def ev_gp(b, ml):
                    P.op("act", lambda e: e.activation(out=sp_.v(ml * NT, (ml + 1) * NT), in_=bank(b, NT), func=AF.Sigmoid),
                         reads=[bankr(b)], writes=[sp_.r(ml * NT, (ml + 1) * NT)])
                formA(ws, "gp", M, hT, NT, range(2), ev_gp)
                def ev_pb(b, ml, M=M):
                    m = M * 2 + ml
                    P.op("dve", lambda e: e.tensor_tensor(out=sp_.v(ml * NT, (ml + 1) * NT), in0=bank(b, NT),
                                                          in1=sp_.v(ml * NT, (ml + 1) * NT), op=ALU.mult),
                         reads=[bankr(b), sp_.r(ml * NT, (ml + 1) * NT)], writes=[sp_.r(ml * NT, (ml + 1) * NT)])
                    P.op("dve", lambda e: e.tensor_tensor(out=mxT.v(m * NT, (m + 1) * NT), in0=sp_.v(ml * NT, (ml + 1) * NT),
                                                          in1=tt.v(ml * NT, (ml + 1) * NT), op=ALU.add),
                         reads=[sp_.r(ml * NT, (ml + 1) * NT), tt.r(ml * NT, (ml + 1) * NT)],
                         writes=[mxT.r(m * NT, (m + 1) * NT)])
                formA(ws, "pb", M, pgT, NT, range(2), ev_pb)
            formB_tokmajor(ws, "o", [[g] for g in range(8)], mxT, NT, NB, Bf, 40)
            norm_residual(8, NB, 0, Bf, 40, Ablk, Ablk)

        def ffn(NB, up_only=False, src=None):
            NT = NB * 128
            ws = WStream(ffn_items(up_only))
            load_gain(2, 0)
            Ablk = [(A.v(j * D, (j + 1) * D), A.r(j * D, (j + 1) * D)) for j in range(NB)]
            front_end(src if src is not None else Ablk, 0, hT, NT)
            if not up_only:
                load_gain(3, 0)
            cw0 = CF_CONVW
            pending = []
            for Fg in range(22):
                for part in range(2):
                    name = "upg" if part == 0 else "upv"
                    def ev(b, ml, Fg=Fg, part=part):
                        f = Fg * 2 + ml
                        ch = part * NFC + f
                        u_ = ubuf[(f + part) % 2]
                        tcb = tcv[f % 3] if part == 0 else tcv[3 + f % 2]
                        hv = uph.v(ch * 2, ch * 2 + 2)
                        hr = uph.r(ch * 2, ch * 2 + 2)
                        P.op("act", lambda e: e.activation(out=u_.v(2, 2 + NT), in_=bank(b, NT), func=AF.Copy),
                             reads=[bankr(b)], writes=[u_.r(2, 2 + NT)])
                        P.op("pool", lambda e: e.tensor_copy(u_.v(0, 2), hv), reads=[hr], writes=[u_.r(0, 2)])
                        if up_only:
                            P.op("pool", lambda e: e.tensor_copy(hv, u_.v(NT, NT + 2)), reads=[u_.r(NT, NT + 2)], writes=[hr])
                            return
                        w0 = cf.v(cw0 + ch * 3 + 0, cw0 + ch * 3 + 1)
                        w1 = cf.v(cw0 + ch * 3 + 1, cw0 + ch * 3 + 2)
                        w2 = cf.v(cw0 + ch * 3 + 2, cw0 + ch * 3 + 3)
                        bb = cf.v(CF_CONVB + ch, CF_CONVB + ch + 1)
                        P.op("act", lambda e: e.activation(out=tcb.v(0, NT), in_=bank(b, NT), func=AF.Identity, scale=w2, bias=bb),
                             reads=[bankr(b), cf.r()], writes=[tcb.r(0, NT)])
                        P.op("dve", lambda e: e.scalar_tensor_tensor(out=tcb.v(0, NT), in0=u_.v(1, 1 + NT), scalar=w1,
                                                                     in1=tcb.v(0, NT), op0=ALU.mult, op1=ALU.add),
                             reads=[u_.r(1, 1 + NT), tcb.r(0, NT), cf.r()], writes=[tcb.r(0, NT)])
                        P.op("dve", lambda e: e.scalar_tensor_tensor(out=tcb.v(0, NT), in0=u_.v(0, NT), scalar=w0,
                                                                     in1=tcb.v(0, NT), op0=ALU.mult, op1=ALU.add),
                             reads=[u_.r(0, NT), tcb.r(0, NT), cf.r()], writes=[tcb.r(0, NT)])
                        P.op("pool", lambda e: e.tensor_copy(hv, u_.v(NT, NT + 2)), reads=[u_.r(NT, NT + 2)], writes=[hr])
                        glb = gl[ml]
                        if part == 0:
                            def tail():
                                P.op("act", lambda e: e.activation(out=glb.v(0, NT), in_=tcb.v(0, NT), func=AF.Gelu_apprx_tanh),
                                     reads=[tcb.r(0, NT)], writes=[glb.r(0, NT)])
                        else:
                            def tail():
                                P.op("dve", lambda e: e.tensor_tensor(out=actT.v(f * NT, (f + 1) * NT), in0=glb.v(0, NT),
                                                                      in1=tcb.v(0, NT), op=ALU.mult),
                                     reads=[glb.r(0, NT), tcb.r(0, NT)], writes=[actT.r(f * NT, (f + 1) * NT)])
                        pending.append(tail)
                        while len(pending) > 1:
                            pending.pop(0)()
                    formA(ws, name, Fg, hT, NT, range(2), ev)
            while pending:
                pending.pop(0)()
            if up_only:
                return
            cis = [[ksi * 4 + cg for ksi in range(6)] for cg in range(4)]
            formB_tokmajor(ws, "d", cis, actT, NT, NB, Bf, 40)
            norm_residual(4, NB, 0, Bf, 40, Ablk, Ablk)

        def ple(NB, tok0):
            NT = NB * 128
            ws = WStream(ple_items())
            load_gain(4, 0)
            Ablk = [(A.v(j * D, (j + 1) * D), A.r(j * D, (j + 1) * D)) for j in range(NB)]
            P.dma("sp", s_pin, [(pin.v().rearrange("p (j c) -> p j c", c=PLE),
                                 pown[tok0:tok0 + NT, :].rearrange("(j p) c -> p j c", p=128))], writes=[pin.r()])
            P.op("dve", lambda e: e.tensor_copy(pnb.v(), pin.v()), reads=[pin.r()], writes=[pnb.r()])
            for j in range(NB):
                xs = 0
                xnv = xn.v(xs * D, (xs + 1) * D)
                xnr = xn.r(xs * D, (xs + 1) * D)
                P.op("act", lambda e, j=j, xnv=xnv: e.activation(out=xnv, in_=Ablk[j][0], func=AF.Copy),
                     reads=[Ablk[j][1]], writes=[xnr])
                for half in range(2):
                    b = next_tbank()
                    pb = bank(b).bitcast(BF16)

                    def tr(e, half=half, pb=pb, xs=xs):
                        ins = None
                        for k in range(8):
                            kc = half * 8 + k
                            ins = e.transpose(pb[:, k * 128:(k + 1) * 128],
                                              xn.v(xs * D + kc * 128, xs * D + (kc + 1) * 128), cmv(CM_IDENT))
                        return ins
                    P.op("pe", tr, reads=[xnr, cmr], writes=[bankr(b)])
                    dst = hT.v().rearrange("p (k t) -> p k t", t=NT)[:, half * 8:half * 8 + 8, j * 128:(j + 1) * 128]
                    srcv = pb.rearrange("p (k t) -> p k t", t=128)
                    if half == 0:
                        P.op("act", lambda e, dst=dst, srcv=srcv: e.activation(out=dst, in_=srcv, func=AF.Copy),
                             reads=[bankr(b)], writes=[hT.r()])
                    else:
                        P.op("dve", lambda e, dst=dst, srcv=srcv: e.tensor_copy(dst, srcv), reads=[bankr(b)], writes=[hT.r()])
                b = next_tbank()
                pb = bank(b).bitcast(BF16)

                def trp(e, j=j, pb=pb):
                    ins = None
                    for k in range(2):
                        ins = e.transpose(pb[:, k * 128:(k + 1) * 128], pnb.v(j * PLE + k * 128, j * PLE + (k + 1) * 128),
                                          cmv(CM_IDENT))
                    return ins
                P.op("pe", trp, reads=[pnb.r(), cmr], writes=[bankr(b)])
                dst = pT.v().rearrange("p (k t) -> p k t", t=NT)[:, :, j * 128:(j + 1) * 128]
                srcv = pb[:, 0:256].rearrange("p (k t) -> p k t", t=128)
                P.op("dve", lambda e, dst=dst, srcv=srcv: e.tensor_copy(dst, srcv), reads=[bankr(b)], writes=[pT.r()])
            P.dma("sp", s_wple, [(wpleb.v(), wdefs["ple"][2][0][:, 0:4096])], reads=[("wscr", 3, 4)], writes=[wpleb.r()])
            slot_ple = wpleb
            wple3 = wpleb.v().rearrange("p (k c) -> p k c", c=2048)
            pend = []
            for cg in range(8):
                slot = ws.take("pgate", cg)
                wg3 = w3(slot, 16, 256)
                for j in range(NB):
                    bg = next_fbank()
                    pieces = [(bank(bg, 256), hT.v(k * NT + j * 128, k * NT + (j + 1) * 128), wg3[:, k, :]) for k in range(16)]
                    P.op("pe", mm_group(bg, pieces), reads=[slot.r(), hT.r()], writes=[bankr(bg)])
                    be = next_fbank()
                    pieces = [(bank(be, 256), pT.v(k * NT + j * 128, k * NT + (j + 1) * 128), wple3[:, k, cg * 256:(cg + 1) * 256])
                              for k in range(2)]
                    P.op("pe", mm_group(be, pieces), reads=[slot_ple.r(), pT.r()], writes=[bankr(be)])
                    sgb = sgp[(cg * NB + j) % 2]
                    yv = Bf.v(j * D + cg * 256, j * D + (cg + 1) * 256)
                    yr = Bf.r(j * D + cg * 256, j * D + (cg + 1) * 256)
                    c = 40 + j * 8 + cg
                    P.op("act", lambda e, sgb=sgb, bg=bg: e.activation(out=sgb.v(), in_=bank(bg, 256), func=AF.Sigmoid),
                         reads=[bankr(bg)], writes=[sgb.r()])
                    P.op("dve", lambda e, sgb=sgb, be=be, yv=yv: e.tensor_tensor(out=yv, in0=bank(be, 256), in1=sgb.v(), op=ALU.mult),
                         reads=[bankr(be), sgb.r()], writes=[yr])

                    def tail(yv=yv, yr=yr, c=c):
                        P.op("act", lambda e: e.activation(out=gsq.v(0, 256), in_=yv, func=AF.Square,
                                                           accum_out=small.v(c, c + 1)),
                             reads=[yr], writes=[gsq.r(), small.r(c, c + 1)])
                    pend.append(tail)
                    while len(pend) > 1:
                        pend.pop(0)()
                ws.done()
            while pend:
                pend.pop(0)()
            norm_residual(8, NB, 0, Bf, 40, Ablk, Ablk)

        def load_x(vb0, NB, a0=0):
            for j in range(NB):
                vb = vb0 + j
                aj = a0 + j
                P.dma("sp", s_A[aj], [(A.v(aj * D, (aj + 1) * D), xall[vb * 128:(vb + 1) * 128, :])],
                      writes=[A.r(aj * D, (aj + 1) * D)])

        def dump(name, buf_ap, reg, dram_ap):
            P.dma("pool", s_dbg, [(dram_ap, buf_ap)], reads=[reg])

        load_x(15, 1)
        mixer(15, 1)
        halo_x1 = (Bf.v(3 * D, 4 * D), Bf.r(3 * D, 4 * D))
        P.op("act", lambda e: e.activation(out=halo_x1[0], in_=A.v(0, D), func=AF.Copy), reads=[A.r(0, D)], writes=[halo_x1[1]])
        ntiles = dbg_cfg.get("ntiles", 4)
        for t in range(ntiles):
            for sub in range(2):
                vb0 = 16 + 4 * t + 2 * sub
                load_x(vb0, 2, a0=2 * sub)
                mixer(vb0, 2, a0=2 * sub)
            if t == 0:
                ffn(1, up_only=True, src=[halo_x1])
            if dbg and t == dbg_cfg.get("dump_t", 0) and "x1" in dbg_out:
                dump("x1", A.v().rearrange("p (j c) -> p j c", c=D), A.r(),
                     dbg_out["x1"].rearrange("(j p) c -> p j c", p=128))
            ffn(4)
            if dbg and t == dbg_cfg.get("dump_t", 0) and "x2" in dbg_out:
                dump("x2", A.v().rearrange("p (j c) -> p j c", c=D), A.r(),
                     dbg_out["x2"].rearrange("(j p) c -> p j c", p=128))
            ple(4, t * 512)
            for j in range(4):
                r0 = t * 512 + j * 128
                P.dma("pool", s_Ast[j], [(out[r0:r0 + 128, :], A.v(j * D, (j + 1) * D))], reads=[A.r(j * D, (j + 1) * D)])
        P.final_wait("sp")

        with nc.Block() as block:
            @block.tensor
            def _(e):
                for f in P.q["pe"]:
                    f(e)

            @block.scalar
            def _(e):
                for f in P.q["act"]:
                    f(e)

            @block.vector
            def _(e):
                for f in P.q["dve"]:
                    f(e)

            @block.gpsimd
            def _(e):
                for f in P.q["pool"]:
                    f(e)

            @block.sync
            def _(e):
                for f in P.q["sp"]:
                    f(e)
        build_program.stats = dict(sb_used=sb_used, ops={k: len(v) for k, v in P.q.items()}, waits=P.nwaits)
    return nc


dbg_cfg = {}


def _const_mats(flag):
    j = np.arange(128)[:, None]
    s = np.arange(128)[None, :]
    mats = np.zeros((CM_N, 128, 128), np.float32)
    mats[CM_IDENT] = np.eye(128, dtype=np.float32)
    ntri = -(j >= s).astype(np.float32)
    mats[CM_NTRI] = ntri
    mats[CM_NONES] = -1.0
    mats[CM_NTRI_P] = ntri * flag
    mats[CM_NONES_P] = -1.0 * flag
    t = np.arange(128)[None, :]
    sidx = np.arange(128)[:, None]
    for g, w in enumerate((2, 4, 8, 16)):
        within = ((sidx <= t) & (sidx > t - w)).astype(np.float32)
        prevm = ((sidx - 128) > (t - w)).astype(np.float32)
        cnt_n = np.full((1, 128), float(w), np.float32)
        mats[CM_RW + g] = within / cnt_n - np.eye(128, dtype=np.float32)
        mats[CM_RP + g] = prevm / cnt_n
        if flag == 0.0:
            cnt0 = np.minimum(t + 1, w).astype(np.float32)
            mats[CM_RW0 + g] = within / cnt0 - np.eye(128, dtype=np.float32)
            mats[CM_RP0 + g] = 0.0
        else:
            mats[CM_RW0 + g] = mats[CM_RW + g]
            mats[CM_RP0 + g] = mats[CM_RP + g]
    tri = (j < s).astype(np.float32)
    z = np.zeros((128, 128), np.float32)
    o = np.ones((128, 128), np.float32)
    for k, m in enumerate((z, tri, z, tri)):
        mats[CM_MASK2A + k] = m
    for k, m in enumerate((tri, o, tri, o)):
        mats[CM_MASK2B + k] = m
    for k in range(4):
        mats[CM_MASK1 + k] = tri
    return np.ascontiguousarray(mats.transpose(1, 0, 2).reshape(128, CM_N * 128))


_NC_CACHE = {}


def kernel(x, p, norm_mix_pre, w_in, w_attn_branch, w_pool_group, pool_scale, w_pool_branch, w_out,
           norm_mix_post, norm_ffn_pre, w_up, conv_w, conv_b, w_down, norm_ffn_post, w_ple, w_ple_gate,
           norm_ple_post, _dbg=None):
    f = lambda a: np.ascontiguousarray(np.asarray(a, dtype=np.float32))
    x = f(x); p = f(p)
    gains = np.stack([f(norm_mix_pre)[0], f(norm_mix_post)[0], f(norm_ffn_pre)[0], f(norm_ffn_post)[0],
                      f(norm_ple_post)[0]], axis=0)
    colf = np.zeros((128, CF_N), np.float32)
    colf[:, CF_PSCALE:CF_PSCALE + 8] = f(pool_scale)[0].reshape(8, 128).T
    cw = f(conv_w)[0]
    colf[:, CF_CONVW:CF_CONVW + 264] = cw.reshape(3, 88, 128).transpose(2, 1, 0).reshape(128, 264)
    colf[:, CF_CONVB:CF_CONVB + 88] = f(conv_b)[0].reshape(88, 128).T
    shared = dict(w_in=f(w_in)[0], w_ab=f(w_attn_branch)[0], w_pg=f(w_pool_group)[0], w_pb=f(w_pool_branch)[0],
                  w_o=f(w_out)[0], w_up=f(w_up)[0], w_d=f(w_down)[0], w_ple=f(w_ple)[0], w_pgate=f(w_ple_gate)[0],
                  gains=gains, colf=colf)
    cm = {0.0: _const_mats(0.0), 1.0: _const_mats(1.0)}
    in_maps = []
    for c in range(8):
        b, half = c // 2, c % 2
        if half == 0:
            xall = np.concatenate([np.zeros((2048, D), np.float32), x[b, :2048]], axis=0)
        else:
            xall = x[b]
        m = dict(shared)
        m["xall"] = np.ascontiguousarray(xall)
        m["pown"] = np.ascontiguousarray(p[0, b, half * 2048:(half + 1) * 2048])
        m["cmat"] = cm[float(half)]
        in_maps.append(m)
    key = repr(_dbg)
    if key not in _NC_CACHE:
        _NC_CACHE[key] = build_program(_dbg)
    nc = _NC_CACHE[key]
    res = run_bass_kernel_spmd(nc, in_maps, core_ids=list(range(8)))
    outp = np.empty((4, 4096, D), np.float32)
    for c in range(8):
        b, half = c // 2, c % 2
        outp[b, half * 2048:(half + 1) * 2048] = res.results[c]["out"]
    if _dbg:
        return outp, res.results
    return outp
```

```python
import numpy as np
import concourse.bass as bass
import concourse.mybir as mybir
from concourse.bass_utils import run_bass_kernel_spmd

F32 = mybir.dt.float32
BF16 = mybir.dt.bfloat16
U8 = mybir.dt.uint8
AF = mybir.ActivationFunctionType
ALU = mybir.AluOpType

D = 2048
NH = 8
DFF = 5632
NFC = DFF // 128
PLE = 256
EPS = 1e-6
NVB = 32
QSCALE = 128.0 ** -0.5
SLOT_ELEMS = 4096
NSLOT = 4
SB_BYTES = 212480

CM_IDENT = 0
CM_NTRI = 1
CM_NONES = 2
CM_NTRI_P = 3
CM_NONES_P = 4
CM_RW = 5
CM_RP = 9
CM_RW0 = 13
CM_RP0 = 17
CM_MASK2A = 21
CM_MASK2B = 25
CM_MASK1 = 29
CM_N = 33

CF_PSCALE = 0
CF_CONVW = 8
CF_CONVB = 8 + 264
CF_N = 8 + 264 + 88


class DSem:
    def __init__(self, h):
        self.h = h
        self.count = 0


class Buf:
    def __init__(self, SB, lo, nelem, dt):
        self.esz = 4 if dt == F32 else 2
        self.lo = lo
        self.n = nelem
        self.dt = dt
        self.ap = SB[:, lo:lo + nelem * self.esz].bitcast(dt)

    def v(self, a=0, b=None):
        b = self.n if b is None else b
        return self.ap[:, a:b]

    def r(self, a=0, b=None):
        b = self.n if b is None else b
        return ("sb", self.lo + a * self.esz, self.lo + b * self.esz)


class Prog:
    COMPUTE = ("pe", "act", "dve", "pool")

    def __init__(self, nc, esems):
        self.nc = nc
        self.q = {e: [] for e in ("pe", "act", "dve", "pool", "sp")}
        self.esem = esems
        self.cnt = {e: 0 for e in self.COMPUTE}
        self.waited = {e: {} for e in self.q}
        self.regs = {}
        self.dsems = []
        self.nwaits = 0

    def _entries(self, reg):
        sp, lo, hi = reg
        lst = self.regs.setdefault(sp, [])
        return [e for e in lst if e[0] < hi and lo < e[1]], lst

    def _collect(self, reads, writes):
        raw, other = {}, {}

        def add(d, tok):
            if tok is None:
                return
            k = tok[0]
            if k not in d or d[k][2] < tok[2]:
                d[k] = tok

        for reg in reads:
            ov, _ = self._entries(reg)
            for e in ov:
                add(raw, e[2])
        for reg in writes:
            ov, _ = self._entries(reg)
            for e in ov:
                add(other, e[2])
                for t in e[3].values():
                    add(other, t)
        return raw, other

    def _commit(self, tok, reads, writes):
        for reg in reads:
            sp, lo, hi = reg
            ov, lst = self._entries(reg)
            pos = lo
            for e in sorted(ov, key=lambda e: e[0]):
                if e[0] > pos:
                    lst.append([pos, e[0], None, {tok[0]: tok}])
                old = e[3].get(tok[0])
                if old is None or old[2] < tok[2]:
                    e[3][tok[0]] = tok
                pos = max(pos, e[1])
            if pos < hi:
                lst.append([pos, hi, None, {tok[0]: tok}])
        for reg in writes:
            sp, lo, hi = reg
            ov, lst = self._entries(reg)
            for e in ov:
                lst.remove(e)
                if e[0] < lo:
                    lst.append([e[0], lo, e[2], dict(e[3])])
                if e[1] > hi:
                    lst.append([hi, e[1], e[2], dict(e[3])])
            lst.append([lo, hi, tok, {}])

    def _waits(self, eng, raw, other):
        need = dict(raw)
        for k, t in other.items():
            if k == eng and eng == "pe":
                continue
            if k not in need or need[k][2] < t[2]:
                need[k] = t
        out = []
        wd = self.waited[eng]
        for k, t in need.items():
            if wd.get(k, 0) >= t[2]:
                continue
            wd[k] = t[2]
            out.append((t[1], t[2]))
        self.nwaits += len(out)
        return out

    def op(self, eng, fn, reads=(), writes=()):
        raw, other = self._collect(reads, writes)
        waits = self._waits(eng, raw, other)
        n = self.cnt[eng] + 1
        self.cnt[eng] = n
        sem = self.esem[eng]
        tok = (eng, sem, n)
        def emit(e, waits=waits, fn=fn, sem=sem):
            for (s, v) in waits:
                e.wait_ge(s, v)
            fn(e).then_inc(sem, 1)
        self.q[eng].append(emit)
        self._commit(tok, reads, writes)
        return tok

    def dma(self, q, dsem, pairs, reads=(), writes=()):
        raw, other = self._collect(reads, writes)
        waits = self._waits(q, raw, other)
        dsem.count += 16 * len(pairs)
        tok = (id(dsem), dsem.h, dsem.count)
        def emit(e, waits=waits, pairs=pairs, h=dsem.h):
            for (s, v) in waits:
                e.wait_ge(s, v)
            for (o, i) in pairs:
                e.dma_start(out=o, in_=i).then_inc(h, 16)
        self.q[q].append(emit)
        self._commit(tok, reads, writes)
        return tok

    def final_wait(self, q):
        sems = [(d.h, d.count) for d in self.dsems if d.count > 0]
        sems += [(self.esem[e], self.cnt[e]) for e in self.COMPUTE if self.cnt[e] > 0]
        def emit(e):
            for (s, v) in sems:
                e.wait_ge(s, v)
        self.q[q].append(emit)


def build_program(dbg=None):
    nc = bass.Bass("TRN2", target_bir_lowering=False)
    dt_in = lambda name, shape: nc.dram_tensor(name, shape, F32, kind="ExternalInput").ap()
    xall = dt_in("xall", [4096, D])
    pown = dt_in("pown", [2048, PLE])
    w_in = dt_in("w_in", [D, 8192])
    w_ab = dt_in("w_ab", [1024, D])
    w_pg = dt_in("w_pg", [4, 256, 256])
    w_pb = dt_in("w_pb", [1024, D])
    w_o = dt_in("w_o", [D, D])
    w_up = dt_in("w_up", [D, 2 * DFF])
    w_d = dt_in("w_d", [DFF, D])
    w_ple = dt_in("w_ple", [PLE, D])
    w_pgate = dt_in("w_pgate", [D, D])
    gains = dt_in("gains", [5, D])
    colf = dt_in("colf", [128, CF_N])
    cmat = dt_in("cmat", [128, CM_N * 128])
    out = nc.dram_tensor("out", [2048, D], F32, kind="ExternalOutput").ap()
    KTs = nc.dram_tensor("KTs", [NH, 128, 4096], BF16, kind="Internal").ap()
    VS = nc.dram_tensor("VS", [NH, 128, NVB, 128], BF16, kind="Internal").ap()
    dbg_out = {}
    if dbg:
        for name, shape in dbg.items():
            dbg_out[name] = nc.dram_tensor("dbg_" + name, shape, F32, kind="ExternalOutput").ap()

    wdefs = {}

    def defw(name, src, K, c_lo, c_hi, cw, ksplit=16):
        chunks = []
        nkc = K // 128
        for ks in range(0, nkc, ksplit):
            nk = min(ksplit, nkc - ks)
            for c0 in range(c_lo, c_hi, cw):
                chunks.append((ks, nk, c0, cw))
        scr = nc.dram_tensor("ws_" + name, [len(chunks), 128, SLOT_ELEMS], BF16, kind="Internal").ap()
        wdefs[name] = (src, chunks, scr)

    defw("q", w_in, D, 0, 1024, 256)
    defw("u", w_in, D, 3072, 4096, 256)
    defw("ga", w_in, D, 4096, 6144, 256)
    defw("gp", w_in, D, 6144, 8192, 256)
    defw("ab", w_ab, 1024, 0, D, 256)
    defw("pb", w_pb, 1024, 0, D, 256)
    defw("o", w_o, D, 0, D, 256)
    defw("upg", w_up, D, 0, DFF, 256)
    defw("upv", w_up, D, DFF, 2 * DFF, 256)
    defw("d", w_d, DFF, 0, D, 512, ksplit=8)
    defw("ple", w_ple, PLE, 0, D, 2048)
    defw("pgate", w_pgate, D, 0, D, 256)

    import contextlib
    with contextlib.ExitStack() as es:
        SBT = es.enter_context(nc.sbuf_tensor("SB", [128, SB_BYTES], U8))
        PS = es.enter_context(nc.psum_tensor("PS", [128, 8 * 512], F32))
        esems = {e: es.enter_context(nc.semaphore("sem_" + e)) for e in Prog.COMPUTE}
        P = Prog(nc, esems)

        def new_dsem(name):
            d = DSem(es.enter_context(nc.semaphore(name)))
            P.dsems.append(d)
            return d

        off = [0]

        def alloc(nelem, dt):
            esz = 4 if dt == F32 else 2
            lo = (off[0] + 63) // 64 * 64
            off[0] = lo + nelem * esz
            assert off[0] <= SB_BYTES, off[0]
            return Buf(SBT, lo, nelem, dt)

        cm = alloc(CM_N * 128, BF16)
        cf = alloc(CF_N, F32)
        wpg = alloc(4 * 2 * 256, BF16)
        A = alloc(4 * D, F32)
        xn = alloc(D, BF16)
        xn_main = xn
        gbc = [alloc(D, F32)]
        hT = alloc(16 * 512, BF16)
        small = alloc(128, F32)
        uph = alloc(88 * 2, F32)
        uprev = alloc(1024, BF16)
        gsq = alloc(512, BF16)
        kv_lo = off[0]
        Bf = alloc(4 * D, F32)
        ring = [alloc(SLOT_ELEMS, BF16) for _ in range(NSLOT)]
        union_lo = off[0]
        qT = alloc(8 * 256, BF16)
        oT = alloc(8 * 256, BF16)
        ub = alloc(2 * 1024, BF16)
        plT = alloc(8 * 256, BF16)
        pgT = alloc(8 * 256, BF16)
        mxT = alloc(16 * 256, BF16)
        Eb = [alloc(512, F32) for _ in range(2)]
        Lb = [alloc(512, BF16) for _ in range(3)]
        ab_ = [alloc(512, BF16) for _ in range(3)]
        Sb = [alloc(512, BF16) for _ in range(2)]
        kvr = [alloc(2048, BF16) for _ in range(3)]
        sg = [alloc(2 * 256, F32) for _ in range(2)]
        tt = alloc(2 * 256, F32)
        mixer_hi = off[0]
        off[0] = kv_lo
        wk = alloc(16 * 1024, BF16)
        wv = alloc(16 * 1024, BF16)
        hT4s = [hT, alloc(16 * 512, BF16)]
        xn4 = [alloc(D, BF16) for _ in range(4)]
        kst = [alloc(512, BF16) for _ in range(3)]
        vst = [alloc(1024, BF16) for _ in range(2)]
        kv_hi = off[0]
        off[0] = union_lo
        actT = alloc(NFC * 512, BF16)
        ubuf = [alloc(514, F32) for _ in range(2)]
        tcv = [alloc(512, F32) for _ in range(5)]
        gl = [alloc(512, F32) for _ in range(2)]
        ffn_hi = off[0]
        off[0] = union_lo
        pin = alloc(4 * PLE, F32)
        pnb = alloc(4 * PLE, BF16)
        pT = alloc(2 * 512, BF16)
        sgp = [alloc(256, F32) for _ in range(2)]
        wpleb = alloc(2 * 2048, BF16)
        ple_hi = off[0]
        off[0] = max(mixer_hi, kv_hi, ffn_hi, ple_hi)
        sb_used = off[0]

        s_wgrp = [new_dsem("s_wg%d" % i) for i in range(4)]
        s_kvw = new_dsem("s_kvw")
        s_A = [new_dsem("s_A%d" % i) for i in range(4)]
        s_Ast = [new_dsem("s_Ast%d" % i) for i in range(4)]
        s_g = [new_dsem("s_g%d" % i) for i in range(1)]
        s_ring = [new_dsem("s_ring%d" % i) for i in range(NSLOT)]
        s_kvr = [new_dsem("s_kvr%d" % i) for i in range(3)]
        s_kst = [new_dsem("s_kst%d" % i) for i in range(3)]
        s_vst = [new_dsem("s_vst%d" % i) for i in range(2)]
        s_pin = new_dsem("s_pin")
        s_wple = new_dsem("s_wple")
        s_c = [new_dsem("s_c%d" % i) for i in range(5)]
        s_dbg = new_dsem("s_dbg")

        def bank(b, n=512, c0=0):
            return PS[:, b * 512 + c0:b * 512 + c0 + n]

        def bankr(b):
            return ("ps", b, b + 1)

        fb = [0]

        def next_fbank():
            b = fb[0] % 6
            fb[0] += 1
            return b

        tb = [0]

        def next_tbank():
            b = 6 + tb[0] % 2
            tb[0] += 1
            return b

        def cmv(idx, n=128):
            return cm.v(idx * 128, idx * 128 + n)

        cmr = cm.r()

        P.dma("pool", s_c[0], [(cm.v(), cmat)], writes=[cm.r()])
        P.dma("sp", s_c[1], [(cf.v(), colf)], writes=[cf.r()])
        P.dma("pool", s_c[2],
              [(wpg.v().rearrange("p (g k c) -> p g k c", g=4, k=2),
                w_pg.rearrange("g (k p) c -> p g k c", p=128))], writes=[wpg.r()])
        for (wb, c0, sc) in ((wk, 1024, s_c[3]), (wv, 2048, s_c[4])):
            pairs = []
            for k4 in range(0, 16, 4):
                pairs.append((wb.v().rearrange("p (k c) -> p k c", c=1024)[:, k4:k4 + 4, :],
                              w_in[k4 * 128:(k4 + 4) * 128, c0:c0 + 1024].rearrange("(k p) c -> p k c", p=128)))
            P.dma("pool", sc, pairs, writes=[wb.r()])
        P.op("dve", lambda e: e.memset(uph.v(), 0.0), writes=[uph.r()])
        P.op("dve", lambda e: e.memset(uprev.v(), 0.0), writes=[uprev.r()])

        conv_order = [("q", 0), ("u", 0), ("ga", 1), ("ab", 1), ("gp", 1), ("pb", 1), ("o", 1),
                      ("upg", 2), ("upv", 2), ("d", 3), ("ple", 3), ("pgate", 3)]
        wtok = {}
        grp_pairs = {g: [] for g in range(4)}
        for name, g in conv_order:
            src, chunks, scr = wdefs[name]
            for ci, (ks, nk, c0, cw) in enumerate(chunks):
                dst = scr[ci][:, 0:nk * cw].rearrange("p (k c) -> p k c", c=cw)
                for k4 in range(0, nk, 4):
                    kk = min(4, nk - k4)
                    r0 = (ks + k4) * 128
                    grp_pairs[g].append((dst[:, k4:k4 + kk, :],
                                         src[r0:r0 + kk * 128, c0:c0 + cw].rearrange("(k p) c -> p k c", p=128)))
        for g in range(2):
            tok = P.dma("pool", s_wgrp[g], grp_pairs[g], reads=[wk.r(), wv.r(), cm.r()], writes=[("wscr", g, g + 1)])

        def late_conversions():
            for g in range(2, 4):
                P.dma("pool", s_wgrp[g], grp_pairs[g], reads=[("vs", NVB - 1, NVB)], writes=[("wscr", g, g + 1)])
        wgrp_of = {name: g for name, g in conv_order}

        ring_state = {"next": 0, "queue": []}

        def ring_issue(item):
            name, ci = item
            src, chunks, scr = wdefs[name]
            ks, nk, c0, cw = chunks[ci]
            slot = ring_state["next"] % NSLOT
            ring_state["next"] += 1
            n = nk * cw
            g = wgrp_of[name]
            P.dma("sp", s_ring[slot], [(ring[slot].v(0, n), scr[ci][:, 0:n])],
                  reads=[("wscr", g, g + 1)], writes=[ring[slot].r()])
            return slot

        class WStream:
            def __init__(self, items):
                self.items = items
                self.issued = 0
                self.slots = {}
                self.cur = 0
                for _ in range(min(NSLOT, len(items))):
                    self._issue()

            def _issue(self):
                if self.issued < len(self.items):
                    self.slots[self.issued] = ring_issue(self.items[self.issued])
                    self.issued += 1

            def take(self, name, ci):
                assert self.items[self.cur] == (name, ci), (self.items[self.cur], name, ci)
                slot = self.slots.pop(self.cur)
                self.cur += 1
                return ring[slot]

            def done(self):
                self._issue()

        def load_gain(gi, slot):
            P.dma("sp", s_g[slot], [(gbc[slot].v(), gains[gi].partition_broadcast(128))], writes=[gbc[slot].r()])

        def rstd_from(col_in, col_out, scratch_col):
            ci, co, cs_ = small.v(col_in, col_in + 1), small.v(col_out, col_out + 1), small.v(scratch_col, scratch_col + 1)
            P.op("dve", lambda e: e.tensor_scalar(out=cs_, in0=ci, scalar1=1.0 / D, scalar2=EPS, op0=ALU.mult, op1=ALU.add),
                 reads=[small.r(col_in, col_in + 1)], writes=[small.r(scratch_col, scratch_col + 1)])
            P.op("act", lambda e: e.activation(out=cs_, in_=cs_, func=AF.Ln),
                 reads=[small.r(scratch_col, scratch_col + 1)], writes=[small.r(scratch_col, scratch_col + 1)])
            P.op("act", lambda e: e.activation(out=co, in_=cs_, func=AF.Exp, scale=-0.5),
                 reads=[small.r(scratch_col, scratch_col + 1)], writes=[small.r(col_out, col_out + 1)])

        def transpose_block(xs, hTb, NT, jj, xn=None):
            xn = xn if xn is not None else xn_main
            xnr = xn.r(xs * D, (xs + 1) * D)
            for half in range(2):
                b = next_tbank()
                pb = bank(b).bitcast(BF16)

                def tr(e, half=half, pb=pb, xs=xs, xn=xn):
                    ins = None
                    for k in range(8):
                        kc = half * 8 + k
                        ins = e.transpose(pb[:, k * 128:(k + 1) * 128],
                                          xn.v(xs * D + kc * 128, xs * D + (kc + 1) * 128), cmv(CM_IDENT))
                    return ins
                P.op("pe", tr, reads=[xnr, cmr], writes=[bankr(b)])
                dst = hTb.v().rearrange("p (k t) -> p k t", t=NT)[:, half * 8:half * 8 + 8, jj * 128:(jj + 1) * 128]
                srcv = pb.rearrange("p (k t) -> p k t", t=128)
                if half == 0:
                    P.op("act", lambda e, dst=dst, srcv=srcv: e.activation(out=dst, in_=srcv, func=AF.Copy),
                         reads=[bankr(b)], writes=[hTb.r()])
                else:
                    P.op("dve", lambda e, dst=dst, srcv=srcv: e.tensor_copy(dst, srcv),
                         reads=[bankr(b)], writes=[hTb.r()])

        def norm_block(xap, xr, gslot, j, xnb):
            sq = small.v(j, j + 1)
            rs = small.v(8 + j, 9 + j)
            xnv, xnr = xnb.v(0, D), xnb.r(0, D)
            P.op("act", lambda e: e.activation(out=xnv, in_=xap, func=AF.Square, accum_out=sq),
                 reads=[xr], writes=[xnr, small.r(j, j + 1)])
            rstd_from(j, 8 + j, 16 + j)
            P.op("dve", lambda e: e.scalar_tensor_tensor(out=xnv, in0=xap, scalar=rs, in1=gbc[gslot].v(),
                                                         op0=ALU.mult, op1=ALU.mult),
                 reads=[xr, small.r(8 + j, 9 + j), gbc[gslot].r()], writes=[xnr])

        def front_end(src_blocks, gslot, hTb, NT, j0=0):
            xn = xn_main
            for j, (xap, xr) in enumerate(src_blocks):
                xs = 0
                sq = small.v(j, j + 1)
                rs = small.v(8 + j, 9 + j)
                xnv = xn.v(xs * D, (xs + 1) * D)
                xnr = xn.r(xs * D, (xs + 1) * D)
                P.op("act", lambda e, xap=xap, xnv=xnv, sq=sq: e.activation(out=xnv, in_=xap, func=AF.Square, accum_out=sq),
                     reads=[xr], writes=[xnr, small.r(j, j + 1)])
                rstd_from(j, 8 + j, 16 + j)
                P.op("dve", lambda e, xap=xap, rs=rs, xnv=xnv: e.scalar_tensor_tensor(
                    out=xnv, in0=xap, scalar=rs, in1=gbc[gslot].v(), op0=ALU.mult, op1=ALU.mult),
                    reads=[xr, small.r(8 + j, 9 + j), gbc[gslot].r()], writes=[xnr])
                transpose_block(xs, hTb, NT, j0 + j)

        def mm_group(b, pieces, first_start=True):
            def f(e):
                ins = None
                for i, (o, l, r) in enumerate(pieces):
                    ins = e.matmul(o, l, r, start=(first_start and i == 0), stop=(i == len(pieces) - 1),
                                   skip_group_check=True)
                return ins
            return f

        def w3(slotbuf, nk, cw):
            return slotbuf.v(0, nk * cw).rearrange("p (k c) -> p k c", c=cw)

        def formA(ws, name, ci, actTb, NT, mlist, evac):
            src, chunks, scr = wdefs[name]
            ks, nk, c0, cw = chunks[ci]
            slot = ws.take(name, ci)
            wv3 = w3(slot, nk, cw)
            for ml in mlist:
                b = next_fbank()
                pieces = [(bank(b, NT), wv3[:, k, ml * 128:(ml + 1) * 128],
                           actTb.v((ks + k) * NT, (ks + k + 1) * NT)) for k in range(nk)]
                P.op("pe", mm_group(b, pieces), reads=[slot.r(), actTb.r()], writes=[bankr(b)])
                evac(b, ml)
            ws.done()

        load_gain(0, 0)

        def kv_fe_a(t):
            for jj in range(4):
                vb = t * 4 + jj
                P.dma("sp", s_A[jj], [(A.v(jj * D, (jj + 1) * D), xall[vb * 128:(vb + 1) * 128, :])],
                      writes=[A.r(jj * D, (jj + 1) * D)])
                norm_block(A.v(jj * D, (jj + 1) * D), A.r(jj * D, (jj + 1) * D), 0, jj, xn4[jj])

        def kv_fe_b(t):
            for jj in range(4):
                transpose_block(0, hT4s[t % 2], 512, jj, xn=xn4[jj])

        kv_fe_a(0)
        kv_fe_b(0)
        for t in range(8):
            hT4 = hT4s[t % 2]
            if t + 1 < 8:
                kv_fe_a(t + 1)
            wk3 = wk.v().rearrange("p (k c) -> p k c", c=1024)
            wv3 = wv.v().rearrange("p (k c) -> p k c", c=1024)
            for h in range(NH):
                b = next_fbank()
                pieces = [(bank(b), wk3[:, k, h * 128:(h + 1) * 128], hT4.v(k * 512, (k + 1) * 512)) for k in range(16)]
                P.op("pe", mm_group(b, pieces), reads=[wk.r(), hT4.r()], writes=[bankr(b)])
                st = (t * NH + h) % 3
                if h % 2 == 0:
                    P.op("act", lambda e, st=st, b=b: e.activation(out=kst[st].v(), in_=bank(b), func=AF.Copy),
                         reads=[bankr(b)], writes=[kst[st].r()])
                else:
                    P.op("dve", lambda e, st=st, b=b: e.tensor_copy(kst[st].v(), bank(b)),
                         reads=[bankr(b)], writes=[kst[st].r()])
                P.dma("sp", s_kst[st], [(KTs[h][:, t * 512:(t + 1) * 512], kst[st].v())],
                      reads=[kst[st].r()], writes=[("kt", h * 8 + t, h * 8 + t + 1)])
            for jj in range(4):
                vb = t * 4 + jj
                st = vb % 2
                for cg in range(2):
                    b = next_fbank()
                    pieces = [(bank(b), hT4.v(k * 512 + jj * 128, k * 512 + (jj + 1) * 128),
                               wv3[:, k, cg * 512:(cg + 1) * 512]) for k in range(16)]
                    P.op("pe", mm_group(b, pieces), reads=[wv.r(), hT4.r()], writes=[bankr(b)])
                    if cg == 0:
                        P.op("act", lambda e, st=st, b=b: e.activation(out=vst[st].v(0, 512), in_=bank(b), func=AF.Copy),
                             reads=[bankr(b)], writes=[vst[st].r(0, 512)])
                    else:
                        P.op("dve", lambda e, st=st, b=b: e.tensor_copy(vst[st].v(512, 1024), bank(b)),
                             reads=[bankr(b)], writes=[vst[st].r(512, 1024)])
                P.dma("sp", s_vst[st],
                      [(VS[:, :, vb, :].rearrange("h p d -> p h d"), vst[st].v().rearrange("p (h d) -> p h d", d=128))],
                      reads=[vst[st].r()], writes=[("vs", vb, vb + 1)])
            if t + 1 < 8:
                kv_fe_b(t + 1)

        late_conversions()

        def mixer_items():
            it = [("q", c) for c in range(4)] + [("u", c) for c in range(4)]
            for M in range(8):
                it += [("ga", M), ("ab", M), ("gp", M), ("pb", M)]
            it += [("o", g) for g in range(8)]
            return it

        def ffn_items(up_only=False):
            it = []
            for Fg in range(22):
                it += [("upg", Fg), ("upv", Fg)]
            if not up_only:
                for cg in range(4):
                    for ksi in range(6):
                        it.append(("d", ksi * 4 + cg))
            return it

        def ple_items():
            return [("pgate", g) for g in range(8)]

        def norm_residual(ncg, NB, gslot, ybuf, sscol0, resid, dest):
            for j in range(NB):
                c = sscol0 + j * 8
                ssum = small.v(32 + j, 33 + j)
                P.op("dve", lambda e, c=c, ssum=ssum: e.tensor_reduce(out=ssum, in_=small.v(c, c + ncg),
                                                                      axis=mybir.AxisListType.X, op=ALU.add),
                     reads=[small.r(c, c + ncg)], writes=[small.r(32 + j, 33 + j)])
                rstd_from(32 + j, 36 + j, 20 + j)
                ssum = small.v(36 + j, 37 + j)
                yv = ybuf.v(j * D, (j + 1) * D)
                yr = ybuf.r(j * D, (j + 1) * D)
                P.op("dve", lambda e, yv=yv, ssum=ssum: e.scalar_tensor_tensor(
                    out=yv, in0=yv, scalar=ssum, in1=gbc[gslot].v(), op0=ALU.mult, op1=ALU.mult),
                    reads=[yr, small.r(36 + j, 37 + j), gbc[gslot].r()], writes=[yr])
                rv, rr = resid[j]
                dv, dr = dest[j]
                P.op("dve", lambda e, yv=yv, rv=rv, dv=dv: e.tensor_tensor(out=dv, in0=yv, in1=rv, op=ALU.add),
                     reads=[yr, rr], writes=[dr])

        def formB_tokmajor(ws, name, cis, actTb, NT, NB, ybuf, sscol0, nk_total_chunks=None):
            for cg, chunk_list in enumerate(cis):
                banks = [next_fbank() for _ in range(NB)]
                for idx, ci in enumerate(chunk_list):
                    src, chunks, scr = wdefs[name]
                    ks, nk, c0, cw = chunks[ci]
                    slot = ws.take(name, ci)
                    wv3_ = w3(slot, nk, cw)
                    for j in range(NB):
                        b = banks[j]
                        pieces = [(bank(b, cw), actTb.v((ks + k) * NT + j * 128, (ks + k) * NT + (j + 1) * 128), wv3_[:, k, :])
                                  for k in range(nk)]
                        P.op("pe", mm_group(b, pieces, first_start=(idx == 0)), reads=[slot.r(), actTb.r()],
                             writes=[bankr(b)])
                    ws.done()
                for j in range(NB):
                    b = banks[j]
                    yv = ybuf.v(j * D + cg * cw, j * D + (cg + 1) * cw)
                    yr = ybuf.r(j * D + cg * cw, j * D + (cg + 1) * cw)
                    c = sscol0 + j * 8 + cg
                    P.op("dve", lambda e, yv=yv, b=b, cw=cw: e.tensor_copy(yv, bank(b, cw)), reads=[bankr(b)], writes=[yr])
                    P.op("act", lambda e, yv=yv, c=c, cw=cw: e.activation(out=gsq.v(0, cw), in_=yv, func=AF.Square,
                                                                   accum_out=small.v(c, c + 1)),
                         reads=[yr], writes=[gsq.r(), small.r(c, c + 1)])


        def attention(vq0, NB, NT):
            HPU = 2
            W = HPU * NT
            nun = NH // HPU
            vmax = vq0 + NB - 1
            units = [(hg, kb) for hg in range(nun) for kb in range(vmax, -1, -1)]
            n = len(units)
            loaded = {}
            slot_key = {}
            kvi = [0]

            lastuse = {}

            def kv_get(hg, G, i, prefetch=False):
                key = (hg, G)
                if key in loaded:
                    buf, slot = loaded[key]
                    if not prefetch:
                        lastuse[slot] = i
                    return buf
                slot = kvi[0] % 3
                if lastuse.get(slot, -100) + 5 > i:
                    assert prefetch, (hg, G, i, lastuse)
                    return None
                kvi[0] += 1
                if slot in slot_key:
                    loaded.pop(slot_key[slot], None)
                slot_key[slot] = key
                buf = kvr[slot]
                pairs, rd = [], []
                for hh in range(HPU):
                    h = hg * HPU + hh
                    pairs.append((buf.v(hh * 512, (hh + 1) * 512), KTs[h][:, G * 512:(G + 1) * 512]))
                    pairs.append((buf.v(1024 + hh * 512, 1024 + (hh + 1) * 512).rearrange("p (b d) -> p b d", d=128),
                                  VS[h][:, G * 4:(G + 1) * 4, :]))
                    rd.append(("kt", h * 8 + G, h * 8 + G + 1))
                rd.append(("vs", G * 4, G * 4 + 4))
                P.dma("sp", s_kvr[slot], pairs, reads=rd, writes=[buf.r()])
                loaded[key] = (buf, slot)
                if not prefetch:
                    lastuse[slot] = i
                return buf

            zb_of, kvb = {}, {}

            def maskidx(kb):
                if kb < vq0:
                    return None
                if NB == 1:
                    return CM_MASK1
                return CM_MASK2B if kb == vq0 else CM_MASK2A

            def stage0(i):
                hg, kb = units[i]
                G, bi = kb // 4, kb % 4
                buf = kv_get(hg, G, i)
                kvb[i] = buf
                if i + 4 < n:
                    hg2, kb2 = units[i + 4]
                    kv_get(hg2, kb2 // 4, i, prefetch=True)
                zb = i % 4
                zb_of[i] = zb
                pieces = []
                for hh in range(HPU):
                    h = hg * HPU + hh
                    pieces.append((bank(zb, NT, hh * NT), buf.v(hh * 512 + bi * 128, hh * 512 + (bi + 1) * 128),
                                   qT.v(h * NT, (h + 1) * NT)))
                P.op("pe", mm_group(zb, pieces), reads=[buf.r(), qT.r()], writes=[bankr(zb)])

            def stage1(i):
                zb = zb_of[i]
                eb = Eb[i % 2]
                P.op("act", lambda e: e.activation(out=eb.v(0, W), in_=bank(zb, W), func=AF.Exp),
                     reads=[bankr(zb)], writes=[eb.r()])

            def stage2(i):
                hg, kb = units[i]
                eb = Eb[i % 2]
                lb = Lb[i % 3]
                P.op("act", lambda e: e.activation(out=lb.v(0, W), in_=eb.v(0, W), func=AF.Ln, bias=1.0),
                     reads=[eb.r()], writes=[lb.r()])
                mi = maskidx(kb)
                if mi is not None:
                    P.op("dve", lambda e: e.tensor_tensor(out=lb.v(0, W), in0=lb.v(0, W), in1=cm.v(mi * 128, mi * 128 + W),
                                                          op=ALU.mult),
                         reads=[lb.r(), cmr], writes=[lb.r()])

            def stage3(i):
                hg, kb = units[i]
                zb = zb_of[i]
                lb = Lb[i % 3]
                sbuf = Sb[hg % 2]
                first = (kb == vmax)
                prev = kb < 16
                tri = cmv(CM_NTRI_P if prev else CM_NTRI)
                ones = cmv(CM_NONES_P if prev else CM_NONES)
                pieces = [(bank(zb, W), tri, lb.v(0, W))]
                rds = [lb.r(), cmr]
                if not first:
                    pieces.append((bank(zb, W), ones, sbuf.v(0, W)))
                    rds.append(sbuf.r())
                P.op("pe", mm_group(zb, pieces, first_start=False), reads=rds, writes=[bankr(zb)])
                if first:
                    P.op("dve", lambda e: e.tensor_copy(sbuf.v(0, W), lb.v(0, W)), reads=[lb.r()], writes=[sbuf.r()])
                elif kb > 0:
                    P.op("dve", lambda e: e.tensor_tensor(out=sbuf.v(0, W), in0=sbuf.v(0, W), in1=lb.v(0, W), op=ALU.add),
                         reads=[lb.r(), sbuf.r()], writes=[sbuf.r()])

            def stage4(i):
                hg, kb = units[i]
                zb = zb_of[i]
                av = ab_[i % 3]
                P.op("act", lambda e: e.activation(out=av.v(0, W), in_=bank(zb, W), func=AF.Exp),
                     reads=[bankr(zb)], writes=[av.r()])
                mi = maskidx(kb)
                if mi is not None:
                    P.op("dve", lambda e: e.tensor_tensor(out=av.v(0, W), in0=av.v(0, W), in1=cm.v(mi * 128, mi * 128 + W),
                                                          op=ALU.mult),
                         reads=[av.r(), cmr], writes=[av.r()])

            def stage5(i):
                hg, kb = units[i]
                bi = kb % 4
                buf = kvb[i]
                av = ab_[i % 3]
                ob = 4 + hg % 2
                pieces = []
                for hh in range(HPU):
                    vo = 1024 + hh * 512 + bi * 128
                    pieces.append((bank(ob, NT, hh * NT), buf.v(vo, vo + 128), av.v(hh * NT, (hh + 1) * NT)))
                P.op("pe", mm_group(ob, pieces, first_start=(kb == vmax)), reads=[buf.r(), av.r()], writes=[bankr(ob)])
                if kb == 0:
                    h0 = hg * HPU
                    P.op("dve", lambda e: e.tensor_copy(oT.v(h0 * NT, (h0 + HPU) * NT), bank(ob, W)),
                         reads=[bankr(ob)], writes=[oT.r(h0 * NT, (h0 + HPU) * NT)])

            stages = [stage0, stage1, stage2, stage3, stage4, stage5]
            for s_ in range(n + 5):
                for k in (5, 4, 3, 2, 1, 0):
                    i = s_ - k
                    if 0 <= i < n:
                        stages[k](i)

        def mixer(vq0, NB, a0=0):
            NT = NB * 128
            ws = WStream(mixer_items())
            load_gain(0, 0)
            Ablk = [(A.v((a0 + j) * D, (a0 + j + 1) * D), A.r((a0 + j) * D, (a0 + j + 1) * D)) for j in range(NB)]
            front_end(Ablk, 0, hT, NT)
            load_gain(1, 0)
            for c in range(4):
                def ev(b, ml, c=c):
                    h = c * 2 + ml
                    P.op("act", lambda e: e.activation(out=qT.v(h * NT, (h + 1) * NT), in_=bank(b, NT), func=AF.Copy, scale=QSCALE),
                         reads=[bankr(b)], writes=[qT.r(h * NT, (h + 1) * NT)])
                formA(ws, "q", c, hT, NT, range(2), ev)
            for c in range(4):
                slot = ws.take("u", c)
                wv3_ = w3(slot, 16, 256)
                for j in range(NB):
                    b = next_fbank()
                    pieces = [(bank(b, 256), hT.v(k * NT + j * 128, k * NT + (j + 1) * 128), wv3_[:, k, :]) for k in range(16)]
                    P.op("pe", mm_group(b, pieces), reads=[slot.r(), hT.r()], writes=[bankr(b)])
                    uo = j * 1024 + c * 256
                    P.op("dve", lambda e, uo=uo, b=b: e.tensor_copy(ub.v(uo, uo + 256), bank(b, 256)),
                         reads=[bankr(b)], writes=[ub.r(uo, uo + 256)])
                ws.done()
            for cc in range(8):
                g = cc // 2
                b = next_fbank()
                pieces = []
                for j in range(NB):
                    first_blk = (vq0 + j == 16)
                    rw = cmv((CM_RW0 if first_blk else CM_RW) + g)
                    rp = cmv((CM_RP0 if first_blk else CM_RP) + g)
                    pieces.append((bank(b, 128, j * 128), ub.v(j * 1024 + cc * 128, j * 1024 + (cc + 1) * 128), rw))
                    pv = uprev.v(cc * 128, (cc + 1) * 128) if j == 0 else ub.v((j - 1) * 1024 + cc * 128, (j - 1) * 1024 + (cc + 1) * 128)
                    pieces.append((bank(b, 128, j * 128), pv, rp))
                P.op("pe", mm_group(b, pieces), reads=[ub.r(), uprev.r(), cmr], writes=[bankr(b)])
                P.op("act" if cc % 2 == 0 else "dve",
                     (lambda e, cc=cc, b=b: e.activation(out=plT.v(cc * NT, (cc + 1) * NT), in_=bank(b, NT), func=AF.Copy))
                     if cc % 2 == 0 else
                     (lambda e, cc=cc, b=b: e.tensor_copy(plT.v(cc * NT, (cc + 1) * NT), bank(b, NT))),
                     reads=[bankr(b)], writes=[plT.r(cc * NT, (cc + 1) * NT)])
            P.op("dve", lambda e: e.tensor_copy(uprev.v(), ub.v((NB - 1) * 1024, NB * 1024)),
                 reads=[ub.r((NB - 1) * 1024, NB * 1024)], writes=[uprev.r()])
            wpg4 = wpg.v().rearrange("p (g k c) -> p g k c", g=4, k=2)
            for dc in range(8):
                g, dd = dc // 2, dc % 2
                b = next_fbank()
                pieces = [(bank(b, NT), wpg4[:, g, kk, dd * 128:(dd + 1) * 128], plT.v((2 * g + kk) * NT, (2 * g + kk + 1) * NT))
                          for kk in range(2)]
                P.op("pe", mm_group(b, pieces), reads=[wpg.r(), plT.r()], writes=[bankr(b)])
                P.op("act", lambda e, dc=dc, b=b: e.activation(out=pgT.v(dc * NT, (dc + 1) * NT), in_=bank(b, NT), func=AF.Copy,
                                                               scale=cf.v(CF_PSCALE + dc, CF_PSCALE + dc + 1)),
                     reads=[bankr(b), cf.r()], writes=[pgT.r(dc * NT, (dc + 1) * NT)])
            attention(vq0, NB, NT)
            if dbg and NB == 2 and vq0 == 16 + 2 * dbg_cfg.get("dump_t", 0):
                for nm, bf_ in (("oT", oT), ("pgT", pgT), ("qT", qT)):
                    if nm in dbg_out:
                        P.dma("pool", s_dbg, [(dbg_out[nm], bf_.v())], reads=[bf_.r()])
            for M in range(8):
                sa, sp_ = sg[0], sg[1]
                def ev_ga(b, ml):
                    P.op("act", lambda e: e.activation(out=sa.v(ml * NT, (ml + 1) * NT), in_=bank(b, NT), func=AF.Sigmoid),
                         reads=[bankr(b)], writes=[sa.r(ml * NT, (ml + 1) * NT)])
                formA(ws, "ga", M, hT, NT, range(2), ev_ga)
                def ev_ab(b, ml):
                    P.op("dve", lambda e: e.tensor_tensor(out=tt.v(ml * NT, (ml + 1) * NT), in0=bank(b, NT),
                                                          in1=sa.v(ml * NT, (ml + 1) * NT), op=ALU.mult),
                         reads=[bankr(b), sa.r(ml * NT, (ml + 1) * NT)], writes=[tt.r(ml * NT, (ml + 1) * NT)])
                formA(ws, "ab", M, oT, NT, range(2), ev_ab)
                def ev_gp(b, ml):
                    P.op("act", lambda e: e.activation(out=sp_.v(ml * NT, (ml + 1) * NT), in_=bank(b, NT), func=AF.Sigmoid),
                         reads=[bankr(b)], writes=[sp_.r(ml * NT, (ml + 1) * NT)])
                formA(ws, "gp", M, hT, NT, range(2), ev_gp)
                def ev_pb(b, ml, M=M):
                    m = M * 2 + ml
                    P.op("dve", lambda e: e.tensor_tensor(out=sp_.v(ml * NT, (ml + 1) * NT), in0=bank(b, NT),
                                                          in1=sp_.v(ml * NT, (ml + 1) * NT), op=ALU.mult),
                         reads=[bankr(b), sp_.r(ml * NT, (ml + 1) * NT)], writes=[sp_.r(ml * NT, (ml + 1) * NT)])
                    P.op("dve", lambda e: e.tensor_tensor(out=mxT.v(m * NT, (m + 1) * NT), in0=sp_.v(ml * NT, (ml + 1) * NT),
                                                          in1=tt.v(ml * NT, (ml + 1) * NT), op=ALU.add),
                         reads=[sp_.r(ml * NT, (ml + 1) * NT), tt.r(ml * NT, (ml + 1) * NT)],
                         writes=[mxT.r(m * NT, (m + 1) * NT)])
                formA(ws, "pb", M, pgT, NT, range(2), ev_pb)
            formB_tokmajor(ws, "o", [[g] for g in range(8)], mxT, NT, NB, Bf, 40)
            norm_residual(8, NB, 0, Bf, 40, Ablk, Ablk)

        def ffn(NB, up_only=False, src=None):
            NT = NB * 128
            ws = WStream(ffn_items(up_only))
            load_gain(2, 0)
            Ablk = [(A.v(j * D, (j + 1) * D), A.r(j * D, (j + 1) * D)) for j in range(NB)]
            front_end(src if src is not None else Ablk, 0, hT, NT)
            if not up_only:
                load_gain(3, 0)
            cw0 = CF_CONVW
            pending = []
            for Fg in range(22):
                for part in range(2):
                    name = "upg" if part == 0 else "upv"
                    def ev(b, ml, Fg=Fg, part=part):
                        f = Fg * 2 + ml
                        ch = part * NFC + f
                        u_ = ubuf[(f + part) % 2]
                        tcb = tcv[f % 3] if part == 0 else tcv[3 + f % 2]
                        hv = uph.v(ch * 2, ch * 2 + 2)
                        hr = uph.r(ch * 2, ch * 2 + 2)
                        P.op("act", lambda e: e.activation(out=u_.v(2, 2 + NT), in_=bank(b, NT), func=AF.Copy),
                             reads=[bankr(b)], writes=[u_.r(2, 2 + NT)])
                        P.op("pool", lambda e: e.tensor_copy(u_.v(0, 2), hv), reads=[hr], writes=[u_.r(0, 2)])
                        if up_only:
                            P.op("pool", lambda e: e.tensor_copy(hv, u_.v(NT, NT + 2)), reads=[u_.r(NT, NT + 2)], writes=[hr])
                            return
                        w0 = cf.v(cw0 + ch * 3 + 0, cw0 + ch * 3 + 1)
                        w1 = cf.v(cw0 + ch * 3 + 1, cw0 + ch * 3 + 2)
                        w2 = cf.v(cw0 + ch * 3 + 2, cw0 + ch * 3 + 3)
                        bb = cf.v(CF_CONVB + ch, CF_CONVB + ch + 1)
                        P.op("act", lambda e: e.activation(out=tcb.v(0, NT), in_=bank(b, NT), func=AF.Identity, scale=w2, bias=bb),
                             reads=[bankr(b), cf.r()], writes=[tcb.r(0, NT)])
                        P.op("dve", lambda e: e.scalar_tensor_tensor(out=tcb.v(0, NT), in0=u_.v(1, 1 + NT), scalar=w1,
                                                                     in1=tcb.v(0, NT), op0=ALU.mult, op1=ALU.add),
                             reads=[u_.r(1, 1 + NT), tcb.r(0, NT), cf.r()], writes=[tcb.r(0, NT)])
                        P.op("dve", lambda e: e.scalar_tensor_tensor(out=tcb.v(0, NT), in0=u_.v(0, NT), scalar=w0,
                                                                     in1=tcb.v(0, NT), op0=ALU.mult, op1=ALU.add),
                             reads=[u_.r(0, NT), tcb.r(0, NT), cf.r()], writes=[tcb.r(0, NT)])
                        P.op("pool", lambda e: e.tensor_copy(hv, u_.v(NT, NT + 2)), reads=[u_.r(NT, NT + 2)], writes=[hr])
                        glb = gl[ml]
                        if part == 0:
                            def tail():
                                P.op("act", lambda e: e.activation(out=glb.v(0, NT), in_=tcb.v(0, NT), func=AF.Gelu_apprx_tanh),
                                     reads=[tcb.r(0, NT)], writes=[glb.r(0, NT)])
                        else:
                            def tail():
                                P.op("dve", lambda e: e.tensor_tensor(out=actT.v(f * NT, (f + 1) * NT), in0=glb.v(0, NT),
                                                                      in1=tcb.v(0, NT), op=ALU.mult),
                                     reads=[glb.r(0, NT), tcb.r(0, NT)], writes=[actT.r(f * NT, (f + 1) * NT)])
                        pending.append(tail)
                        while len(pending) > 1:
                            pending.pop(0)()
                    formA(ws, name, Fg, hT, NT, range(2), ev)
            while pending:
                pending.pop(0)()
            if up_only:
                return
            cis = [[ksi * 4 + cg for ksi in range(6)] for cg in range(4)]
            formB_tokmajor(ws, "d", cis, actT, NT, NB, Bf, 40)
            norm_residual(4, NB, 0, Bf, 40, Ablk, Ablk)

        def ple(NB, tok0):
            NT = NB * 128
            ws = WStream(ple_items())
            load_gain(4, 0)
            Ablk = [(A.v(j * D, (j + 1) * D), A.r(j * D, (j + 1) * D)) for j in range(NB)]
            P.dma("sp", s_pin, [(pin.v().rearrange("p (j c) -> p j c", c=PLE),
                                 pown[tok0:tok0 + NT, :].rearrange("(j p) c -> p j c", p=128))], writes=[pin.r()])
            P.op("dve", lambda e: e.tensor_copy(pnb.v(), pin.v()), reads=[pin.r()], writes=[pnb.r()])
            for j in range(NB):
                xs = 0
                xnv = xn.v(xs * D, (xs + 1) * D)
                xnr = xn.r(xs * D, (xs + 1) * D)
                P.op("act", lambda e, j=j, xnv=xnv: e.activation(out=xnv, in_=Ablk[j][0], func=AF.Copy),
                     reads=[Ablk[j][1]], writes=[xnr])
                for half in range(2):
                    b = next_tbank()
                    pb = bank(b).bitcast(BF16)

                    def tr(e, half=half, pb=pb, xs=xs):
                        ins = None
                        for k in range(8):
                            kc = half * 8 + k
                            ins = e.transpose(pb[:, k * 128:(k + 1) * 128],
                                              xn.v(xs * D + kc * 128, xs * D + (kc + 1) * 128), cmv(CM_IDENT))
                        return ins
                    P.op("pe", tr, reads=[xnr, cmr], writes=[bankr(b)])
                    dst = hT.v().rearrange("p (k t) -> p k t", t=NT)[:, half * 8:half * 8 + 8, j * 128:(j + 1) * 128]
                    srcv = pb.rearrange("p (k t) -> p k t", t=128)
                    if half == 0:
                        P.op("act", lambda e, dst=dst, srcv=srcv: e.activation(out=dst, in_=srcv, func=AF.Copy),
                             reads=[bankr(b)], writes=[hT.r()])
                    else:
                        P.op("dve", lambda e, dst=dst, srcv=srcv: e.tensor_copy(dst, srcv), reads=[bankr(b)], writes=[hT.r()])
                b = next_tbank()
                pb = bank(b).bitcast(BF16)

                def trp(e, j=j, pb=pb):
                    ins = None
                    for k in range(2):
                        ins = e.transpose(pb[:, k * 128:(k + 1) * 128], pnb.v(j * PLE + k * 128, j * PLE + (k + 1) * 128),
                                          cmv(CM_IDENT))
                    return ins
                P.op("pe", trp, reads=[pnb.r(), cmr], writes=[bankr(b)])
                dst = pT.v().rearrange("p (k t) -> p k t", t=NT)[:, :, j * 128:(j + 1) * 128]
                srcv = pb[:, 0:256].rearrange("p (k t) -> p k t", t=128)
                P.op("dve", lambda e, dst=dst, srcv=srcv: e.tensor_copy(dst, srcv), reads=[bankr(b)], writes=[pT.r()])
            P.dma("sp", s_wple, [(wpleb.v(), wdefs["ple"][2][0][:, 0:4096])], reads=[("wscr", 3, 4)], writes=[wpleb.r()])
            slot_ple = wpleb
            wple3 = wpleb.v().rearrange("p (k c) -> p k c", c=2048)
            pend = []
            for cg in range(8):
                slot = ws.take("pgate", cg)
                wg3 = w3(slot, 16, 256)
                for j in range(NB):
                    bg = next_fbank()
                    pieces = [(bank(bg, 256), hT.v(k * NT + j * 128, k * NT + (j + 1) * 128), wg3[:, k, :]) for k in range(16)]
                    P.op("pe", mm_group(bg, pieces), reads=[slot.r(), hT.r()], writes=[bankr(bg)])
                    be = next_fbank()
                    pieces = [(bank(be, 256), pT.v(k * NT + j * 128, k * NT + (j + 1) * 128), wple3[:, k, cg * 256:(cg + 1) * 256])
                              for k in range(2)]
                    P.op("pe", mm_group(be, pieces), reads=[slot_ple.r(), pT.r()], writes=[bankr(be)])
                    sgb = sgp[(cg * NB + j) % 2]
                    yv = Bf.v(j * D + cg * 256, j * D + (cg + 1) * 256)
                    yr = Bf.r(j * D + cg * 256, j * D + (cg + 1) * 256)
                    c = 40 + j * 8 + cg
                    P.op("act", lambda e, sgb=sgb, bg=bg: e.activation(out=sgb.v(), in_=bank(bg, 256), func=AF.Sigmoid),
                         reads=[bankr(bg)], writes=[sgb.r()])
                    P.op("dve", lambda e, sgb=sgb, be=be, yv=yv: e.tensor_tensor(out=yv, in0=bank(be, 256), in1=sgb.v(), op=ALU.mult),
                         reads=[bankr(be), sgb.r()], writes=[yr])

                    def tail(yv=yv, yr=yr, c=c):
                        P.op("act", lambda e: e.activation(out=gsq.v(0, 256), in_=yv, func=AF.Square,
                                                           accum_out=small.v(c, c + 1)),
                             reads=[yr], writes=[gsq.r(), small.r(c, c + 1)])
                    pend.append(tail)
                    while len(pend) > 1:
                        pend.pop(0)()
                ws.done()
            while pend:
                pend.pop(0)()
            norm_residual(8, NB, 0, Bf, 40, Ablk, Ablk)

        def load_x(vb0, NB, a0=0):
            for j in range(NB):
                vb = vb0 + j
                aj = a0 + j
                P.dma("sp", s_A[aj], [(A.v(aj * D, (aj + 1) * D), xall[vb * 128:(vb + 1) * 128, :])],
                      writes=[A.r(aj * D, (aj + 1) * D)])

        def dump(name, buf_ap, reg, dram_ap):
            P.dma("pool", s_dbg, [(dram_ap, buf_ap)], reads=[reg])

        load_x(15, 1)
        mixer(15, 1)
        halo_x1 = (Bf.v(3 * D, 4 * D), Bf.r(3 * D, 4 * D))
        P.op("act", lambda e: e.activation(out=halo_x1[0], in_=A.v(0, D), func=AF.Copy), reads=[A.r(0, D)], writes=[halo_x1[1]])
        ntiles = dbg_cfg.get("ntiles", 4)
        for t in range(ntiles):
            for sub in range(2):
                vb0 = 16 + 4 * t + 2 * sub
                load_x(vb0, 2, a0=2 * sub)
                mixer(vb0, 2, a0=2 * sub)
            if t == 0:
                ffn(1, up_only=True, src=[halo_x1])
            if dbg and t == dbg_cfg.get("dump_t", 0) and "x1" in dbg_out:
                dump("x1", A.v().rearrange("p (j c) -> p j c", c=D), A.r(),
                     dbg_out["x1"].rearrange("(j p) c -> p j c", p=128))
            ffn(4)
            if dbg and t == dbg_cfg.get("dump_t", 0) and "x2" in dbg_out:
                dump("x2", A.v().rearrange("p (j c) -> p j c", c=D), A.r(),
                     dbg_out["x2"].rearrange("(j p) c -> p j c", p=128))
            ple(4, t * 512)
            for j in range(4):
                r0 = t * 512 + j * 128
                P.dma("pool", s_Ast[j], [(out[r0:r0 + 128, :], A.v(j * D, (j + 1) * D))], reads=[A.r(j * D, (j + 1) * D)])
        P.final_wait("sp")

        with nc.Block() as block:
            @block.tensor
            def _(e):
                for f in P.q["pe"]:
                    f(e)

            @block.scalar
            def _(e):
                for f in P.q["act"]:
                    f(e)

            @block.vector
            def _(e):
                for f in P.q["dve"]:
                    f(e)

            @block.gpsimd
            def _(e):
                for f in P.q["pool"]:
                    f(e)

            @block.sync
            def _(e):
                for f in P.q["sp"]:
                    f(e)
        build_program.stats = dict(sb_used=sb_used, ops={k: len(v) for k, v in P.q.items()}, waits=P.nwaits)
    return nc


dbg_cfg = {}


def _const_mats(flag):
    j = np.arange(128)[:, None]
    s = np.arange(128)[None, :]
    mats = np.zeros((CM_N, 128, 128), np.float32)
    mats[CM_IDENT] = np.eye(128, dtype=np.float32)
    ntri = -(j >= s).astype(np.float32)
    mats[CM_NTRI] = ntri
    mats[CM_NONES] = -1.0
    mats[CM_NTRI_P] = ntri * flag
    mats[CM_NONES_P] = -1.0 * flag
    t = np.arange(128)[None, :]
    sidx = np.arange(128)[:, None]
    for g, w in enumerate((2, 4, 8, 16)):
        within = ((sidx <= t) & (sidx > t - w)).astype(np.float32)
        prevm = ((sidx - 128) > (t - w)).astype(np.float32)
        cnt_n = np.full((1, 128), float(w), np.float32)
        mats[CM_RW + g] = within / cnt_n - np.eye(128, dtype=np.float32)
        mats[CM_RP + g] = prevm / cnt_n
        if flag == 0.0:
            cnt0 = np.minimum(t + 1, w).astype(np.float32)
            mats[CM_RW0 + g] = within / cnt0 - np.eye(128, dtype=np.float32)
            mats[CM_RP0 + g] = 0.0
        else:
            mats[CM_RW0 + g] = mats[CM_RW + g]
            mats[CM_RP0 + g] = mats[CM_RP + g]
    tri = (j < s).astype(np.float32)
    z = np.zeros((128, 128), np.float32)
    o = np.ones((128, 128), np.float32)
    for k, m in enumerate((z, tri, z, tri)):
        mats[CM_MASK2A + k] = m
    for k, m in enumerate((tri, o, tri, o)):
        mats[CM_MASK2B + k] = m
    for k in range(4):
        mats[CM_MASK1 + k] = tri
    return np.ascontiguousarray(mats.transpose(1, 0, 2).reshape(128, CM_N * 128))


_NC_CACHE = {}


def kernel(x, p, norm_mix_pre, w_in, w_attn_branch, w_pool_group, pool_scale, w_pool_branch, w_out,
           norm_mix_post, norm_ffn_pre, w_up, conv_w, conv_b, w_down, norm_ffn_post, w_ple, w_ple_gate,
           norm_ple_post, _dbg=None):
    f = lambda a: np.ascontiguousarray(np.asarray(a, dtype=np.float32))
    x = f(x); p = f(p)
    gains = np.stack([f(norm_mix_pre)[0], f(norm_mix_post)[0], f(norm_ffn_pre)[0], f(norm_ffn_post)[0],
                      f(norm_ple_post)[0]], axis=0)
    colf = np.zeros((128, CF_N), np.float32)
    colf[:, CF_PSCALE:CF_PSCALE + 8] = f(pool_scale)[0].reshape(8, 128).T
    cw = f(conv_w)[0]
    colf[:, CF_CONVW:CF_CONVW + 264] = cw.reshape(3, 88, 128).transpose(2, 1, 0).reshape(128, 264)
    colf[:, CF_CONVB:CF_CONVB + 88] = f(conv_b)[0].reshape(88, 128).T
    shared = dict(w_in=f(w_in)[0], w_ab=f(w_attn_branch)[0], w_pg=f(w_pool_group)[0], w_pb=f(w_pool_branch)[0],
                  w_o=f(w_out)[0], w_up=f(w_up)[0], w_d=f(w_down)[0], w_ple=f(w_ple)[0], w_pgate=f(w_ple_gate)[0],
                  gains=gains, colf=colf)
    cm = {0.0: _const_mats(0.0), 1.0: _const_mats(1.0)}
    in_maps = []
    for c in range(8):
        b, half = c // 2, c % 2
        if half == 0:
            xall = np.concatenate([np.zeros((2048, D), np.float32), x[b, :2048]], axis=0)
        else:
            xall = x[b]
        m = dict(shared)
        m["xall"] = np.ascontiguousarray(xall)
        m["pown"] = np.ascontiguousarray(p[0, b, half * 2048:(half + 1) * 2048])
        m["cmat"] = cm[float(half)]
        in_maps.append(m)
    key = repr(_dbg)
    if key not in _NC_CACHE:
        _NC_CACHE[key] = build_program(_dbg)
    nc = _NC_CACHE[key]
    res = run_bass_kernel_spmd(nc, in_maps, core_ids=list(range(8)))
    outp = np.empty((4, 4096, D), np.float32)
    for c in range(8):
        b, half = c // 2, c % 2
        outp[b, half * 2048:(half + 1) * 2048] = res.results[c]["out"]
    if _dbg:
        return outp, res.results
    return outp
```
